# Optimizing a Trainium2 kernel written in Bass

```python
import math
import jax, jax.numpy as jnp
from jax import lax
import numpy as np

D_MODEL = 1024
BATCH = 2
SEQ = 8192
DEPTH = 1
DEC_BATCH = 128
DEC_SEQ = 4
PAST_LEN = 8192
PAGE_SIZE = 128

NORM_EPS = 1e-6
PLE_DIM = 256
FFN_DIM = ((8 * D_MODEL) // 3 + 127) // 128 * 128
GDN_HEADS = D_MODEL // 256
GDN_DK = 128
GDN_DV = 128
GDN_KEY_DIM = GDN_HEADS * GDN_DK
GDN_VAL_DIM = GDN_HEADS * GDN_DV
CONV_DIM = 2 * GDN_KEY_DIM + GDN_VAL_DIM
CONV_WIDTH = 4
GDN_CHUNK = 64
SWA_HEADS = D_MODEL // 128
SWA_KV_HEADS = SWA_HEADS // 4
SWA_GROUP = SWA_HEADS // SWA_KV_HEADS
SWA_HEAD_DIM = 64
SWA_Q_DIM = SWA_HEADS * SWA_HEAD_DIM
SWA_KV_DIM = SWA_KV_HEADS * SWA_HEAD_DIM
SWA_WINDOW = 128
NUM_BUCKETS = 32
REL_MAX_DISTANCE = 128
MIX_DIM = GDN_VAL_DIM + SWA_Q_DIM
PROJ_SPLITS = (CONV_DIM, GDN_VAL_DIM, GDN_HEADS, GDN_HEADS, SWA_Q_DIM, SWA_KV_DIM, SWA_KV_DIM)
PROJ_DIM = sum(PROJ_SPLITS)

kernel_name = 'hymba_gdn_swa_macaron_step'


def rmsnorm(x, gain):
    x32 = x.astype(jnp.float32)
    y = x32 * lax.rsqrt(jnp.mean(x32 * x32, axis=-1, keepdims=True) + NORM_EPS)
    return (y * gain.astype(jnp.float32)).astype(x.dtype)


def l2norm(t):
    return t * lax.rsqrt(jnp.sum(t * t, axis=-1, keepdims=True) + 1e-6)


def swiglu(h, w_gate, w_up, w_down):
    return (jax.nn.silu(h @ w_gate) * (h @ w_up)) @ w_down


def _chunk(t, n, c):
    return jnp.moveaxis(t.reshape((t.shape[0], n, c) + t.shape[2:]), 3, 1)


def gated_delta_chunked(q, k, v, g, beta, s0):
    B, L, H, DK = q.shape
    DV = v.shape[-1]
    C = min(GDN_CHUNK, L)
    n = -(-L // C)
    pad = n * C - L
    if pad:
        q, k, v, g, beta = [jnp.pad(t, [(0, 0), (0, pad)] + [(0, 0)] * (t.ndim - 2)) for t in (q, k, v, g, beta)]
    q, k, v, g, beta = [_chunk(t, n, C) for t in (q, k, v, g, beta)]
    gc = jnp.cumsum(g, axis=-1)
    idx = jnp.arange(C)
    incl = idx[:, None] >= idx[None, :]
    strict = idx[:, None] > idx[None, :]
    decay = jnp.exp(jnp.where(incl, gc[..., :, None] - gc[..., None, :], -jnp.inf))
    kb = k * beta[..., None]
    lmat = jnp.where(strict, jnp.einsum('bhncd,bhnsd->bhncs', kb, k) * decay, 0.0)
    a_mat = lmat + jnp.eye(C, dtype=lmat.dtype)
    rhs = jnp.concatenate([v * beta[..., None], kb * jnp.exp(gc)[..., None]], axis=-1)
    sol = lax.linalg.triangular_solve(a_mat, rhs, left_side=True, lower=True, unit_diagonal=True)
    u, w = sol[..., :DV], sol[..., DV:]
    qk = jnp.einsum('bhncd,bhnsd->bhncs', q, k) * decay
    q_dec = q * jnp.exp(gc)[..., None]
    k_dec = k * jnp.exp(gc[..., -1:] - gc)[..., None]
    chunk_decay = jnp.exp(gc[..., -1])

    def step(s, xs):
        qk_c, qd_c, kd_c, u_c, w_c, cd_c = xs
        v_new = u_c - jnp.einsum('bhcd,bhde->bhce', w_c, s)
        o_c = jnp.einsum('bhcd,bhde->bhce', qd_c, s) + jnp.einsum('bhcs,bhse->bhce', qk_c, v_new)
        s = s * cd_c[..., None, None] + jnp.einsum('bhcd,bhce->bhde', kd_c, v_new)
        return s, o_c

    xs = tuple(jnp.moveaxis(t, 2, 0) for t in (qk, q_dec, k_dec, u, w, chunk_decay))
    s_final, o = lax.scan(step, s0, xs)
    o = jnp.moveaxis(jnp.moveaxis(o, 0, 2).reshape(B, H, n * C, DV), 1, 2)[:, :L]
    return o, s_final


def gdn_branch(qkv_raw, z, b, a, conv_prev, s0, conv_w, a_log, dt_bias, norm_gain):
    B, L, _ = qkv_raw.shape
    xp = jnp.concatenate([conv_prev.astype(qkv_raw.dtype), qkv_raw], axis=1)
    y = xp[:, 0:L] * conv_w[0]
    for j in range(1, CONV_WIDTH):
        y = y + xp[:, j:j + L] * conv_w[j]
    y = jax.nn.silu(y).astype(jnp.float32)
    q = l2norm(y[..., :GDN_KEY_DIM].reshape(B, L, GDN_HEADS, GDN_DK)) * GDN_DK ** -0.5
    k = l2norm(y[..., GDN_KEY_DIM:2 * GDN_KEY_DIM].reshape(B, L, GDN_HEADS, GDN_DK))
    v = y[..., 2 * GDN_KEY_DIM:].reshape(B, L, GDN_HEADS, GDN_DV)
    beta = jax.nn.sigmoid(b.astype(jnp.float32))
    g = -jnp.exp(a_log.astype(jnp.float32)) * jax.nn.softplus(a.astype(jnp.float32) + dt_bias.astype(jnp.float32))
    o, s_new = gated_delta_chunked(q, k, v, g, beta, s0.astype(jnp.float32))
    o = rmsnorm(o, norm_gain) * jax.nn.silu(z.astype(jnp.float32).reshape(B, L, GDN_HEADS, GDN_DV))
    return o.reshape(B, L, GDN_VAL_DIM).astype(qkv_raw.dtype), xp[:, L:], s_new


def t5_bias(dist, rel_bias):
    d = jnp.maximum(dist, 0)
    exact = NUM_BUCKETS // 2
    log_ratio = jnp.log(jnp.maximum(d, 1).astype(jnp.float32) / exact) / math.log(REL_MAX_DISTANCE / exact)
    large = jnp.minimum(exact + (log_ratio * (NUM_BUCKETS - exact)).astype(jnp.int32), NUM_BUCKETS - 1)
    bucket = jnp.where(d < exact, d, large)
    bias = jnp.moveaxis(rel_bias[bucket].astype(jnp.float32), -1, 0)
    return bias.reshape((SWA_KV_HEADS, SWA_GROUP) + dist.shape)


def sink_softmax(s, sinks):
    sink = jnp.broadcast_to(sinks.reshape(SWA_KV_HEADS, SWA_GROUP, 1, 1).astype(jnp.float32), s.shape[:-1] + (1,))
    return jax.nn.softmax(jnp.concatenate([s, sink], axis=-1), axis=-1)[..., :-1]


def swa_banded(q, k, v, sinks, rel_bias):
    B, L = q.shape[:2]
    W = SWA_WINDOW
    nb = L // W
    qb = q.reshape(B, nb, W, SWA_KV_HEADS, SWA_GROUP, SWA_HEAD_DIM)

    def two_blocks(t):
        tp = jnp.pad(t, [(0, 0), (W, 0), (0, 0), (0, 0)]).reshape(B, nb + 1, W, SWA_KV_HEADS, SWA_HEAD_DIM)
        return jnp.concatenate([tp[:, :-1], tp[:, 1:]], axis=2)

    kb, vb = two_blocks(k), two_blocks(v)
    dist = W + jnp.arange(W)[:, None] - jnp.arange(2 * W)[None, :]
    key_pos = (jnp.arange(nb)[:, None] - 1) * W + jnp.arange(2 * W)[None, :]
    valid = ((dist >= 0) & (dist < W))[None] & (key_pos >= 0)[:, None, :]
    bias = t5_bias(dist, rel_bias)
    s = jnp.einsum('bnqhgd,bnkhd->bnhgqk', qb, kb).astype(jnp.float32) * SWA_HEAD_DIM ** -0.5 + bias
    s = jnp.where(valid[None, :, None, None], s, -jnp.inf)
    p = sink_softmax(s, sinks).astype(v.dtype)
    o = jnp.einsum('bnhgqk,bnkhd->bnqhgd', p, vb)
    return o.reshape(B, L, SWA_Q_DIM)


def swa_buffered(q, k, v, k_buf, v_buf, sinks, rel_bias):
    B, T = q.shape[:2]
    wb = k_buf.shape[1]
    k_all = jnp.concatenate([k_buf.astype(k.dtype), k], axis=1)
    v_all = jnp.concatenate([v_buf.astype(v.dtype), v], axis=1)
    dist = wb + jnp.arange(T)[:, None] - jnp.arange(wb + T)[None, :]
    valid = (dist >= 0) & (dist < SWA_WINDOW)
    bias = t5_bias(dist, rel_bias)
    s = jnp.einsum('bqhgd,bkhd->bhgqk', q, k_all).astype(jnp.float32) * SWA_HEAD_DIM ** -0.5 + bias
    s = jnp.where(valid, s, -jnp.inf)
    p = sink_softmax(s, sinks).astype(v.dtype)
    o = jnp.einsum('bhgqk,bkhd->bqhgd', p, v_all).reshape(B, T, SWA_Q_DIM)
    return o, k_all[:, T:], v_all[:, T:]


def decoder_layer(x, ple, past, lw, rel_bias):
    B, L, _ = x.shape
    h = rmsnorm(x, lw['norm_ffn1_pre'])
    x = x + 0.5 * rmsnorm(swiglu(h, lw['ffn1_w_gate'], lw['ffn1_w_up'], lw['ffn1_w_down']), lw['norm_ffn1_post'])

    h = rmsnorm(x, lw['norm_mix_pre'])
    split_points = np.cumsum(PROJ_SPLITS)[:-1].tolist()
    qkv_raw, z, b, a, q_s, k_s, v_s = jnp.split(h @ lw['w_in'], split_points, axis=-1)
    if past is None:
        conv_prev = jnp.zeros((B, CONV_WIDTH - 1, CONV_DIM), x.dtype)
        s0 = jnp.zeros((B, GDN_HEADS, GDN_DK, GDN_DV), jnp.float32)
    else:
        conv_prev, s0, k_buf, v_buf = past
    gdn_out, conv_new, s_new = gdn_branch(qkv_raw, z, b, a, conv_prev, s0, lw['conv_w'],
                                          lw['gdn_a_log'], lw['gdn_dt_bias'], lw['gdn_norm'])
    q_s = q_s.reshape(B, L, SWA_KV_HEADS, SWA_GROUP, SWA_HEAD_DIM)
    k_s = k_s.reshape(B, L, SWA_KV_HEADS, SWA_HEAD_DIM)
    v_s = v_s.reshape(B, L, SWA_KV_HEADS, SWA_HEAD_DIM)
    if past is None:
        swa_out = swa_banded(q_s, k_s, v_s, lw['swa_sinks'], rel_bias)
        wb = min(SWA_WINDOW, L)
        k_new, v_new = k_s[:, L - wb:], v_s[:, L - wb:]
    else:
        swa_out, k_new, v_new = swa_buffered(q_s, k_s, v_s, k_buf, v_buf, lw['swa_sinks'], rel_bias)
    mix = jnp.concatenate([gdn_out, swa_out], axis=-1) @ lw['w_out']
    x = x + rmsnorm(mix, lw['norm_mix_post'])

    h = rmsnorm(x, lw['norm_ffn2_pre'])
    x = x + 0.5 * rmsnorm(swiglu(h, lw['ffn2_w_gate'], lw['ffn2_w_up'], lw['ffn2_w_down']), lw['norm_ffn2_post'])

    gate = jax.nn.sigmoid(x @ lw['ple_gate'])
    x = x + rmsnorm(gate * (ple @ lw['ple_proj']), lw['norm_ple_post'])
    return x, conv_new, s_new, k_new, v_new


def setup_inputs(seed: int = 0) -> dict:
    key = jax.random.key(seed)
    keys = iter(jax.random.split(key, 48))

    def nrm(shape, scale):
        return scale * jax.random.normal(next(keys), shape, jnp.float32)

    def gain(shape):
        return 1.0 + nrm(shape, 0.05)

    swa_buf = min(SWA_WINDOW, PAST_LEN)
    dt = jnp.exp(jax.random.uniform(next(keys), (DEPTH, GDN_HEADS), jnp.float32, math.log(1e-3), math.log(1e-1)))
    a_init = jax.random.uniform(next(keys), (DEPTH, GDN_HEADS), jnp.float32, 1.0, 16.0)
    return {
        'x_prompt': nrm((BATCH, SEQ, D_MODEL), 1.0),
        'x_sample': nrm((DEC_BATCH, DEC_SEQ, D_MODEL), 1.0),
        'state_conv': nrm((DEPTH, DEC_BATCH, CONV_WIDTH - 1, CONV_DIM), 1.0),
        'state_gdn': nrm((DEPTH, DEC_BATCH, GDN_HEADS, GDN_DK, GDN_DV), 0.5),
        'cache_swa_k': nrm((DEPTH, DEC_BATCH, swa_buf, SWA_KV_HEADS, SWA_HEAD_DIM), 1.0),
        'cache_swa_v': nrm((DEPTH, DEC_BATCH, swa_buf, SWA_KV_HEADS, SWA_HEAD_DIM), 1.0),
        'p_prompt': nrm((DEPTH, BATCH, SEQ, PLE_DIM), 1.0),
        'p_sample': nrm((DEPTH, DEC_BATCH, DEC_SEQ, PLE_DIM), 1.0),
        'rel_bias': nrm((NUM_BUCKETS, SWA_HEADS), 0.5),
        'norm_ffn1_pre': gain((DEPTH, D_MODEL)),
        'norm_ffn1_post': gain((DEPTH, D_MODEL)),
        'ffn1_w_gate': nrm((DEPTH, D_MODEL, FFN_DIM), D_MODEL ** -0.5),
        'ffn1_w_up': nrm((DEPTH, D_MODEL, FFN_DIM), D_MODEL ** -0.5),
        'ffn1_w_down': nrm((DEPTH, FFN_DIM, D_MODEL), FFN_DIM ** -0.5),
        'norm_mix_pre': gain((DEPTH, D_MODEL)),
        'norm_mix_post': gain((DEPTH, D_MODEL)),
        'w_in': nrm((DEPTH, D_MODEL, PROJ_DIM), D_MODEL ** -0.5),
        'conv_w': nrm((DEPTH, CONV_WIDTH, CONV_DIM), CONV_WIDTH ** -0.5),
        'gdn_a_log': jnp.log(a_init),
        'gdn_dt_bias': dt + jnp.log(-jnp.expm1(-dt)),
        'gdn_norm': gain((DEPTH, GDN_DV)),
        'swa_sinks': nrm((DEPTH, SWA_HEADS), 0.5),
        'w_out': nrm((DEPTH, MIX_DIM, D_MODEL), MIX_DIM ** -0.5),
        'norm_ffn2_pre': gain((DEPTH, D_MODEL)),
        'norm_ffn2_post': gain((DEPTH, D_MODEL)),
        'ffn2_w_gate': nrm((DEPTH, D_MODEL, FFN_DIM), D_MODEL ** -0.5),
        'ffn2_w_up': nrm((DEPTH, D_MODEL, FFN_DIM), D_MODEL ** -0.5),
        'ffn2_w_down': nrm((DEPTH, FFN_DIM, D_MODEL), FFN_DIM ** -0.5),
        'ple_gate': nrm((DEPTH, D_MODEL, D_MODEL), D_MODEL ** -0.5),
        'ple_proj': nrm((DEPTH, PLE_DIM, D_MODEL), PLE_DIM ** -0.5),
        'norm_ple_post': gain((DEPTH, D_MODEL)),
    }


def reference(x_prompt, x_sample, state_conv, state_gdn, cache_swa_k, cache_swa_v, p_prompt, p_sample,
              rel_bias, norm_ffn1_pre, norm_ffn1_post, ffn1_w_gate, ffn1_w_up, ffn1_w_down,
              norm_mix_pre, norm_mix_post, w_in, conv_w, gdn_a_log, gdn_dt_bias, gdn_norm, swa_sinks,
              w_out, norm_ffn2_pre, norm_ffn2_post, ffn2_w_gate, ffn2_w_up, ffn2_w_down,
              ple_gate, ple_proj, norm_ple_post):
    yp, ys = x_prompt, x_sample
    conv_p, gdn_p, k_p, v_p = [], [], [], []
    conv_s, gdn_s, k_s, v_s = [], [], [], []
    for i in range(DEPTH):
        lw = dict(norm_ffn1_pre=norm_ffn1_pre[i], norm_ffn1_post=norm_ffn1_post[i],
                  ffn1_w_gate=ffn1_w_gate[i], ffn1_w_up=ffn1_w_up[i], ffn1_w_down=ffn1_w_down[i],
                  norm_mix_pre=norm_mix_pre[i], norm_mix_post=norm_mix_post[i], w_in=w_in[i],
                  conv_w=conv_w[i], gdn_a_log=gdn_a_log[i], gdn_dt_bias=gdn_dt_bias[i],
                  gdn_norm=gdn_norm[i], swa_sinks=swa_sinks[i], w_out=w_out[i],
                  norm_ffn2_pre=norm_ffn2_pre[i], norm_ffn2_post=norm_ffn2_post[i],
                  ffn2_w_gate=ffn2_w_gate[i], ffn2_w_up=ffn2_w_up[i], ffn2_w_down=ffn2_w_down[i],
                  ple_gate=ple_gate[i], ple_proj=ple_proj[i], norm_ple_post=norm_ple_post[i])
        yp, c1, s1, k1, v1 = decoder_layer(yp, p_prompt[i], None, lw, rel_bias)
        ys, c2, s2, k2, v2 = decoder_layer(ys, p_sample[i],
                                           (state_conv[i], state_gdn[i], cache_swa_k[i], cache_swa_v[i]),
                                           lw, rel_bias)
        conv_p.append(c1); gdn_p.append(s1); k_p.append(k1); v_p.append(v1)
        conv_s.append(c2); gdn_s.append(s2); k_s.append(k2); v_s.append(v2)
    return (yp, ys,
            jnp.stack(conv_p), jnp.stack(gdn_p), jnp.stack(k_p), jnp.stack(v_p),
            jnp.stack(conv_s), jnp.stack(gdn_s), jnp.stack(k_s), jnp.stack(v_s))
```

```python
import contextlib
import numpy as np
import concourse.bass as bass
import concourse.mybir as mybir
from concourse.bass_utils import run_bass_kernel_spmd

F32 = mybir.dt.float32
BF16 = mybir.dt.bfloat16
AF = mybir.ActivationFunctionType
ALU = mybir.AluOpType
AX = mybir.AxisListType

NCORES = 8
D = 1024
FF = 2816
NJ = FF // 128
PLE = 256
EPS = 1e-6
NMAIN = 2048
NHALO = 128
NSAMP = 64
NTOK1 = NHALO + NMAIN + NSAMP
NPRE = 6144
NTOKF = NPRE + NMAIN + NSAMP
NTOK2 = NMAIN + NSAMP
GW = 256
NG = FF // GW


class Buf:
    __slots__ = ("name", "w", "rs")

    def __init__(self, name):
        self.name = name
        self.w = None
        self.rs = []


class Sched:
    def __init__(self, nc, n_dma_sems=24, same_engine_sync=True):
        self.nc = nc
        self.eng = {"pe": nc.tensor, "act": nc.scalar, "dve": nc.vector,
                    "pool": nc.gpsimd, "sp": nc.sync}
        self.sem = {k: nc.alloc_semaphore("cs_" + k) for k in ("pe", "act", "dve", "pool")}
        self.cnt = {k: 0 for k in self.sem}
        self.seen = {}
        self.dsem = [nc.alloc_semaphore("ds%d" % i) for i in range(n_dma_sems)]
        self.dval = [0] * n_dma_sems
        self.dpool = {"sp": list(range(0, n_dma_sems - 8)), "pool": list(range(n_dma_sems - 8, n_dma_sems))}
        self.dnext = {"sp": 0, "pool": 0}
        self.ses = same_engine_sync
        self.pe_pending = []
        self.n_inst = 0

    def _wait(self, on, ev):
        if ev is None:
            return
        if ev[0] == "c":
            _, e, v = ev
            if e == on and (not self.ses or e == "pe"):
                return
            key = (on, e)
            if self.seen.get(key, 0) >= v:
                return
            self.seen[key] = v
            self.eng[on].wait_ge(self.sem[e], v)
        else:
            _, i, v = ev
            key = (on, "d", i)
            if self.seen.get(key, 0) >= v:
                return
            self.seen[key] = v
            self.eng[on].wait_ge(self.dsem[i], v)

    def _deps(self, on, reads, writes):
        for b in reads:
            self._wait(on, b.w)
        for b in writes:
            self._wait(on, b.w)
            for r in b.rs:
                self._wait(on, r)

    @staticmethod
    def _compact(rs):
        best = {}
        for ev in rs:
            k = ev[:2]
            if k not in best or best[k][2] < ev[2]:
                best[k] = ev
        return list(best.values())

    def _record(self, ev, reads, writes):
        for b in reads:
            b.rs.append(ev)
            if len(b.rs) > 12:
                b.rs = self._compact(b.rs)
        for b in writes:
            b.w = ev
            b.rs = []

    def op(self, on, fn, reads=(), writes=(), inc=True):
        self._deps(on, reads, writes)
        ins = fn()
        self.n_inst += 1
        if on == "pe" and not inc:
            self.pe_pending.append((tuple(reads), tuple(writes)))
            return ins
        self.cnt[on] += 1
        ins.then_inc(self.sem[on], 1)
        ev = ("c", on, self.cnt[on])
        groups = [(tuple(reads), tuple(writes))]
        if on == "pe":
            groups += self.pe_pending
            self.pe_pending = []
        for rd, wr in groups:
            self._record(ev, rd, wr)
        return ins

    def dma(self, on, out, in_, reads=(), writes=(), **kw):
        pl = self.dpool[on]
        i = pl[self.dnext[on] % len(pl)]
        self.dnext[on] += 1
        if self.dval[i] > 0:
            self._wait(on, ("d", i, self.dval[i]))
        self._deps(on, reads, writes)
        ins = self.eng[on].dma_start(out=out, in_=in_, **kw)
        self.n_inst += 1
        self.dval[i] += 16
        ins.then_inc(self.dsem[i], 16)
        self._record(("d", i, self.dval[i]), reads, writes)
        return ins

    def barrier(self):
        for on in ("pe", "act", "dve", "pool", "sp"):
            for e in ("pe", "act", "dve", "pool"):
                if e != on and self.cnt[e] > 0:
                    self._wait(on, ("c", e, self.cnt[e]))
            for i, v in enumerate(self.dval):
                if v > 0:
                    self._wait(on, ("d", i, v))

    def finish(self, bufs):
        for i, v in enumerate(self.dval):
            if v > 0:
                self._wait("sp", ("d", i, v))
        for b in bufs:
            self._wait("sp", b.w)


class Ctx:
    pass


def tiles_of(n_rows):
    out = []
    r = 0
    while r < n_rows:
        n = min(128, n_rows - r)
        out.append((r, n))
        r += n
    return out


def blocks_of(lo, hi, maxw=512):
    out = []
    while lo < hi:
        n = min(maxw, hi - lo)
        out.append((lo, n))
        lo += n
    return out


def rstd_ops(C, ssq_ap, out_ap, n, dim, B_stat):
    S, nc = C.S, C.nc
    S.op("act", lambda: nc.scalar.activation(out=out_ap, in_=ssq_ap, func=AF.Ln, scale=1.0 / dim, bias=C.eps_ap(n)),
         reads=[B_stat, C.B_id], writes=[B_stat])
    S.op("act", lambda: nc.scalar.activation(out=out_ap, in_=out_ap, func=AF.Exp, scale=-0.5),
         reads=[B_stat], writes=[B_stat])


def ffn_phase(C, name, x_src, x_dst, tiles, wg_d, wu_d, wd_d, gpre_d, gpost_d, ple=None):
    S, nc = C.S, C.nc
    S.barrier()
    with contextlib.ExitStack() as es:
        def sb(nm, shape, dt):
            return es.enter_context(nc.sbuf_tensor(name + "_" + nm, shape, dt))

        def ps(nm, shape, dt):
            return es.enter_context(nc.psum_tensor(name + "_" + nm, shape, dt))

        ntl = len(tiles)
        if ntl <= 18:
            halves = [tiles[: (ntl + 1) // 2], tiles[(ntl + 1) // 2:]]
        else:
            halves = [tiles[i:i + FFN_PART] for i in range(0, ntl, FFN_PART)]
            if len(halves[-1]) < 4:
                halves[-2] = halves[-2] + halves[-1]
                halves.pop()
        maxtok = max(sum(t[2] for t in h) for h in halves)
        hT = sb("hT", [128, 8, maxtok], BF16)
        aT = sb("aT", [128, NJ, maxtok], BF16)
        wd = sb("wd", [128, NJ, D], BF16)
        wg = [sb("wg%d" % i, [128, 8, GW], BF16) for i in range(2)]
        wu = [sb("wu%d" % i, [128, 8, GW], BF16) for i in range(2)]
        xin = [sb("xin%d" % i, [128, D], F32) for i in range(2)]
        hb = [sb("hb%d" % i, [128, D], BF16) for i in range(2)]
        yo = [sb("yo%d" % i, [128, D], F32) for i in range(2)]
        tmp = [sb("tmp%d" % i, [128, D], F32) for i in range(2)]
        junk = sb("junk", [128, D], BF16)
        gpre = sb("gpre", [128, D], F32)
        gpost = sb("gpost", [128, D], F32)
        sg = [sb("sg%d" % i, [128, 512], F32) for i in range(2)]
        stat = sb("stat", [128, 64], F32)
        pT = [ps("pT%d" % i, [128, 8, 128], BF16) for i in range(2)]
        pg = [ps("pg%d" % i, [128, 512], F32) for i in range(2)]
        pu = [ps("pu%d" % i, [128, 512], F32) for i in range(2)]
        po = ps("po", [128, D], F32)

        B = {}
        for k in ["hT", "aT", "wd", "junk", "gpre", "gpost", "stat", "po"]:
            B[k] = Buf(name + k)
        for k in ["wg", "wu", "xin", "hb", "yo", "tmp", "sg", "pT", "pg", "pu"]:
            for i in range(2):
                B[k, i] = Buf(name + k + str(i))
        if ple is not None:
            gple = sb("gple", [128, D], F32)
            wpg = sb("wpg", [128, 8, D], BF16)
            wpp = sb("wpp", [128, 2, D], BF16)
            pin = [sb("pin%d" % i, [128, PLE], F32) for i in range(2)]
            pb = [sb("pb%d" % i, [128, PLE], BF16) for i in range(2)]
            xT3 = sb("xT3", [128, 8, 128], BF16)
            pT3 = sb("pT3", [128, 2, 128], BF16)
            prod = sb("prod", [128, D], F32)
            for k in ["gple", "wpg", "wpp", "xT3", "pT3", "prod"]:
                B[k] = Buf(name + k)
            for i in range(2):
                B["pin", i] = Buf(name + "pin%d" % i)
                B["pb", i] = Buf(name + "pb%d" % i)

        S.dma("sp", gpre[:], gpre_d.partition_broadcast(128), writes=[B["gpre"]])
        S.dma("sp", gpost[:], gpost_d.partition_broadcast(128), writes=[B["gpost"]])
        S.op("pool", lambda: nc.gpsimd.memset(stat[:], 0.0), writes=[B["stat"]])
        S.op("pool", lambda: nc.gpsimd.tensor_scalar(out=gpost[:], in0=gpost[:], scalar1=0.5, scalar2=None,
                                                     op0=ALU.mult), reads=[B["gpost"]], writes=[B["gpost"]])
        if ple is not None:
            S.dma("sp", gple[:], ple["gain"].partition_broadcast(128), writes=[B["gple"]])
            S.dma("pool", wpg[:], ple["wg"].rearrange("p (k n) -> p k n", k=8), writes=[B["wpg"]])
            S.dma("pool", wpp[:], ple["wp"].rearrange("p (k n) -> p k n", k=2), writes=[B["wpp"]])

        wd_loaded = False
        tcount = 0
        gcount = 0
        bcount = 0
        for half in halves:
            if not half:
                continue
            col = 0
            cols = []
            for (r0, d0, n) in half:
                s = tcount % 2
                st = 8 * (tcount % 8)
                S.dma("sp", xin[s][:n, :], x_src[r0:r0 + n, :], writes=[B["xin", s]])
                S.op("pool", lambda: nc.gpsimd.memset(stat[:, st:st + 8], 0.0), writes=[B["stat"]])
                S.op("act", lambda: nc.scalar.activation(out=junk[:n, :], in_=xin[s][:n, :], func=AF.Square,
                                                         accum_out=stat[:n, st:st + 1]),
                     reads=[B["xin", s], B["stat"]], writes=[B["junk"], B["stat"]])
                rstd_ops(C, stat[:n, st:st + 1], stat[:n, st + 1:st + 2], n, D, B["stat"])
                S.op("dve", lambda: nc.vector.scalar_tensor_tensor(
                    out=hb[s][:n, :], in0=xin[s][:n, :], scalar=stat[:n, st + 1:st + 2], in1=gpre[:n, :],
                    op0=ALU.mult, op1=ALU.mult), reads=[B["xin", s], B["stat"], B["gpre"]], writes=[B["hb", s]])
                for k in range(8):
                    S.op("pe", lambda: nc.tensor.transpose(pT[s][:, k, :n], hb[s][:n, k * 128:(k + 1) * 128],
                                                           C.idb[:n, :n]),
                         reads=[B["hb", s], C.B_id], writes=[B["pT", s]], inc=(k == 7))
                S.op("act", lambda: nc.scalar.copy(out=hT[:, :, col:col + n], in_=pT[s][:, :, :n]),
                     reads=[B["pT", s]], writes=[B["hT"]])
                cols.append(col)
                col += n
                tcount += 1
            ntok = col
            tblocks = blocks_of(0, ntok)
            for g in range(NG):
                s = gcount % 2
                gcount += 1
                S.dma("pool", wg[s][:], wg_d[g].rearrange("p (k c) -> p k c", k=8), writes=[B["wg", s]])
                S.dma("pool", wu[s][:], wu_d[g].rearrange("p (k c) -> p k c", k=8), writes=[B["wu", s]])
                if not wd_loaded and g == 2:
                    for q in range(2):
                        S.dma("pool", wd[:, q * 11:(q + 1) * 11, :],
                              wd_d[:, q * 11 * D:(q + 1) * 11 * D].rearrange("p (j n) -> p j n", j=11),
                              writes=[B["wd"]])
                    wd_loaded = True
                for jj in range(GW // 128):
                    j = g * (GW // 128) + jj
                    for (b0, bn) in tblocks:
                        bs = bcount % 2
                        bcount += 1
                        for k in range(8):
                            S.op("pe", lambda: nc.tensor.matmul(pg[bs][:, :bn], lhsT=wg[s][:, k, jj * 128:(jj + 1) * 128],
                                                                rhs=hT[:, k, b0:b0 + bn], start=(k == 0), stop=(k == 7)),
                                 reads=[B["wg", s], B["hT"]], writes=[B["pg", bs]], inc=(k == 7))
                        for k in range(8):
                            S.op("pe", lambda: nc.tensor.matmul(pu[bs][:, :bn], lhsT=wu[s][:, k, jj * 128:(jj + 1) * 128],
                                                                rhs=hT[:, k, b0:b0 + bn], start=(k == 0), stop=(k == 7)),
                                 reads=[B["wu", s], B["hT"]], writes=[B["pu", bs]], inc=(k == 7))
                        S.op("act", lambda: nc.scalar.activation(out=sg[bs][:, :bn], in_=pg[bs][:, :bn], func=AF.Silu),
                             reads=[B["pg", bs]], writes=[B["sg", bs]])
                        S.op("dve", lambda: nc.vector.tensor_tensor(out=aT[:, j, b0:b0 + bn], in0=sg[bs][:, :bn],
                                                                    in1=pu[bs][:, :bn], op=ALU.mult),
                             reads=[B["sg", bs], B["pu", bs]], writes=[B["aT"]])
            for ti, (r0, d0, n) in enumerate(half):
                c0 = cols[ti]
                s = tcount % 2
                st = 8 * (tcount % 8)
                tcount += 1
                S.dma("sp", xin[s][:n, :], x_src[r0:r0 + n, :], writes=[B["xin", s]])
                S.op("pool", lambda: nc.gpsimd.memset(stat[:, st:st + 8], 0.0), writes=[B["stat"]])
                use_b = (ple is None) and (ti % 2 == 1)
                for nh in range(2):
                    dst = pg[nh][:n, :] if use_b else po[:n, nh * 512:(nh + 1) * 512]
                    wb = B["pg", nh] if use_b else B["po"]
                    for j in range(NJ):
                        S.op("pe", lambda: nc.tensor.matmul(dst, lhsT=aT[:, j, c0:c0 + n],
                                                            rhs=wd[:, j, nh * 512:(nh + 1) * 512],
                                                            start=(j == 0), stop=(j == NJ - 1)),
                             reads=[B["aT"], B["wd"]], writes=[wb], inc=(j == NJ - 1 and (nh == 1 or use_b)))
                if use_b:
                    for nh in range(2):
                        S.op("act", lambda: nc.scalar.activation(out=junk[:n, nh * 512:(nh + 1) * 512], in_=pg[nh][:n, :],
                                                                 func=AF.Square, accum_out=stat[:n, st + 2 + nh:st + 3 + nh]),
                             reads=[B["pg", nh], B["stat"]], writes=[B["junk"], B["stat"]])
                    S.op("dve", lambda: nc.vector.tensor_tensor(out=stat[:n, st:st + 1], in0=stat[:n, st + 2:st + 3],
                                                                in1=stat[:n, st + 3:st + 4], op=ALU.add),
                         reads=[B["stat"]], writes=[B["stat"]])
                    rstd_ops(C, stat[:n, st:st + 1], stat[:n, st + 1:st + 2], n, D, B["stat"])
                    for nh in range(2):
                        S.op("dve", lambda: nc.vector.scalar_tensor_tensor(
                            out=tmp[s][:n, nh * 512:(nh + 1) * 512], in0=pg[nh][:n, :], scalar=stat[:n, st + 1:st + 2],
                            in1=gpost[:n, nh * 512:(nh + 1) * 512], op0=ALU.mult, op1=ALU.mult),
                            reads=[B["pg", nh], B["stat"], B["gpost"]], writes=[B["tmp", s]])
                else:
                    S.op("act", lambda: nc.scalar.activation(out=junk[:n, :], in_=po[:n, :], func=AF.Square,
                                                             accum_out=stat[:n, st:st + 1]),
                         reads=[B["po"], B["stat"]], writes=[B["junk"], B["stat"]])
                    rstd_ops(C, stat[:n, st:st + 1], stat[:n, st + 1:st + 2], n, D, B["stat"])
                    S.op("dve", lambda: nc.vector.scalar_tensor_tensor(
                        out=tmp[s][:n, :], in0=po[:n, :], scalar=stat[:n, st + 1:st + 2], in1=gpost[:n, :],
                        op0=ALU.mult, op1=ALU.mult), reads=[B["po"], B["stat"], B["gpost"]], writes=[B["tmp", s]])
                S.op("pool", lambda: nc.gpsimd.tensor_tensor(out=yo[s][:n, :], in0=tmp[s][:n, :], in1=xin[s][:n, :],
                                                             op=ALU.add),
                     reads=[B["tmp", s], B["xin", s]], writes=[B["yo", s]])
                if ple is None:
                    S.dma("sp", x_dst[d0:d0 + n, :], yo[s][:n, :], reads=[B["yo", s]], writes=[C.B_dram[x_dst.tensor.name]])
                    continue
                pr0 = ple["prow"](d0)
                S.dma("sp", pin[s][:n, :], ple["p"][pr0:pr0 + n, :], writes=[B["pin", s]])
                S.op("act", lambda: nc.scalar.copy(out=hb[s][:n, :], in_=yo[s][:n, :]),
                     reads=[B["yo", s]], writes=[B["hb", s]])
                S.op("pool", lambda: nc.gpsimd.tensor_copy(out=pb[s][:n, :], in_=pin[s][:n, :]),
                     reads=[B["pin", s]], writes=[B["pb", s]])
                for k in range(8):
                    S.op("pe", lambda: nc.tensor.transpose(pT[0][:, k, :n], hb[s][:n, k * 128:(k + 1) * 128],
                                                           C.idb[:n, :n]),
                         reads=[B["hb", s], C.B_id], writes=[B["pT", 0]], inc=(k == 7))
                S.op("dve", lambda: nc.vector.tensor_copy(out=xT3[:, :, :n], in_=pT[0][:, :, :n]),
                     reads=[B["pT", 0]], writes=[B["xT3"]])
                for k in range(2):
                    S.op("pe", lambda: nc.tensor.transpose(pT[1][:, k, :n], pb[s][:n, k * 128:(k + 1) * 128],
                                                           C.idb[:n, :n]),
                         reads=[B["pb", s], C.B_id], writes=[B["pT", 1]], inc=(k == 1))
                S.op("dve", lambda: nc.vector.tensor_copy(out=pT3[:, :, :n], in_=pT[1][:, 0:2, :n]),
                     reads=[B["pT", 1]], writes=[B["pT3"]])
                for nh in range(2):
                    for k in range(8):
                        S.op("pe", lambda: nc.tensor.matmul(pg[nh][:n, :], lhsT=xT3[:, k, :n],
                                                            rhs=wpg[:, k, nh * 512:(nh + 1) * 512],
                                                            start=(k == 0), stop=(k == 7)),
                             reads=[B["xT3"], B["wpg"]], writes=[B["pg", nh]], inc=(k == 7))
                    for k in range(2):
                        S.op("pe", lambda: nc.tensor.matmul(pu[nh][:n, :], lhsT=pT3[:, k, :n],
                                                            rhs=wpp[:, k, nh * 512:(nh + 1) * 512],
                                                            start=(k == 0), stop=(k == 1)),
                             reads=[B["pT3"], B["wpp"]], writes=[B["pu", nh]], inc=(k == 1))
                    S.op("act", lambda: nc.scalar.activation(out=sg[nh][:n, :], in_=pg[nh][:n, :],
                                                             func=AF.Sigmoid),
                         reads=[B["pg", nh]], writes=[B["sg", nh]])
                    S.op("dve", lambda: nc.vector.tensor_tensor(out=prod[:n, nh * 512:(nh + 1) * 512],
                                                                in0=sg[nh][:n, :],
                                                                in1=pu[nh][:n, :], op=ALU.mult),
                         reads=[B["sg", nh], B["pu", nh]], writes=[B["prod"]])
                S.op("act", lambda: nc.scalar.activation(out=junk[:n, :], in_=prod[:n, :], func=AF.Square,
                                                         accum_out=stat[:n, st + 2:st + 3]),
                     reads=[B["prod"], B["stat"]], writes=[B["junk"], B["stat"]])
                rstd_ops(C, stat[:n, st + 2:st + 3], stat[:n, st + 3:st + 4], n, D, B["stat"])
                S.op("dve", lambda: nc.vector.scalar_tensor_tensor(
                    out=tmp[s][:n, :], in0=prod[:n, :], scalar=stat[:n, st + 3:st + 4], in1=gple[:n, :],
                    op0=ALU.mult, op1=ALU.mult), reads=[B["prod"], B["stat"], B["gple"]], writes=[B["tmp", s]])
                S.op("pool", lambda: nc.gpsimd.tensor_tensor(out=tmp[s][:n, :], in0=tmp[s][:n, :], in1=yo[s][:n, :],
                                                             op=ALU.add),
                     reads=[B["tmp", s], B["yo", s]], writes=[B["tmp", s]])
                S.dma("sp", x_dst[d0:d0 + n, :], tmp[s][:n, :], reads=[B["tmp", s]], writes=[C.B_dram[x_dst.tensor.name]])


O_Z, O_QS, O_KS, O_VS, O_B, O_A, NPROJ = 1536, 2048, 2560, 2688, 2816, 2820, 2824
NEG = -30000.0


DBG_STOP = None
MIXW = (3, 1, 1)
FFN_PART = 12
MIXP = ((0, 1, 2), (3, 4), (5, 6))
FILLER = False


class _Stop(Exception):
    pass


def _ck(tag):
    if DBG_STOP == tag:
        raise _Stop()


def mix_phase(C, d, part="AB"):
    try:
        _mix_phase(C, d, part)
    except _Stop:
        pass


def _mix_phase(C, d, part="AB"):
    S, nc = C.S, C.nc
    do1 = part in ("A", "AB", "F")
    do2 = part in ("B", "AB", "F")
    FUS = part == "F"
    PRE0 = NPRE if FUS else NHALO
    DVW = 128 if FUS else 256
    NT = NMAIN // 128
    S.barrier()
    with contextlib.ExitStack() as es:
        pre = {}
        for nm_, shape_, dt_ in [("cm", [128, 4, 128], F32), ("gmpre", [128, D], F32), ("gmpost", [128, D], F32),
                                 ("gdnn", [128, 128], F32), ("sm", [128, 32], F32), ("xt", [128, D], F32),
                                 ("junk", [128, D], BF16), ("st", [128, 64], F32), ("zsb", [128, 512], F32),
                                 ("oloc", [128, 4, 128], F32), ("oPT", [128, 4, 128], BF16),
                                 ("swaT", [64, 8, 128], BF16), ("swaTs", [64, 8, 64], BF16),
                                 ("og", [128, 4, 128], F32), ("og2", [128, 4, 128], F32), ("sz", [128, 512], F32),
                                 ("ogb", [128, 512], BF16), ("mixT", [128, 4, 128], BF16), ("mtmp", [128, D], F32)]:
            pre[nm_] = es.enter_context(nc.sbuf_tensor("m_" + nm_, shape_, dt_))
        esA = es.enter_context(contextlib.ExitStack())

        def sb(nm, shape, dt=F32):
            if nm in pre:
                return pre[nm]
            return es.enter_context(nc.sbuf_tensor("m_" + nm, shape, dt))

        def sbA(nm, shape, dt=F32):
            return esA.enter_context(nc.sbuf_tensor("m_" + nm, shape, dt))

        def ps(nm, shape, dt=F32):
            return es.enter_context(nc.psum_tensor("m_" + nm, shape, dt))

        Bd = {}

        def B(k):
            if k not in Bd:
                Bd[k] = Buf("m" + str(k))
            return Bd[k]

        def DB(ap):
            return C.B_dram[ap.tensor.name]

        def dve(fn, r, w):
            return S.op("dve", fn, [B(x) if not isinstance(x, Buf) else x for x in r],
                        [B(x) if not isinstance(x, Buf) else x for x in w])

        def act(fn, r, w):
            return S.op("act", fn, [B(x) if not isinstance(x, Buf) else x for x in r],
                        [B(x) if not isinstance(x, Buf) else x for x in w])

        def pool(fn, r, w):
            return S.op("pool", fn, [B(x) if not isinstance(x, Buf) else x for x in r],
                        [B(x) if not isinstance(x, Buf) else x for x in w])

        def pe(fn, r, w, inc=True):
            ins = S.op("pe", fn, [B(x) if not isinstance(x, Buf) else x for x in r],
                       [B(x) if not isinstance(x, Buf) else x for x in w], inc=inc)
            if inc and FILLER and fill_on[0]:
                nc.tensor.matmul(dummy_bank[:, :], lhsT=win[:, 0, 0:128], rhs=win[:, 1, 0:512], start=True, stop=True)
            return ins

        def dma(on, out, in_, r, w, **kw):
            return S.dma(on, out, in_, [B(x) if not isinstance(x, Buf) else x for x in r],
                         [B(x) if not isinstance(x, Buf) else x for x in w], **kw)

        pTb = ps("pTb", [128, 8, 128], BF16)
        NBK = 6 if FILLER else 7
        banks = [ps("bk%d" % i, [128, 512], F32) for i in range(NBK)]
        dummy_bank = ps("bkdummy", [128, 512], F32) if FILLER else None
        bstate = [0]
        fill_on = [False]

        bpool = [tuple(range(NBK))]
        bpos = {}

        def bank():
            pl = bpool[0]
            k_ = bpos.get(pl, 0)
            bpos[pl] = k_ + 1
            i = pl[k_ % len(pl)]
            return banks[i], "bk%d" % i

        def v3(t, n, a):
            return t[:n, :].rearrange("p (a b) -> p a b", a=a)

        win = sbA("win", [128, 8, NPROJ], BF16)
        for k in range(8):
            dma("pool", win[:, k, :], d["win"][:, k * NPROJ:(k + 1) * NPROJ], [], ["win"])
        cw = sbA("cw", [128, 12, 4])
        dma("sp", cw[:], d["convw"].rearrange("p (c j) -> p c j", c=12), [], ["cw"])
        cm = sb("cm", [128, 4, 128])
        dma("sp", cm[:], d["cmask"].rearrange("p (c j) -> p c j", c=4), [], ["cm"])
        TRI, INCL, STRICT, ONES = cm[:, 0, :], cm[:, 1, :], cm[:, 2, :], cm[:, 3, :]
        sel65 = sbA("sel65", [65, 64])
        dma("sp", sel65[:], d["sel65"], [], ["sel65"])
        bprev = sbA("bprev", [128, 8, 128]); bcur = sbA("bcur", [128, 8, 128])
        dma("sp", bprev[:], d["bprev1"].rearrange("p (h q) -> p h q", h=8), [], ["bprev"])
        dma("sp", bcur[:], d["bcur"].rearrange("p (h q) -> p h q", h=8), [], ["bcur"])
        bsc = sbA("bsc", [128, 8, 4]); bsn = sbA("bsn", [4, 8, 4])
        dma("sp", bsc[:], d["bs_cache"].rearrange("p (h q) -> p h q", h=8), [], ["bsc"])
        dma("sp", bsn[:], d["bs_new"].rearrange("p (h q) -> p h q", h=8), [], ["bsn"])
        gmpre = sb("gmpre", [128, D]); gmpost = sb("gmpost", [128, D])
        dma("sp", gmpre[:], d["gmpre"].partition_broadcast(128), [], ["gmpre"])
        dma("sp", gmpost[:], d["gmpost"].partition_broadcast(128), [], ["gmpost"])
        gdnn = sb("gdnn", [128, 128])
        dma("sp", gdnn[:], d["gdnn"].partition_broadcast(128), [], ["gdnn"])
        sm = sb("sm", [128, 32])
        dma("sp", sm[:, 0:4], d["alog"].partition_broadcast(128), [], ["sm"])
        dma("sp", sm[:, 4:8], d["dtb"].partition_broadcast(128), [], ["sm"])
        dma("sp", sm[:, 8:16], d["sinks"].partition_broadcast(128), [], ["sm"])
        dma("sp", sm[:, 16:24], d["rmask"].partition_broadcast(128), [], ["sm"])
        act(lambda: nc.scalar.activation(out=sm[:, 0:4], in_=sm[:, 0:4], func=AF.Exp), ["sm"], ["sm"])
        dve(lambda: nc.vector.tensor_scalar(out=sm[:, 0:4], in0=sm[:, 0:4], scalar1=-1.0, scalar2=None, op0=ALU.mult),
            ["sm"], ["sm"])
        act(lambda: nc.scalar.activation(out=sm[:, 8:16], in_=sm[:, 8:16], func=AF.Exp), ["sm"], ["sm"])
        negA, dtb, rmask = sm[:, 0:4], sm[:, 4:8], sm[:, 16:24]
        idf, idb = C.idf, C.idb
        ID = C.B_id

        xt = sb("xt", [128, D]); junk = sb("junk", [128, D], BF16)
        hbm = sbA("hbm", [128, D], BF16)
        st = sb("st", [128, 64])
        pool(lambda: nc.gpsimd.memset(st[:], 0.0), [], ["st"])
        hT = sbA("hT", [128, 8, 128], BF16)
        rawx = [sbA("rawx%d" % i, [128, 12, 131]) for i in range(2)]
        rawxs = rawx[1][:, :, 0:112].rearrange("p c (s t) -> p c s t", t=7)
        ctmp = sbA("ctmp", [128, 12, 128])
        sq = ctmp[:, 0:8, :]
        rn = sbA("rn", [128, 8, 128])
        zsb = sb("zsb", [128, 512])
        ksT = [sbA("ksT%d" % i, [64, 2, 128], BF16) for i in range(4)]
        vaug = [sbA("vaug%d" % i, [128, 2, 65], BF16) for i in range(4)]
        for i in range(4):
            pool(lambda: nc.gpsimd.memset(vaug[i][:], 1.0), [], ["vaug%d" % i])

        class Slot:
            pass
        PS = []
        for i in range(2):
            P_ = Slot()
            P_.nm = {}
            for nm_, shape_, dt_ in [("qkf", [128, 8, 128], F32), ("qkb", [128, 8, 128], BF16),
                                     ("cacc", [128, 12, 128], F32), ("misc", [128, 264], F32),
                                     ("qsT", [64, 8, 128], BF16), ("qdecT", [128, 4, 128], BF16),
                                     ("qkT", [128, 4, 128], BF16), ("kdec", [128, 4, 128], BF16),
                                     ("wT", [128, 4, 128], BF16), ("uaug", [128, 4, DVW], F32), ("glb", [128, 8], F32)]:
                setattr(P_, nm_, sbA("%s_%d" % (nm_, i), shape_, dt_))
                P_.nm[nm_] = "%s_%d" % (nm_, i)
            P_.ys = P_.cacc
            pool(lambda: nc.gpsimd.memset(P_.uaug[:], 0.0), [], [P_.nm["uaug"]])
            PS.append(P_)
        qsT3 = [PS[0].qsT, PS[1].qsT, sbA("qsT_2", [64, 8, 128], BF16)]
        _slots = {}

        def slot(ti):
            key = (ti % 2, ti % 3)
            if key not in _slots:
                Q = Slot()
                Q.__dict__.update(PS[ti % 2].__dict__)
                Q.nm = dict(PS[ti % 2].nm)
                Q.qsT = qsT3[ti % 3]
                Q.nm["qsT"] = "qsT_%d" % (ti % 3)
                _slots[key] = Q
            return _slots[key]
        g4 = sbA("g4", [128, 48])
        pool(lambda: nc.gpsimd.memset(g4[:], 0.0), [], ["g4"])
        diag = sbA("t1", [128, 4, 128]); erow = sbA("erow", [128, 4, 128])
        t1 = diag; dec = sbA("dec", [128, 4, 128]); decT = sbA("decT", [128, 4, 128])
        ktm = sbA("ktm", [128, 4, 128]); vtm = sbA("vtm", [128, 4, 128])
        kbg = sbA("kbg", [128, 4, 128], BF16)
        vb = sbA("vb", [128, 4, 128], BF16)
        X = [sbA("X%d" % i, [128, 4, 128]) for i in range(2)]
        XT = [sbA("XT%d" % i, [128, 4, 128]) for i in range(2)]
        Nm = X[1]
        TT = sbA("TT", [128, 4, 128]); TTb = sbA("TTb", [128, 4, 128], BF16)
        vnew = sbA("vnew", [128, 4, DVW], BF16)
        Sf = sbA("Sf", [128, 4, DVW]); Sb_ = sbA("Sb", [128, 4, DVW], BF16)
        oloc = sb("oloc", [128, 4, 128]); oPT = sb("oPT", [128, 4, 128], BF16)
        tp = sbA("tp", [128, 512]); pTp = sbA("pTp", [128, 512], BF16)
        tc_ = sbA("tc", [128, 512]); pTc = sbA("pTc", [128, 512], BF16)
        oTa = sbA("oTa", [65, 512]); rden = sbA("rden", [64, 512])

        def drain(g):
            for _ in g:
                pass

        def interleave(*gens, weights=None, pools=None):
            ws = list(weights) if weights else [1] * len(gens)
            ps_ = list(pools) if pools else [tuple(range(NBK))] * len(gens)
            gw = [(g, w_, p_) for g, w_, p_ in zip(gens, ws, ps_) if g is not None]
            full = bpool[0]
            while gw:
                for g, w_, p_ in list(gw):
                    bpool[0] = tuple(p_)
                    for _ in range(w_):
                        try:
                            next(g)
                        except StopIteration:
                            gw.remove((g, w_, p_))
                            break
            bpool[0] = full
        swaT = sb("swaT", [64, 8, 128], BF16); swaTs = sb("swaTs", [64, 8, 64], BF16)
        kcf = sbA("kcf", [128, 128]); vcf = sbA("vcf", [128, 128])
        og = sb("og", [128, 4, 128]); og2 = sb("og2", [128, 4, 128]); sz = sb("sz", [128, 512])
        ogb = sb("ogb", [128, 512], BF16)
        mixT = sb("mixT", [128, 4, 128], BF16)
        mtmp = sb("mtmp", [128, D])
        cvo = sbA("cvo", [48, 512])
        stc = [0]

        def newstat(k=4):
            c = stc[0]
            stc[0] += k
            assert stc[0] <= 64
            return c

        def load_norm_T(r0, n):
            dma("sp", xt[:n, :], d["x1"][r0:r0 + n, :], [DB(d["x1"])], ["xt"])
            pool(lambda: nc.gpsimd.memset(st[:, 0:4], 0.0), [], ["st"])
            yield
            act(lambda: nc.scalar.activation(out=junk[:n, :], in_=xt[:n, :], func=AF.Square, accum_out=st[:n, 0:1]),
                ["xt", "st"], ["junk", "st"])
            yield
            rstd_ops(C, st[:n, 0:1], st[:n, 1:2], n, D, B("st"))
            dve(lambda: nc.vector.scalar_tensor_tensor(out=hbm[:n, :], in0=xt[:n, :], scalar=st[:n, 1:2],
                                                       in1=gmpre[:n, :], op0=ALU.mult, op1=ALU.mult),
                ["xt", "st", "gmpre"], ["hbm"])
            yield
            for k in range(8):
                pe(lambda: nc.tensor.transpose(pTb[:, k, :n], hbm[:n, k * 128:(k + 1) * 128], idb[:n, :n]),
                   ["hbm", ID], ["pTb"], inc=(k == 7))
            act(lambda: nc.scalar.copy(out=hT[:, :, :n], in_=pTb[:, :, :n]), ["pTb"], ["hT"])
            yield

        def proj_feat(P, n, raw_dst, with_qs=True, with_q=True):
            traw = ctmp[:, :, :].rearrange("p c t -> p (c t)")
            nb0 = 0 if with_q else 1
            for nb in range(nb0, 3):
                bk, bn = bank()
                for k in range(8):
                    pe(lambda: nc.tensor.matmul(bk[:n, :], lhsT=hT[:, k, :n], rhs=win[:, k, nb * 512:(nb + 1) * 512],
                                                start=(k == 0), stop=(k == 7)), ["win", "hT"], [bn], inc=(k == 7))
                act(lambda: nc.scalar.copy(out=traw[:n, nb * 512:(nb + 1) * 512], in_=bk[:n, :]), [bn], ["ctmp"])
                yield
            for g in range(nb0, 3):
                bk, bn = bank()
                for cc in range(4):
                    c = g * 4 + cc
                    pe(lambda: nc.tensor.transpose(bk[:, cc * 128:cc * 128 + n], traw[:n, c * 128:(c + 1) * 128], idf[:n, :n]),
                       ["ctmp", ID], [bn], inc=(cc == 3))
                raw_dst(g, v3(bk, 128, 4)[:, :, :n], bn)
                yield
        def proj_qs(P, n, with_qs=True):
            if with_qs:
                bk, bn = bank()
                for k in range(8):
                    pe(lambda: nc.tensor.matmul(bk[:n, :], lhsT=hT[:, k, :n], rhs=win[:, k, O_QS:O_QS + 512],
                                                start=(k == 0), stop=(k == 7)), ["win", "hT"], [bn], inc=(k == 7))
                act(lambda: nc.scalar.copy(out=hbm[:n, 0:512], in_=bk[:n, :]), [bn], ["hbm"])
                yield
                for h in range(8):
                    pe(lambda: nc.tensor.transpose(pTb[:64, h, :n], hbm[:n, h * 64:(h + 1) * 64], idb[:n, :n]),
                       ["hbm", ID], ["pTb"], inc=(h == 7))
                act(lambda: nc.scalar.copy(out=P.qsT[:, :, :n], in_=pTb[:64, :, :n]), ["pTb"], [P.nm["qsT"]])
                yield

        def proj_ksT(P, n, dst, dname):
            act(lambda: nc.scalar.copy(out=hbm[:n, 512:640], in_=P.misc[:n, 0:128]), [P.nm["misc"]], ["hbm"])
            yield
            for kh in range(2):
                pe(lambda: nc.tensor.transpose(pTb[:64, kh, :n], hbm[:n, 512 + kh * 64:512 + (kh + 1) * 64], idb[:n, :n]),
                   ["hbm", ID], ["pTb"], inc=(kh == 1))
            act(lambda: nc.scalar.copy(out=dst[:, :, :n], in_=pTb[:64, 0:2, :n]), ["pTb"], [dname])
            yield

        def proj_tok(P, c0, n, with_z=True, zdst=None, zname="zsb"):
            zdst = zsb if zdst is None else zdst
            if with_z:
                bk, bn = bank()
                for k in range(8):
                    pe(lambda: nc.tensor.matmul(bk[:n, :], lhsT=hT[:, k, c0:c0 + n], rhs=win[:, k, O_Z:O_Z + 512],
                                                start=(k == 0), stop=(k == 7)), ["win", "hT"], [bn], inc=(k == 7))
                act(lambda: nc.scalar.copy(out=zdst[:n, :], in_=bk[:n, :]), [bn], [zname])
                yield
            bk2, bn2 = bank()
            for k in range(8):
                pe(lambda: nc.tensor.matmul(bk2[:n, 0:264], lhsT=hT[:, k, c0:c0 + n], rhs=win[:, k, O_KS:O_KS + 264],
                                            start=(k == 0), stop=(k == 7)), ["win", "hT"], [bn2], inc=(k == 7))
            dve(lambda: nc.vector.tensor_copy(out=P.misc[:n, :], in_=bk2[:n, 0:264]), [bn2], [P.nm["misc"]])
            yield

        def conv_l2(P, n, taps, ys_v, full=True):
            c_lo = 0 if full else 4
            cwv = cw[:, c_lo:12, :]
            cw_b = lambda j, shape: cwv[:, :, j:j + 1].to_broadcast(shape) if len(shape) == 3 else \
                cwv[:, :, j:j + 1].unsqueeze(3).to_broadcast(shape)
            tp_ = lambda j: taps(j)[:, c_lo:12]
            shape = list(tp_(0).shape)
            accv = P.cacc[:, c_lo:12, :n] if len(shape) == 3 else P.cacc[:, c_lo:12, :n].rearrange("p c (s t) -> p c s t", t=4)
            tmpv = ctmp[:, c_lo:12, :n] if len(shape) == 3 else ctmp[:, c_lo:12, :n].rearrange("p c (s t) -> p c s t", t=4)
            dve(lambda: nc.vector.tensor_tensor(out=accv, in0=tp_(0), in1=cw_b(0, shape), op=ALU.mult),
                ["rawcur", "cw"], [P.nm["cacc"]])
            yield
            for j in range(1, 4):
                dve(lambda: nc.vector.tensor_tensor(out=tmpv, in0=tp_(j), in1=cw_b(j, shape), op=ALU.mult),
                    ["rawcur", "cw"], ["ctmp"])
                yield
                dve(lambda: nc.vector.tensor_tensor(out=accv, in0=accv, in1=tmpv, op=ALU.add), [P.nm["cacc"], "ctmp"], [P.nm["cacc"]])
                yield
            act(lambda: nc.scalar.activation(out=P.ys[:, c_lo:12, :n], in_=P.cacc[:, c_lo:12, :n], func=AF.Silu), [P.nm["cacc"]], [P.nm["cacc"]])
            yield
            pool(lambda: nc.gpsimd.tensor_tensor(out=sq[:, c_lo:8, :n], in0=P.ys[:, c_lo:8, :n], in1=P.ys[:, c_lo:8, :n], op=ALU.mult),
                 [P.nm["cacc"]], ["ctmp"])
            yield
            for g in range(0 if full else 1, 2):
                bk, bn = bank()
                pe(lambda: nc.tensor.matmul(bk[:, :4 * n], lhsT=ONES, rhs=sq[:, g * 4:(g + 1) * 4, :n],
                                            start=True, stop=True), ["ctmp", "cm"], [bn])
                act(lambda: nc.scalar.activation(out=rn[:, g * 4:(g + 1) * 4, :n],
                                                 in_=bk[:, :4 * n].rearrange("p (a b) -> p a b", a=4),
                                                 func=AF.Ln, bias=C.epsc[:, 0:1]), [bn, ID], ["rn"])
                yield
            act(lambda: nc.scalar.activation(out=rn[:, c_lo:8, :n], in_=rn[:, c_lo:8, :n], func=AF.Exp, scale=-0.5),
                ["rn"], ["rn"])
            yield
            if full:
                dve(lambda: nc.vector.scalar_tensor_tensor(out=P.qkf[:, 0:4, :n], in0=P.ys[:, 0:4, :n], scalar=128.0 ** -0.5,
                                                           in1=rn[:, 0:4, :n], op0=ALU.mult, op1=ALU.mult),
                    [P.nm["cacc"], "rn"], [P.nm["qkf"]])
                yield
            dve(lambda: nc.vector.tensor_tensor(out=P.qkf[:, 4:8, :n], in0=P.ys[:, 4:8, :n], in1=rn[:, 4:8, :n], op=ALU.mult),
                [P.nm["cacc"], "rn"], [P.nm["qkf"]])
            yield
            act(lambda: nc.scalar.copy(out=P.qkb[:, c_lo:8, :n], in_=P.qkf[:, c_lo:8, :n]), [P.nm["qkf"]], [P.nm["qkb"]])
            yield

        def gdn_pre(P, n, c0, nlev, full=True):
            bcol = lambda ap: ap.unsqueeze(2).to_broadcast([n, 4, 128])
            bcn = lambda ap: ap.unsqueeze(2).to_broadcast([n, 4, n])
            G = g4
            dve(lambda: nc.vector.tensor_tensor(out=G[:n, 0:4], in0=P.misc[:n, 260:264], in1=dtb[:n, :], op=ALU.add),
                [P.nm["misc"], "sm"], ["g4"])
            yield
            dve(lambda: nc.vector.scalar_tensor_tensor(out=G[:n, 4:8], in0=G[:n, 0:4], scalar=-1.0, in1=G[:n, 0:4],
                                                       op0=ALU.mult, op1=ALU.max), ["g4"], ["g4"])
            yield
            act(lambda: nc.scalar.activation(out=G[:n, 4:8], in_=G[:n, 4:8], func=AF.Exp, scale=-1.0), ["g4"], ["g4"])
            yield
            act(lambda: nc.scalar.activation(out=G[:n, 4:8], in_=G[:n, 4:8], func=AF.Ln, bias=C.epsc[:n, 1:2]),
                ["g4", ID], ["g4"])
            yield
            dve(lambda: nc.vector.scalar_tensor_tensor(out=G[:n, 8:12], in0=G[:n, 0:4], scalar=0.0, in1=G[:n, 4:8],
                                                       op0=ALU.max, op1=ALU.add), ["g4"], ["g4"])
            yield
            dve(lambda: nc.vector.tensor_tensor(out=G[:n, 12:16], in0=G[:n, 8:12], in1=negA[:n, :], op=ALU.mult),
                ["g4", "sm"], ["g4"])
            yield
            act(lambda: nc.scalar.activation(out=G[:n, 16:20], in_=P.misc[:n, 256:260], func=AF.Exp, scale=-1.0),
                [P.nm["misc"]], ["g4"])
            yield
            dve(lambda: nc.vector.tensor_scalar(out=G[:n, 16:20], in0=G[:n, 16:20], scalar1=1.0, scalar2=None,
                                                op0=ALU.add), ["g4"], ["g4"])
            yield
            dve(lambda: nc.vector.reciprocal(out=G[:n, 16:20], in_=G[:n, 16:20]), ["g4"], ["g4"])
            yield
            dve(lambda: nc.vector.tensor_scalar(out=G[:n, 20:24], in0=G[:n, 16:20], scalar1=-1.0, scalar2=None,
                                                op0=ALU.mult), ["g4"], ["g4"])
            yield
            gcol, beta, nbeta = G[:n, 12:16], G[:n, 16:20], G[:n, 20:24]
            _ck("p1")
            bk, bn = bank()
            pe(lambda: nc.tensor.matmul(bk[:n, 0:4], lhsT=TRI[:n, :n], rhs=gcol, start=True, stop=True),
               ["g4", "cm"], [bn], inc=False)
            pe(lambda: nc.tensor.matmul(bk[:, 4:8], lhsT=ONES[:n, :], rhs=gcol, start=True, stop=True),
               ["g4", "cm"], [bn])
            dve(lambda: nc.vector.tensor_copy(out=G[:n, 24:28], in_=bk[:n, 0:4]), [bn], ["g4"])
            yield
            dve(lambda: nc.vector.tensor_copy(out=P.glb[:, 0:4], in_=bk[:, 4:8]), [bn], [P.nm["glb"]])
            yield
            gc = G[:n, 24:28]
            act(lambda: nc.scalar.activation(out=G[:n, 28:32], in_=gc, func=AF.Exp), ["g4"], ["g4"])
            yield
            dve(lambda: nc.vector.tensor_tensor(out=G[:n, 32:36], in0=P.glb[:n, 0:4], in1=gc, op=ALU.subtract),
                ["g4", P.nm["glb"]], ["g4"])
            yield
            act(lambda: nc.scalar.activation(out=G[:n, 32:36], in_=G[:n, 32:36], func=AF.Exp), ["g4"], ["g4"])
            yield
            act(lambda: nc.scalar.activation(out=P.glb[:, 4:8], in_=P.glb[:, 0:4], func=AF.Exp), [P.nm["glb"]], [P.nm["glb"]])
            yield
            eg, ekl = G[:n, 28:32], G[:n, 32:36]
            _ck("p2")
            for h in range(4):
                dve(lambda: nc.vector.tensor_scalar(out=diag[:n, h, :n], in0=idf[:n, :n], scalar1=gc[:, h:h + 1],
                                                    scalar2=None, op0=ALU.mult), ["g4", ID], ["t1"])
                yield
            gbk, gbn = bank()
            pe(lambda: nc.tensor.matmul(gbk[:, :4 * n], lhsT=ONES[:n, :], rhs=diag[:n, :, :n],
                                        start=True, stop=True), ["t1", "cm"], [gbn])
            grow = gbk[:, :4 * n].rearrange("p (a b) -> p a b", a=4)
            _ck("p2a")
            if full:
                act(lambda: nc.scalar.activation(out=erow[:, :, :n], in_=grow[:, :, :n], func=AF.Exp), [gbn], ["erow"])
                yield
            _ck("p2b")
            dve(lambda: nc.vector.tensor_scalar(out=G[:n, 44:48], in0=gc, scalar1=-1.0, scalar2=None, op0=ALU.mult),
                ["g4"], ["g4"])
            yield
            for h in range(4):
                act(lambda: nc.scalar.activation(out=dec[:n, h, :n], in_=grow[:n, h, :n], func=AF.Exp, scale=-1.0,
                                                 bias=gc[:, h:h + 1]), [gbn, "g4"], ["dec"])
                yield
            _ck("p2c")
            dve(lambda: nc.vector.scalar_tensor_tensor(out=dec[:n, :, :n], in0=dec[:n, :, :n], scalar=1.0,
                                                       in1=INCL[:n, :n].unsqueeze(1).to_broadcast([n, 4, n]),
                                                       op0=ALU.min, op1=ALU.mult), ["dec", "cm"], ["dec"])
            yield
            for h in range(4 if full else 0):
                act(lambda: nc.scalar.activation(out=decT[:n, h, :n], in_=grow[:n, h, :n], func=AF.Exp, scale=1.0,
                                                 bias=G[:n, 44 + h:45 + h]), [gbn, "g4"], ["decT"])
                yield
            if full:
                dve(lambda: nc.vector.scalar_tensor_tensor(out=decT[:n, :, :n], in0=decT[:n, :, :n], scalar=1.0,
                                                           in1=TRI[:n, :n].unsqueeze(1).to_broadcast([n, 4, n]),
                                                           op0=ALU.min, op1=ALU.mult), ["decT", "cm"], ["decT"])
                yield
                dve(lambda: nc.vector.tensor_tensor(out=P.qdecT[:, :, :n], in0=P.qkf[:, 0:4, c0:c0 + n], in1=erow[:, :, :n],
                                                    op=ALU.mult), [P.nm["qkf"], "erow"], [P.nm["qdecT"]])
                yield
            _ck("p3")
            bk, bn = bank()
            for h in range(4):
                pe(lambda: nc.tensor.transpose(bk[:n, h * 128:(h + 1) * 128], P.qkf[:, 4 + h, c0:c0 + n], idf[:, :]),
                   [P.nm["qkf"], ID], [bn], inc=(h == 3))
            act(lambda: nc.scalar.copy(out=ktm[:n, :, :], in_=v3(bk, n, 4)), [bn], ["ktm"])
            yield
            bk, bn = bank()
            for h in range(4):
                pe(lambda: nc.tensor.transpose(bk[:n, h * 128:(h + 1) * 128], P.ys[:, 8 + h, c0:c0 + n], idf[:, :]),
                   [P.nm["cacc"], ID], [bn], inc=(h == 3))
            act(lambda: nc.scalar.copy(out=vtm[:n, :, :], in_=v3(bk, n, 4)), [bn], ["vtm"])
            yield
            dve(lambda: nc.vector.tensor_tensor(out=G[:n, 36:40], in0=beta, in1=eg, op=ALU.mult), ["g4"], ["g4"])
            yield
            dve(lambda: nc.vector.tensor_tensor(out=kbg[:n], in0=ktm[:n], in1=bcol(G[:n, 36:40]), op=ALU.mult),
                ["ktm", "g4"], ["kbg"])
            yield
            pool(lambda: nc.gpsimd.tensor_tensor(out=P.kdec[:n], in0=ktm[:n], in1=bcol(ekl), op=ALU.mult),
                 ["ktm", "g4"], [P.nm["kdec"]])
            yield
            pool(lambda: nc.gpsimd.tensor_tensor(out=vb[:n], in0=vtm[:n], in1=bcol(beta), op=ALU.mult),
                 ["vtm", "g4"], ["vb"])
            yield
            _ck("p4")
            kbk, kbn = bank()
            for h in range(4):
                pe(lambda: nc.tensor.matmul(kbk[:n, h * 128:h * 128 + n], lhsT=P.qkb[:, 4 + h, c0:c0 + n],
                                            rhs=P.qkb[:, 4 + h, c0:c0 + n], start=True, stop=True),
                   [P.nm["qkb"]], [kbn], inc=(h == 3))
            if full:
                qbk, qbn = bank()
                for h in range(4):
                    pe(lambda: nc.tensor.matmul(qbk[:n, h * 128:h * 128 + n], lhsT=P.qkb[:, 4 + h, c0:c0 + n],
                                                rhs=P.qkb[:, h, c0:c0 + n], start=True, stop=True),
                       [P.nm["qkb"]], [qbn], inc=(h == 3))
            dve(lambda: nc.vector.tensor_tensor(out=Nm[:n, :, :n], in0=v3(kbk, n, 4)[:, :, :n], in1=dec[:n, :, :n],
                                                op=ALU.mult), [kbn, "dec"], ["X1"])
            yield
            pool(lambda: nc.gpsimd.tensor_tensor(out=Nm[:n, :, :n], in0=Nm[:n, :, :n], in1=bcn(nbeta), op=ALU.mult),
                 ["X1", "g4"], ["X1"])
            yield
            pool(lambda: nc.gpsimd.tensor_tensor(out=X[0][:n, :, :n], in0=Nm[:n, :, :n],
                                                 in1=STRICT[:n, :n].unsqueeze(1).to_broadcast([n, 4, n]), op=ALU.mult),
                 ["X1", "cm"], ["X0"])
            yield
            if full:
                dve(lambda: nc.vector.tensor_tensor(out=P.qkT[:n, :, :n], in0=v3(qbk, n, 4)[:, :, :n], in1=decT[:n, :, :n],
                                                    op=ALU.mult), [qbn, "decT"], [P.nm["qkT"]])
                yield
            _ck("p5")
            bk, bn = bank()
            for h in range(4):
                pe(lambda: nc.tensor.transpose(bk[:n, h * 128:h * 128 + n], X[0][:n, h, :n], idf[:n, :n]),
                   ["X0", ID], [bn], inc=(h == 3))
            act(lambda: nc.scalar.copy(out=XT[0][:n, :, :n], in_=v3(bk, n, 4)[:, :, :n]), [bn], ["XT0"])
            yield
            dve(lambda: nc.vector.tensor_tensor(out=TT[:n, :, :n], in0=XT[0][:n, :, :n],
                                                in1=idf[:n, :n].unsqueeze(1).to_broadcast([n, 4, n]), op=ALU.add),
                ["XT0", ID], ["TT"])
            yield
            cur = 0
            for lev in range(1, nlev):
                nx = 1 - cur
                b1, b1n = bank()
                for h in range(4):
                    pe(lambda: nc.tensor.matmul(b1[:n, h * 128:h * 128 + n], lhsT=XT[cur][:n, h, :n],
                                                rhs=X[cur][:n, h, :n], start=True, stop=True),
                       ["X%d" % cur, "XT%d" % cur], [b1n], inc=(h == 3))
                act(lambda: nc.scalar.copy(out=X[nx][:n, :, :n], in_=v3(b1, n, 4)[:, :, :n]), [b1n], ["X%d" % nx])
                yield
                if lev < nlev - 1:
                    b2, b2n = bank()
                    for h in range(4):
                        pe(lambda: nc.tensor.matmul(b2[:n, h * 128:h * 128 + n], lhsT=X[cur][:n, h, :n],
                                                    rhs=XT[cur][:n, h, :n], start=True, stop=True),
                           ["X%d" % cur, "XT%d" % cur], [b2n], inc=(h == 3))
                    dve(lambda: nc.vector.tensor_copy(out=XT[nx][:n, :, :n], in_=v3(b2, n, 4)[:, :, :n]),
                        [b2n], ["XT%d" % nx])
                    yield
                b3, b3n = bank()
                for h in range(4):
                    pe(lambda: nc.tensor.matmul(b3[:n, h * 128:h * 128 + n], lhsT=X[nx][:n, h, :n],
                                                rhs=TT[:n, h, :n], start=True, stop=True),
                       ["X%d" % nx, "TT"], [b3n], inc=(h == 3))
                dve(lambda: nc.vector.tensor_tensor(out=TT[:n, :, :n], in0=TT[:n, :, :n], in1=v3(b3, n, 4)[:, :, :n],
                                                    op=ALU.add), ["TT", b3n], ["TT"])
                yield
                cur = nx
            act(lambda: nc.scalar.copy(out=TTb[:n, :, :n], in_=TT[:n, :, :n]), ["TT"], ["TTb"])
            yield
            _ck("p6")
            bk, bn = bank()
            for h in range(4):
                pe(lambda: nc.tensor.matmul(bk[:n, h * 128:(h + 1) * 128], lhsT=TTb[:n, h, :n], rhs=vb[:n, h, :],
                                            start=True, stop=True), ["TTb", "vb"], [bn], inc=(h == 3))
            act(lambda: nc.scalar.copy(out=P.uaug[:n, :, 0:128], in_=v3(bk, n, 4)), [bn], [P.nm["uaug"]])
            yield
            bk, bn = bank()
            for h in range(4):
                pe(lambda: nc.tensor.matmul(bk[:, h * 128:h * 128 + n], lhsT=kbg[:n, h, :], rhs=TTb[:n, h, :n],
                                            start=True, stop=True), ["TTb", "kbg"], [bn], inc=(h == 3))
            dve(lambda: nc.vector.tensor_copy(out=P.wT[:, :, :n], in_=v3(bk, 128, 4)[:, :, :n]), [bn], [P.nm["wT"]])
            yield

        def gdn_scan(P, n, dvw, state_only=False):
            aug = dvw == 256
            pb = [bank() for _ in range(2 if aug else 1)]
            per = 2 if aug else 4

            def reg(pbk, h):
                return pbk[h // per][0][:, (h % per) * dvw:(h % per + 1) * dvw]

            for h in range(4):
                pe(lambda: nc.tensor.matmul(reg(pb, h)[:n, :], lhsT=P.wT[:, h, :n], rhs=Sb_[:, h, :dvw],
                                            start=True, stop=True), [P.nm["wT"], "Sb"], [pb[h // per][1]],
                   inc=(h % per == per - 1))
            for i, (bk, bn) in enumerate(pb):
                dve(lambda: nc.vector.tensor_tensor(out=vnew[:n, i * per:(i + 1) * per, :dvw],
                                                    in0=P.uaug[:n, i * per:(i + 1) * per, :dvw],
                                                    in1=bk[:n, :per * dvw].rearrange("p (a b) -> p a b", a=per),
                                                    op=ALU.subtract), [P.nm["uaug"], bn], ["vnew"])
                yield
            if not state_only:
                obk, obn = bank()
                for h in range(4):
                    pe(lambda: nc.tensor.matmul(obk[:n, h * 128:(h + 1) * 128], lhsT=P.qdecT[:, h, :n], rhs=Sb_[:, h, 0:128],
                                                start=True, stop=False), [P.nm["qdecT"], "Sb"], [obn], inc=False)
                    pe(lambda: nc.tensor.matmul(obk[:n, h * 128:(h + 1) * 128], lhsT=P.qkT[:n, h, :n], rhs=vnew[:n, h, 0:128],
                                                start=False, stop=True), [P.nm["qkT"], "vnew"], [obn], inc=(h == 3))
                act(lambda: nc.scalar.copy(out=oloc[:n], in_=v3(obk, n, 4)), [obn], ["oloc"])
                yield
            if aug:
                pbk, pbn = bank()
                for h in range(4):
                    pe(lambda: nc.tensor.matmul(pbk[:, h * 128:h * 128 + n], lhsT=Sb_[:, h, 128:256], rhs=P.qdecT[:, h, :n],
                                                start=True, stop=False), [P.nm["qdecT"], "Sb"], [pbn], inc=False)
                    pe(lambda: nc.tensor.matmul(pbk[:, h * 128:h * 128 + n], lhsT=vnew[:n, h, 128:256], rhs=P.qkT[:n, h, :n],
                                                start=False, stop=True), [P.nm["qkT"], "vnew"], [pbn], inc=(h == 3))
                dve(lambda: nc.vector.tensor_copy(out=oPT[:, :, :n], in_=v3(pbk, 128, 4)[:, :, :n]), [pbn], ["oPT"])
                yield
            sbk = [bank() for _ in range(2 if aug else 1)]
            for h in range(4):
                pe(lambda: nc.tensor.matmul(reg(sbk, h), lhsT=P.kdec[:n, h, :], rhs=vnew[:n, h, :dvw],
                                            start=True, stop=True), [P.nm["kdec"], "vnew"], [sbk[h // per][1]],
                   inc=(h % per == per - 1))
            for h in range(4):
                dve(lambda: nc.vector.scalar_tensor_tensor(out=Sf[:, h, :dvw], in0=Sf[:, h, :dvw], scalar=P.glb[:, 4 + h:5 + h],
                                                           in1=reg(sbk, h), op0=ALU.mult, op1=ALU.add),
                    ["Sf", P.nm["glb"], sbk[h // per][1]], ["Sf"])
                yield
            act(lambda: nc.scalar.copy(out=Sb_[:, :, :dvw], in_=Sf[:, :, :dvw]), ["Sf"], ["Sb"])
            yield

        def swa(P, n, qc0, kprevT, kpn, vprev, vpn, kcurT, kcn, vcur, vcn, bp, bpn, bc, bcn_, dst, dstn, dst_c0):
            for kh in range(2):
                hs = slice(kh * 4, (kh + 1) * 4)
                pb_, pbn = bank()
                pe(lambda: nc.tensor.matmul(pb_[:, :4 * n], lhsT=kprevT[:, kh, :], rhs=P.qsT[:, hs, qc0:qc0 + n],
                                            start=True, stop=True), [kpn, P.nm["qsT"]], [pbn])
                cb_, cbn = bank()
                pe(lambda: nc.tensor.matmul(cb_[:n, :4 * n], lhsT=kcurT[:, kh, qc0:qc0 + n], rhs=P.qsT[:, hs, qc0:qc0 + n],
                                            start=True, stop=True), [kcn, P.nm["qsT"]], [cbn])
                dve(lambda: nc.vector.scalar_tensor_tensor(out=tp[:, :4 * n].rearrange("p (a b) -> p a b", a=4),
                                                           in0=pb_[:, :4 * n].rearrange("p (a b) -> p a b", a=4),
                                                           scalar=0.125, in1=bp[:, hs, :n], op0=ALU.mult, op1=ALU.add),
                    [pbn, bpn], ["tp"])
                yield
                act(lambda: nc.scalar.activation(out=pTp[:, :4 * n], in_=tp[:, :4 * n], func=AF.Exp), ["tp"], ["pTp"])
                yield
                dve(lambda: nc.vector.scalar_tensor_tensor(out=tc_[:n, :4 * n].rearrange("p (a b) -> p a b", a=4),
                                                           in0=cb_[:n, :4 * n].rearrange("p (a b) -> p a b", a=4),
                                                           scalar=0.125, in1=bc[:n, hs, :n], op0=ALU.mult, op1=ALU.add),
                    [cbn, bcn_], ["tc"])
                yield
                act(lambda: nc.scalar.activation(out=pTc[:n, :4 * n], in_=tc_[:n, :4 * n], func=AF.Exp), ["tc"], ["pTc"])
                yield
                ob_, obn = bank()
                pe(lambda: nc.tensor.matmul(ob_[:65, :4 * n], lhsT=vprev[:, kh, :], rhs=pTp[:, :4 * n],
                                            start=True, stop=False), [vpn, "pTp"], [obn], inc=False)
                pe(lambda: nc.tensor.matmul(ob_[:65, :4 * n], lhsT=vcur[:n, kh, :], rhs=pTc[:n, :4 * n],
                                            start=False, stop=True), [vcn, "pTc"], [obn])
                act(lambda: nc.scalar.copy(out=oTa[:, :4 * n], in_=ob_[:65, :4 * n]), [obn], ["oTa"])
                yield
                db_, dbn = bank()
                pe(lambda: nc.tensor.matmul(db_[:64, :4 * n], lhsT=sel65[:, :], rhs=oTa[:, :4 * n], start=True, stop=True),
                   ["oTa", "sel65"], [dbn])
                dve(lambda: nc.vector.tensor_tensor(out=rden[:, :4 * n].rearrange("p (a b) -> p a b", a=4),
                                                    in0=db_[:64, :4 * n].rearrange("p (a b) -> p a b", a=4),
                                                    in1=sm[:64, 8 + kh * 4:12 + kh * 4].unsqueeze(2).to_broadcast([64, 4, n]),
                                                    op=ALU.add), [dbn, "sm"], ["rden"])
                yield
                act(lambda: nc.scalar.activation(out=rden[:, :4 * n], in_=rden[:, :4 * n], func=AF.Ln), ["rden"], ["rden"])
                yield
                act(lambda: nc.scalar.activation(out=rden[:, :4 * n], in_=rden[:, :4 * n], func=AF.Exp, scale=-1.0),
                    ["rden"], ["rden"])
                yield
                dve(lambda: nc.vector.tensor_tensor(out=dst[:, hs, dst_c0:dst_c0 + n],
                                                    in0=oTa[0:64, :4 * n].rearrange("p (a b) -> p a b", a=4),
                                                    in1=rden[:, :4 * n].rearrange("p (a b) -> p a b", a=4), op=ALU.mult),
                    ["oTa", "rden"], [dstn])
                yield

        def gate_cols(n, o_ap, on, z_ap, zn, dst_c0):
            c = 8
            pool(lambda: nc.gpsimd.tensor_tensor(out=og2[:n], in0=o_ap, in1=o_ap, op=ALU.mult), [on], ["og2"])
            dve(lambda: nc.vector.tensor_reduce(out=st[:n, c:c + 4], in_=og2[:n], axis=AX.X, op=ALU.add),
                ["og2"], ["st"])
            act(lambda: nc.scalar.activation(out=st[:n, c + 4:c + 8], in_=st[:n, c:c + 4], func=AF.Ln, scale=1.0 / 128,
                                             bias=C.epsc[:n, 0:1]), ["st", ID], ["st"])
            act(lambda: nc.scalar.activation(out=st[:n, c + 4:c + 8], in_=st[:n, c + 4:c + 8], func=AF.Exp, scale=-0.5),
                ["st"], ["st"])
            dve(lambda: nc.vector.tensor_tensor(out=og[:n], in0=o_ap,
                                                in1=st[:n, c + 4:c + 8].unsqueeze(2).to_broadcast([n, 4, 128]),
                                                op=ALU.mult), [on, "st"], ["og"])
            pool(lambda: nc.gpsimd.tensor_tensor(out=og[:n], in0=og[:n],
                                                 in1=gdnn[:n, :].unsqueeze(1).to_broadcast([n, 4, 128]), op=ALU.mult),
                 ["og", "gdnn"], ["og"])
            act(lambda: nc.scalar.activation(out=sz[:n, :], in_=z_ap, func=AF.Silu), [zn], ["sz"])
            dve(lambda: nc.vector.tensor_tensor(out=ogb[:n, :], in0=og[:n].rearrange("p a b -> p (a b)"), in1=sz[:n, :],
                                                op=ALU.mult), ["og", "sz"], ["ogb"])
            for cc in range(4):
                pe(lambda: nc.tensor.transpose(pTb[:, cc, :n], ogb[:n, cc * 128:(cc + 1) * 128], idb[:n, :n]),
                   ["ogb", ID], ["pTb"], inc=(cc == 3))
            act(lambda: nc.scalar.copy(out=mixT[:, :, dst_c0:dst_c0 + n], in_=pTb[:, 0:4, :n]), ["pTb"], ["mixT"])

        def out_proj(n, swa_ap, swn, r1, r2):
            dma("sp", xt[:n, :], d["x1"][r1:r1 + n, :], [DB(d["x1"])], ["xt"])
            bks = [bank(), bank()]
            for nh in range(2):
                bk, bn = bks[nh]
                for cc in range(4):
                    pe(lambda: nc.tensor.matmul(bk[:n, :], lhsT=mixT[:, cc, :n], rhs=wog[:, cc, nh * 512:(nh + 1) * 512],
                                                start=(cc == 0), stop=False), ["mixT", "wog"], [bn], inc=False)
                for h in range(8):
                    pe(lambda: nc.tensor.matmul(bk[:n, :], lhsT=swa_ap[:, h, :n], rhs=wos[:, h, nh * 512:(nh + 1) * 512],
                                                start=False, stop=(h == 7)), [swn, "wos"], [bn], inc=(h == 7))
                act(lambda: nc.scalar.copy(out=mtmp[:n, nh * 512:(nh + 1) * 512], in_=bk[:n, :]), [bn], ["mtmp"])
            pool(lambda: nc.gpsimd.memset(st[:, 16:20], 0.0), [], ["st"])
            act(lambda: nc.scalar.activation(out=junk[:n, :], in_=mtmp[:n, :], func=AF.Square, accum_out=st[:n, 16:17]),
                ["mtmp", "st"], ["junk", "st"])
            rstd_ops(C, st[:n, 16:17], st[:n, 17:18], n, D, B("st"))
            dve(lambda: nc.vector.scalar_tensor_tensor(out=mtmp[:n, :], in0=mtmp[:n, :], scalar=st[:n, 17:18],
                                                       in1=gmpost[:n, :], op0=ALU.mult, op1=ALU.mult),
                ["mtmp", "st", "gmpost"], ["mtmp"])
            pool(lambda: nc.gpsimd.tensor_tensor(out=mtmp[:n, :], in0=mtmp[:n, :], in1=xt[:n, :], op=ALU.add),
                 ["mtmp", "xt"], ["mtmp"])
            dma("sp", d["x2"][r2:r2 + n, :], mtmp[:n, :], ["mtmp"], [DB(d["x2"])])

        def conv_state_out(src_view, ncols, dst, srcn="rawcur"):
            for g in range(3):
                bk, bn = bank()
                for cc in range(4):
                    pe(lambda: nc.tensor.transpose(bk[:ncols, cc * 128:(cc + 1) * 128], src_view(g * 4 + cc), idf[:, :]),
                       [srcn, ID], [bn], inc=(cc == 3))
                act(lambda: nc.scalar.copy(out=cvo[:ncols, :], in_=bk[:ncols, :]), [bn], ["cvo"])
                dma("sp", dst[:, g * 512:(g + 1) * 512], cvo[:ncols, :], ["cvo"], [DB(dst)])

        if do1:
            pool(lambda: nc.gpsimd.memset(Sf[:], 0.0), [], ["Sf"])
            if not FUS:
                dve(lambda: nc.vector.tensor_tensor(out=Sf[:, :, 128:256], in0=Sf[:, :, 128:256],
                                                    in1=idf[:, :].unsqueeze(1).to_broadcast([128, 4, 128]), op=ALU.add),
                    ["Sf", ID], ["Sf"])
            act(lambda: nc.scalar.copy(out=Sb_[:], in_=Sf[:]), ["Sf"], ["Sb"])
            pool(lambda: nc.gpsimd.memset(rawx[0][:], 0.0), [], ["rawcur"])
            pool(lambda: nc.gpsimd.memset(rawx[1][:], 0.0), [], ["rawcur"])

            NT = NMAIN // 128
            L = list(range(-(NPRE // 128) if FUS else -1, NT))

            def F1(ti):
                P = slot(ti)
                cur = ti % 2
                prv = 1 - cur
                k3 = ti % 4
                r0 = PRE0 + ti * 128
                yield from load_norm_T(r0, 128)
                pool(lambda: nc.gpsimd.tensor_copy(out=rawx[cur][:, :, 0:3], in_=rawx[prv][:, :, 128:131]),
                     ["rawcur"], ["rawcur"])

                def raw_dst(g, src, bn):
                    act(lambda: nc.scalar.copy(out=rawx[cur][:, g * 4:(g + 1) * 4, 3:131], in_=src), [bn], ["rawcur"])
                yield from proj_feat(P, 128, raw_dst, with_qs=(ti >= 0), with_q=(ti >= -1))
                if ti == NT - 1:
                    conv_state_out(lambda c: rawx[cur][:, c, 128:131], 3, d["conv_p"])
                if FUS or ti >= 0:
                    yield from conv_l2(P, 128, lambda j: rawx[cur][:, :, j:j + 128], None, full=(ti >= 0))
                yield from proj_tok(P, 0, 128, with_z=(ti >= 0))
                if ti >= 0:
                    dma("sp", d["z_s"][ti], zsb[:, :], ["zsb"], [DB(d["z_s"])])
                yield from proj_qs(P, 128, with_qs=(ti >= 0))
                if ti >= -1:
                    yield from proj_ksT(P, 128, ksT[k3], "ksT%d" % k3)
                    pool(lambda: nc.gpsimd.tensor_copy(out=vaug[k3][:, :, 0:64],
                                                       in_=P.misc[:, 128:256].rearrange("p (a b) -> p a b", a=2)),
                         [P.nm["misc"]], ["vaug%d" % k3])
                    yield
                if ti == NT - 1:
                    dma("sp", d["swak_p"], P.misc[:, 0:128], [P.nm["misc"]], [DB(d["swak_p"])])
                    dma("sp", d["swav_p"], P.misc[:, 128:256], [P.nm["misc"]], [DB(d["swav_p"])])

            def F2(ti):
                P = slot(ti)
                if FUS or ti >= 0:
                    yield from gdn_pre(P, 128, 0, 7, full=(ti >= 0))

            def Bk(ti):
                P = slot(ti)
                if not (FUS or ti >= 0):
                    return
                yield from gdn_scan(P, 128, DVW, state_only=(ti < 0))
                if ti < 0:
                    return
                dma("sp", d["oloc_s"][ti], oloc[:].rearrange("p a b -> p (a b)"), ["oloc"], [DB(d["oloc_s"])])
                if not FUS:
                    dma("sp", d["opt_s"][ti], oPT[:].rearrange("p a b -> p (a b)"), ["oPT"], [DB(d["opt_s"])])
                kc, kp = ti % 4, (ti - 1) % 4
                yield from swa(P, 128, 0, ksT[kp], "ksT%d" % kp, vaug[kp], "vaug%d" % kp, ksT[kc], "ksT%d" % kc,
                               vaug[kc], "vaug%d" % kc, bprev, "bprev", bcur, "bcur", swaT, "swaT", 0)
                dma("sp", d["swat_s"][ti], swaT[:].rearrange("p a b -> p (a b)"), ["swaT"], [DB(d["swat_s"])])
                if ti == 0:
                    dma("sp", bprev[:], d["bprev"].rearrange("p (h q) -> p h q", h=8), [], ["bprev"])

            fill_on[0] = True
            drain(F1(L[0]))
            interleave(F2(L[0]), F1(L[1]), pools=MIXP[0:3:2])
            for idx in range(len(L)):
                interleave(F2(L[idx + 1]) if idx + 1 < len(L) else None, Bk(L[idx]),
                           F1(L[idx + 2]) if idx + 2 < len(L) else None, weights=MIXW, pools=MIXP)

            fill_on[0] = False
            if FUS:
                dma("sp", d["gdn_p"].rearrange("h k v -> k h v"), Sf[:, :, 0:128], ["Sf"], [DB(d["gdn_p"])])
            else:
                dma("sp", d["cc_send"], Sf[:].rearrange("p a b -> p (a b)"), ["Sf"], [DB(d["cc_send"])])
            if part == "AB":
                S._deps("pool", [DB(d["cc_send"])], [DB(d["cc_recv"])])
                ccs = nc.alloc_semaphore("ccsem")
                cc = nc.gpsimd.collective_compute("AllGather", ALU.bypass, replica_groups=[list(range(NCORES))],
                                                  ins=[d["cc_send"]], outs=[d["cc_recv"]])
                cc.then_inc(ccs, 16)
                S.dsem.append(ccs); S.dval.append(16)
                DB(d["cc_recv"]).w = ("d", len(S.dsem) - 1, 16)

            _ck("main")
            P = PS[0]
            r0 = PRE0 + NMAIN
            drain(load_norm_T(r0, NSAMP))
            for g in range(3):
                dma("sp", cvo[:, :], d["state_conv"][:, g * 512:(g + 1) * 512], [], ["cvo"])
                bk, bn = bank()
                for cc_ in range(4):
                    pe(lambda: nc.tensor.transpose(bk[:, cc_ * 128:cc_ * 128 + 48], cvo[:48, cc_ * 128:(cc_ + 1) * 128], idf[:48, :48]),
                       ["cvo", ID], [bn], inc=(cc_ == 3))
                act(lambda: nc.scalar.copy(out=rawxs[:, g * 4:(g + 1) * 4, :, 0:3],
                                           in_=v3(bk, 128, 4)[:, :, 0:48].rearrange("p a (s r) -> p a s r", r=3)),
                    [bn], ["rawcur"])

            def raw_dst_s(g, src, bn):
                act(lambda: nc.scalar.copy(out=rawxs[:, g * 4:(g + 1) * 4, :, 3:7],
                                           in_=src.rearrange("p a (s t) -> p a s t", t=4)), [bn], ["rawcur"])
            drain(proj_feat(P, NSAMP, raw_dst_s))
            drain(proj_qs(P, NSAMP))
            ksTs = ksT[0]
            drain(proj_tok(P, 0, NSAMP, with_z=False))
            drain(proj_ksT(P, NSAMP, ksTs, "ksT0"))
            drain(conv_l2(P, NSAMP, lambda j: rawxs[:, :, :, j:j + 4], None))
            pool(lambda: nc.gpsimd.tensor_copy(out=ctmp[:, :, 0:48].rearrange("p c (s r) -> p c s r", r=3),
                                               in_=rawxs[:, :, :, 4:7]), ["rawcur"], ["ctmp"])
            conv_state_out(lambda c: ctmp[:, c, 0:48], 48, d["conv_s"], srcn="ctmp")
            _ck("sconv")
            zs2 = [(zsb, "zsb"), (PS[1].cacc[:, 0:4, :].rearrange("p a b -> p (a b)"), PS[1].nm["cacc"])]
            _sq = {}

            def sslot(sq_):
                if sq_ % 2 not in _sq:
                    Q = Slot()
                    Q.__dict__.update(PS[0].__dict__)
                    Q.nm = dict(PS[0].nm)
                    for nm_ in ("misc", "qdecT", "qkT", "kdec", "wT", "uaug", "glb"):
                        setattr(Q, nm_, getattr(PS[sq_ % 2], nm_))
                        Q.nm[nm_] = PS[sq_ % 2].nm[nm_]
                    _sq[sq_ % 2] = Q
                return _sq[sq_ % 2]

            def s_front(sq_):
                Q = sslot(sq_)
                zd, zn = zs2[sq_ % 2]
                yield from proj_tok(Q, sq_ * 4, 4, zdst=zd, zname=zn)
                yield from gdn_pre(Q, 4, sq_ * 4, 2)

            def s_back(sq_):
                Q = sslot(sq_)
                zd, zn = zs2[sq_ % 2]
                c0 = sq_ * 4
                dma("sp", Sf[:, :, 0:128], d["state_gdn"][sq_].rearrange("h k v -> k h v"), [], ["Sf"])
                act(lambda: nc.scalar.copy(out=Sb_[:, :, 0:128], in_=Sf[:, :, 0:128]), ["Sf"], ["Sb"])
                yield
                yield from gdn_scan(Q, 4, 128)
                dma("sp", d["gdn_s"][sq_].rearrange("h k v -> k h v"), Sf[:, :, 0:128], ["Sf"], [DB(d["gdn_s"])])
                gate_cols(4, oloc[:4], "oloc", zd[:4, :], zn, c0)
                yield
                dma("sp", kcf[:, :], d["cache_k"][sq_], [], ["kcf"])
                dma("sp", vcf[:, :], d["cache_v"][sq_], [], ["vcf"])
                bk, bn = bank()
                for kh in range(2):
                    pe(lambda: nc.tensor.transpose(bk[:64, kh * 128:(kh + 1) * 128], kcf[:, kh * 64:(kh + 1) * 64], idf[:, :]),
                       ["kcf", ID], [bn], inc=(kh == 1))
                act(lambda: nc.scalar.copy(out=ksT[1][:, :, :], in_=v3(bk, 64, 4)[:, 0:2, :]), [bn], ["ksT1"])
                yield
                pool(lambda: nc.gpsimd.tensor_copy(out=vaug[1][:, :, 0:64], in_=vcf[:, :].rearrange("p (a b) -> p a b", a=2)),
                     ["vcf"], ["vaug1"])
                pool(lambda: nc.gpsimd.tensor_copy(out=vaug[0][:4, :, 0:64],
                                                   in_=Q.misc[:4, 128:256].rearrange("p (a b) -> p a b", a=2)),
                     [Q.nm["misc"]], ["vaug0"])
                yield
                yield from swa(Q, 4, c0, ksT[1], "ksT1", vaug[1], "vaug1", ksT[0], "ksT0", vaug[0], "vaug0",
                               bsc, "bsc", bsn, "bsn", swaTs, "swaTs", c0)
                dma("sp", d["swak_s"][sq_, 0:124, :], d["cache_k"][sq_, 4:128, :], [], [DB(d["swak_s"])])
                dma("sp", d["swav_s"][sq_, 0:124, :], d["cache_v"][sq_, 4:128, :], [], [DB(d["swav_s"])])
                dma("sp", d["swak_s"][sq_, 124:128, :], Q.misc[:4, 0:128], [Q.nm["misc"]], [DB(d["swak_s"])])
                dma("sp", d["swav_s"][sq_, 124:128, :], Q.misc[:4, 128:256], [Q.nm["misc"]], [DB(d["swav_s"])])

            drain(s_front(0))
            for sq_ in range(16):
                interleave(s_back(sq_), s_front(sq_ + 1) if sq_ + 1 < 16 else None,
                           pools=((3, 4, 5, 6), (0, 1, 2)))
        S.barrier()
        esA.close()
        wog = sb("wog", [128, 4, D], BF16)
        dma("pool", wog[:], d["wout_g"].rearrange("p (c n) -> p c n", c=4), [], ["wog"])
        wos = sb("wos", [64, 8, D], BF16)
        dma("pool", wos[:], d["wout_s"].rearrange("p (c n) -> p c n", c=8), [], ["wos"])
        Pr = sb("Pr", [128, 4, 256]); PmT = sb("PmT", [128, 4, 128]); Sin = sb("Sin", [128, 4, 128])
        Sinb = sb("Sinb", [128, 4, 128], BF16); cand = sb("cand", [128, 4, 128])
        if do1:
            out_proj(NSAMP, swaTs, "swaTs", PRE0 + NMAIN, NMAIN)
        if not do2:
            return
        if not do1:
            dma("sp", d["x2"][NMAIN:NMAIN + NSAMP, :], d["x2in"][NMAIN:NMAIN + NSAMP, :], [], [DB(d["x2"])])

        if not FUS:
            pool(lambda: nc.gpsimd.memset(Sin[:], 0.0), [], ["Sin"])

        def apply_P(Pbuf, Pn):
            bk, bn = bank()
            for h in range(4):
                pe(lambda: nc.tensor.transpose(bk[:, h * 128:(h + 1) * 128], Pbuf[:, h, 128:256], idf[:, :]),
                   [Pn, ID], [bn], inc=(h == 3))
            act(lambda: nc.scalar.copy(out=PmT[:], in_=v3(bk, 128, 4)), [bn], ["PmT"])
            bk2, bn2 = bank()
            for h in range(4):
                pe(lambda: nc.tensor.matmul(bk2[:, h * 128:(h + 1) * 128], lhsT=PmT[:, h, :], rhs=Sin[:, h, :],
                                            start=True, stop=True), ["PmT", "Sin"], [bn2], inc=(h == 3))
            dve(lambda: nc.vector.tensor_tensor(out=cand[:], in0=v3(bk2, 128, 4), in1=Pbuf[:, :, 0:128], op=ALU.add),
                [bn2, Pn], ["cand"])

        for r in range(0 if FUS else NCORES):
            dma("sp", Pr[:].rearrange("p a b -> p (a b)"), d["cc_recv"][r * 128:(r + 1) * 128, :],
                [DB(d["cc_recv"])], ["Pr"])
            apply_P(Pr, "Pr")
            dve(lambda: nc.vector.tensor_tensor(out=cand[:], in0=cand[:], in1=Sin[:], op=ALU.subtract),
                ["cand", "Sin"], ["cand"])
            for h in range(4):
                dve(lambda: nc.vector.scalar_tensor_tensor(out=Sin[:, h, :], in0=cand[:, h, :], scalar=rmask[:, r:r + 1],
                                                           in1=Sin[:, h, :], op0=ALU.mult, op1=ALU.add),
                    ["cand", "Sin", "sm"], ["Sin"])
        if not FUS:
            act(lambda: nc.scalar.copy(out=Sinb[:], in_=Sin[:]), ["Sin"], ["Sinb"])
            dma("sp", Pr[:].rearrange("p a b -> p (a b)"), d["cc_send"], [DB(d["cc_send"])], ["Pr"])
            apply_P(Pr, "Pr")
            dma("sp", d["gdn_p"].rearrange("h k v -> k h v"), cand[:], ["cand"], [DB(d["gdn_p"])])

        for ti in range(NT):
            dma("sp", oloc[:].rearrange("p a b -> p (a b)"), d["oloc_s"][ti], [DB(d["oloc_s"])], ["oloc"])
            dma("sp", zsb[:, :], d["z_s"][ti], [DB(d["z_s"])], ["zsb"])
            dma("sp", swaT[:].rearrange("p a b -> p (a b)"), d["swat_s"][ti], [DB(d["swat_s"])], ["swaT"])
            if not FUS:
                dma("sp", oPT[:].rearrange("p a b -> p (a b)"), d["opt_s"][ti], [DB(d["opt_s"])], ["oPT"])
                bk, bn = bank()
                for h in range(4):
                    pe(lambda: nc.tensor.matmul(bk[:, h * 128:(h + 1) * 128], lhsT=oPT[:, h, :], rhs=Sinb[:, h, :],
                                                start=True, stop=True), ["oPT", "Sinb"], [bn], inc=(h == 3))
                dve(lambda: nc.vector.tensor_tensor(out=oloc[:], in0=oloc[:], in1=v3(bk, 128, 4), op=ALU.add),
                    ["oloc", bn], ["oloc"])
            gate_cols(128, oloc[:], "oloc", zsb[:, :], "zsb", 0)
            out_proj(128, swaT, "swaT", PRE0 + ti * 128, ti * 128)


def build_program(dbg=False, part="AB"):
    nc = bass.Bass("TRN2", target_bir_lowering=False)
    C = Ctx()
    C.nc = nc
    C.S = Sched(nc)
    S = C.S
    C.B_dram = {}
    A, Bp = part in ("A", "AB", "F"), part in ("B", "AB", "F")
    FUS = part == "F"
    NT1 = NTOKF if FUS else NTOK1

    def dt_(nm, shape, dt, kind):
        t = nc.dram_tensor(nm, list(shape), dt, kind=kind).ap()
        C.B_dram[nm] = Buf(nm)
        return t

    def din(nm, shape, dt=F32):
        return dt_(nm, shape, dt, "ExternalInput")

    def dout(nm, shape, cond=True):
        return dt_(nm, shape, F32, "ExternalOutput" if cond else "Internal")

    def dlink(nm, shape, dt=F32):
        kind = "Internal" if part in ("AB", "F") else ("ExternalOutput" if part == "A" else "ExternalInput")
        if dbg and part == "AB" and dt == F32:
            kind = "ExternalOutput"
        return dt_(nm, shape, dt, kind)

    xin = din("xin", [NT1, D])
    pin = din("pin", [NTOK2, PLE])
    wg1 = din("wg1", [NG, 128, 8 * GW]); wu1 = din("wu1", [NG, 128, 8 * GW]); wd1 = din("wd1", [128, NJ * D])
    wg2 = din("wg2", [NG, 128, 8 * GW]); wu2 = din("wu2", [NG, 128, 8 * GW]); wd2 = din("wd2", [128, NJ * D])
    gains = {k: din(k, [1, D]) for k in ("g1pre", "g1post", "gmpre", "gmpost", "g2pre", "g2post", "gple")}
    wpg = din("wpg", [128, 8 * D])
    wpp = din("wpp", [128, 2 * D])
    d = dict(gmpre=gains["gmpre"][0:1, :], gmpost=gains["gmpost"][0:1, :])
    d["win"] = din("win", [128, 8 * NPROJ])
    d["convw"] = din("convw", [128, 48])
    d["alog"] = din("alog", [1, 4])[0:1, :]; d["dtb"] = din("dtb", [1, 4])[0:1, :]
    d["gdnn"] = din("gdnn", [1, 128])[0:1, :]; d["sinks"] = din("sinks", [1, 8])[0:1, :]
    d["rmask"] = din("rmask", [1, 8])[0:1, :]
    d["wout_g"] = din("wout_g", [128, 4 * D]); d["wout_s"] = din("wout_s", [64, 8 * D])
    for k in ("bprev", "bcur", "bprev1"):
        d[k] = din(k, [128, 8 * 128])
    d["bs_cache"] = din("bs_cache", [128, 32]); d["bs_new"] = din("bs_new", [4, 32])
    d["cmask"] = din("cmask", [128, 4 * 128]); d["sel65"] = din("sel65", [65, 64])
    d["state_conv"] = din("state_conv", [48, 1536]); d["state_gdn"] = din("state_gdn", [16, 4, 128, 128])
    d["cache_k"] = din("cache_k", [16, 128, 128]); d["cache_v"] = din("cache_v", [16, 128, 128])
    y_out = dout("y", [NTOK2, D], Bp)
    d["gdn_p"] = dout("gdn_p", [4, 128, 128], Bp)
    d["conv_p"] = dout("conv_p", [3, 1536], A)
    d["swak_p"] = dout("swak_p", [128, 128], A); d["swav_p"] = dout("swav_p", [128, 128], A)
    d["conv_s"] = dout("conv_s", [48, 1536], A); d["gdn_s"] = dout("gdn_s", [16, 4, 128, 128], A)
    d["swak_s"] = dout("swak_s", [16, 128, 128], A); d["swav_s"] = dout("swav_s", [16, 128, 128], A)
    x1 = dlink("x1", [NT1, D])
    if part == "B":
        d["x2in"] = din("x2in", [NTOK2, D])
        x2 = dt_("x2", [NTOK2, D], F32, "Internal")
    elif part == "A":
        x2 = dt_("x2in", [NTOK2, D], F32, "ExternalOutput")
    else:
        x2 = dt_("x2", [NTOK2, D], F32, "ExternalOutput" if dbg else "Internal")
    d["x1"] = x1; d["x2"] = x2
    NT = NMAIN // 128
    d["oloc_s"] = dlink("oloc_s", [NT, 128, 512]); d["opt_s"] = dlink("opt_s", [NT, 128, 512], BF16)
    d["z_s"] = dlink("z_s", [NT, 128, 512]); d["swat_s"] = dlink("swat_s", [NT, 64, 1024], BF16)
    d["cc_send"] = dlink("cc_send", [128, 1024])
    d["cc_recv"] = dt_("cc_recv", [NCORES * 128, 1024], F32, "ExternalInput" if part == "B" else "Internal")

    C.idf = nc.alloc_sbuf_tensor("idf", [128, 128], F32)
    C.idb = nc.alloc_sbuf_tensor("idb", [128, 128], BF16)
    C.B_id = Buf("id")
    C.epsc = nc.alloc_sbuf_tensor("epsc", [128, 2], F32)
    S.op("pool", lambda: nc.gpsimd.memset(C.epsc[:, 0:1], EPS), writes=[C.B_id])
    S.op("pool", lambda: nc.gpsimd.memset(C.epsc[:, 1:2], 1.0), writes=[C.B_id])
    C.eps_ap = lambda n: C.epsc[:n, 0:1]
    S.op("pool", lambda: nc.gpsimd.memset(C.idf[:], 0.0), writes=[C.B_id])
    S.op("pool", lambda: nc.gpsimd.affine_select(out=C.idf[:], in_=C.idf[:], pattern=[[-1, 128]],
                                                 compare_op=ALU.not_equal, fill=1.0, base=0, channel_multiplier=1),
         reads=[C.B_id], writes=[C.B_id])
    S.op("dve", lambda: nc.vector.tensor_copy(out=C.idb[:], in_=C.idf[:]), reads=[C.B_id], writes=[C.B_id])

    outs = []
    if A:
        npm = (NPRE if FUS else NHALO) + NMAIN
        t1 = [(r0, r0, n) for (r0, n) in tiles_of(npm)] + [(npm, npm, NSAMP)]
        ffn_phase(C, "f1", xin, x1, t1, wg1, wu1, wd1, gains["g1pre"][0:1, :], gains["g1post"][0:1, :])
        outs += ["conv_p", "swak_p", "swav_p", "conv_s", "gdn_s", "swak_s", "swav_s"]
        if part == "A":
            outs += ["x1", "x2in", "oloc_s", "opt_s", "z_s", "swat_s", "cc_send"]
    mix_phase(C, d, part)
    if Bp:
        t2 = [(r0, r0, n) for (r0, n) in tiles_of(NMAIN)] + [(NMAIN, NMAIN, NSAMP)]
        ffn_phase(C, "f2", x2, y_out, t2, wg2, wu2, wd2, gains["g2pre"][0:1, :], gains["g2post"][0:1, :],
                  ple=dict(gain=gains["gple"][0:1, :], wg=wpg, wp=wpp, p=pin, prow=lambda d0: d0))
        outs += ["y", "gdn_p"]
    S.finish([C.B_dram[k] for k in outs])
    return nc, C


def _lay_gu(w):
    return np.ascontiguousarray(w.reshape(8, 128, NG, GW).transpose(2, 1, 0, 3).reshape(NG, 128, 8 * GW))


def _lay_rows(w, nk):
    n = w.shape[1]
    return np.ascontiguousarray(w.reshape(nk, 128, n).transpose(1, 0, 2).reshape(128, nk * n))


def _bucket_table():
    dd = np.arange(128)
    lr = np.log(np.maximum(dd, 1).astype(np.float32) / np.float32(16)) / np.float32(np.log(128 / 16))
    large = np.minimum(16 + (lr.astype(np.float32) * np.float32(16)).astype(np.int32), 31)
    return np.where(dd < 16, dd, large)


def _bias_tables(rel_bias):
    bt = _bucket_table()
    bv = rel_bias[bt, :]
    k = np.arange(128)[:, None]; q = np.arange(128)[None, :]
    dcur = q - k
    dprev = 128 + q - k
    bcur = np.full((128, 8, 128), NEG, np.float32); bprev = np.full((128, 8, 128), NEG, np.float32)
    for h in range(8):
        t = bv[np.clip(dcur, 0, 127), h]
        bcur[:, h, :] = np.where(dcur >= 0, t, NEG)
        t = bv[np.clip(dprev, 0, 127), h]
        bprev[:, h, :] = np.where(dprev < 128, t, NEG)
    j = np.arange(128)[:, None]; t4 = np.arange(4)[None, :]
    dc = 128 + t4 - j
    bsc = np.full((128, 8, 4), NEG, np.float32)
    kk = np.arange(4)[:, None]
    dn = t4 - kk
    bsn = np.full((4, 8, 4), NEG, np.float32)
    for h in range(8):
        bsc[:, h, :] = np.where(dc < 128, bv[np.clip(dc, 0, 127), h], NEG)
        bsn[:, h, :] = np.where(dn >= 0, bv[np.clip(dn, 0, 127), h], NEG)
    return bprev.reshape(128, -1), bcur.reshape(128, -1), bsc.reshape(128, -1), bsn.reshape(4, -1)


def make_in_maps(inp, fused=False):
    f = lambda a: np.ascontiguousarray(np.asarray(a, dtype=np.float32))
    w_in = f(inp["w_in"][0])
    perm = np.concatenate([np.arange(0, 1536), np.arange(1536, 2048), np.arange(2056, 2568), np.arange(2568, 2696),
                           np.arange(2696, 2824), np.arange(2048, 2052), np.arange(2052, 2056)])
    w_out = f(inp["w_out"][0])
    conv_w = f(inp["conv_w"][0])
    bprev, bcur, bsc, bsn = _bias_tables(f(inp["rel_bias"]))
    ii = np.arange(128)
    tri = (ii[:, None] <= ii[None, :]).astype(np.float32)
    cmask = np.stack([tri, tri.T, (ii[:, None] > ii[None, :]).astype(np.float32), np.ones((128, 128), np.float32)], 1)
    sel65 = np.zeros((65, 64), np.float32); sel65[64, :] = 1.0
    shared = {
        "wg1": _lay_gu(f(inp["ffn1_w_gate"][0])), "wu1": _lay_gu(f(inp["ffn1_w_up"][0])),
        "wd1": _lay_rows(f(inp["ffn1_w_down"][0]), NJ),
        "wg2": _lay_gu(f(inp["ffn2_w_gate"][0])), "wu2": _lay_gu(f(inp["ffn2_w_up"][0])),
        "wd2": _lay_rows(f(inp["ffn2_w_down"][0]), NJ),
        "g1pre": f(inp["norm_ffn1_pre"]), "g1post": f(inp["norm_ffn1_post"]),
        "gmpre": f(inp["norm_mix_pre"]), "gmpost": f(inp["norm_mix_post"]),
        "g2pre": f(inp["norm_ffn2_pre"]), "g2post": f(inp["norm_ffn2_post"]),
        "gple": f(inp["norm_ple_post"]),
        "wpg": _lay_rows(f(inp["ple_gate"][0]), 8), "wpp": _lay_rows(f(inp["ple_proj"][0]), 2),
        "win": _lay_rows(np.ascontiguousarray(w_in[:, perm]), 8),
        "convw": np.ascontiguousarray(conv_w.T.reshape(12, 128, 4).transpose(1, 0, 2).reshape(128, 48)),
        "alog": f(inp["gdn_a_log"]), "dtb": f(inp["gdn_dt_bias"]), "gdnn": f(inp["gdn_norm"]),
        "sinks": f(inp["swa_sinks"]),
        "wout_g": _lay_rows(w_out[:512], 4),
        "wout_s": np.ascontiguousarray(w_out[512:].reshape(8, 64, D).transpose(1, 0, 2).reshape(64, 8 * D)),
        "bprev": bprev, "bcur": bcur, "bs_cache": bsc, "bs_new": bsn,
        "cmask": np.ascontiguousarray(cmask.reshape(128, 512)), "sel65": sel65,
    }
    xp = f(inp["x_prompt"]); xs = f(inp["x_sample"]).reshape(-1, D)
    pp = f(inp["p_prompt"][0]); psm = f(inp["p_sample"][0]).reshape(-1, PLE)
    sconv = f(inp["state_conv"][0]); sgdn = f(inp["state_gdn"][0])
    ck = f(inp["cache_swa_k"][0]).reshape(128, 128, 128); cv = f(inp["cache_swa_v"][0]).reshape(128, 128, 128)
    maps = []
    for c in range(NCORES):
        b, q = c // 4, c % 4
        t0 = q * NMAIN
        if fused:
            halo = np.zeros((NPRE, D), np.float32)
            if t0 > 0:
                halo[NPRE - t0:] = xp[b, 0:t0]
        else:
            halo = xp[b, t0 - NHALO:t0] if q > 0 else np.zeros((NHALO, D), np.float32)
        m = dict(shared)
        m["xin"] = np.ascontiguousarray(np.concatenate([halo, xp[b, t0:t0 + NMAIN], xs[c * NSAMP:(c + 1) * NSAMP]], 0))
        m["pin"] = np.ascontiguousarray(np.concatenate([pp[b, t0:t0 + NMAIN], psm[c * NSAMP:(c + 1) * NSAMP]], 0))
        m["bprev1"] = bprev if q > 0 else np.full_like(bprev, NEG)
        rm = np.zeros((1, 8), np.float32)
        for r in range(NCORES):
            if r // 4 == b and r % 4 < q:
                rm[0, r] = 1.0
        m["rmask"] = rm
        m["state_conv"] = np.ascontiguousarray(sconv[c * 16:(c + 1) * 16].reshape(48, 1536))
        m["state_gdn"] = np.ascontiguousarray(sgdn[c * 16:(c + 1) * 16])
        m["cache_k"] = np.ascontiguousarray(ck[c * 16:(c + 1) * 16]); m["cache_v"] = np.ascontiguousarray(cv[c * 16:(c + 1) * 16])
        maps.append(m)
    return maps


def assemble(R):
    yp = np.zeros((2, 8192, D), np.float32); ys = np.zeros((128, 4, D), np.float32)
    conv_p = np.zeros((1, 2, 3, 1536), np.float32); gdn_p = np.zeros((1, 2, 4, 128, 128), np.float32)
    kp = np.zeros((1, 2, 128, 2, 64), np.float32); vp = np.zeros((1, 2, 128, 2, 64), np.float32)
    conv_s = np.zeros((1, 128, 3, 1536), np.float32); gdn_s = np.zeros((1, 128, 4, 128, 128), np.float32)
    ks = np.zeros((1, 128, 128, 2, 64), np.float32); vs = np.zeros((1, 128, 128, 2, 64), np.float32)
    for c in range(NCORES):
        b, q = c // 4, c % 4
        r = R[c]
        yp[b, q * NMAIN:(q + 1) * NMAIN] = r["y"][:NMAIN]
        ys[c * 16:(c + 1) * 16] = r["y"][NMAIN:].reshape(16, 4, D)
        if q == 3:
            conv_p[0, b] = r["conv_p"]; gdn_p[0, b] = r["gdn_p"]
            kp[0, b] = r["swak_p"].reshape(128, 2, 64); vp[0, b] = r["swav_p"].reshape(128, 2, 64)
        conv_s[0, c * 16:(c + 1) * 16] = r["conv_s"].reshape(16, 3, 1536)
        gdn_s[0, c * 16:(c + 1) * 16] = r["gdn_s"]
        ks[0, c * 16:(c + 1) * 16] = r["swak_s"].reshape(16, 128, 2, 64)
        vs[0, c * 16:(c + 1) * 16] = r["swav_s"].reshape(16, 128, 2, 64)
    return (yp, ys, conv_p, gdn_p, kp, vp, conv_s, gdn_s, ks, vs)


_CACHE = {}
A_KEYS = ("x1", "x2in", "oloc_s", "opt_s", "z_s", "swat_s", "cc_send")


def kernel(**inputs):
    if "F" not in _CACHE:
        _CACHE["F"] = build_program(False, "F")[0]
    maps = make_in_maps(inputs, fused=True)
    res = run_bass_kernel_spmd(_CACHE["F"], maps, core_ids=list(range(NCORES)))
    return assemble(res.results)
```

```python
import contextlib
import numpy as np
import concourse.bass as bass
import concourse.mybir as mybir
from concourse.bass_utils import run_bass_kernel_spmd

F32 = mybir.dt.float32
BF16 = mybir.dt.bfloat16
AF = mybir.ActivationFunctionType
ALU = mybir.AluOpType
AX = mybir.AxisListType

NCORES = 8
D = 1024
FF = 2816
NJ = FF // 128
PLE = 256
EPS = 1e-6
NMAIN = 2048
NHALO = 128
NSAMP = 64
NTOK1 = NHALO + NMAIN + NSAMP
NPRE = 6144
NTOKF = NPRE + NMAIN + NSAMP
NTOK2 = NMAIN + NSAMP
GW = 256
NG = FF // GW


class Buf:
    __slots__ = ("name", "w", "rs")

    def __init__(self, name):
        self.name = name
        self.w = None
        self.rs = []


class Sched:
    def __init__(self, nc, n_dma_sems=24, same_engine_sync=True):
        self.nc = nc
        self.eng = {"pe": nc.tensor, "act": nc.scalar, "dve": nc.vector,
                    "pool": nc.gpsimd, "sp": nc.sync}
        self.sem = {k: nc.alloc_semaphore("cs_" + k) for k in ("pe", "act", "dve", "pool")}
        self.cnt = {k: 0 for k in self.sem}
        self.seen = {}
        self.dsem = [nc.alloc_semaphore("ds%d" % i) for i in range(n_dma_sems)]
        self.dval = [0] * n_dma_sems
        self.dpool = {"sp": list(range(0, n_dma_sems - 8)), "pool": list(range(n_dma_sems - 8, n_dma_sems))}
        self.dnext = {"sp": 0, "pool": 0}
        self.ses = same_engine_sync
        self.pe_pending = []
        self.n_inst = 0

    def _wait(self, on, ev):
        if ev is None:
            return
        if ev[0] == "c":
            _, e, v = ev
            if e == on and (not self.ses or e == "pe"):
                return
            key = (on, e)
            if self.seen.get(key, 0) >= v:
                return
            self.seen[key] = v
            self.eng[on].wait_ge(self.sem[e], v)
        else:
            _, i, v = ev
            key = (on, "d", i)
            if self.seen.get(key, 0) >= v:
                return
            self.seen[key] = v
            self.eng[on].wait_ge(self.dsem[i], v)

    def _deps(self, on, reads, writes):
        for b in reads:
            self._wait(on, b.w)
        for b in writes:
            self._wait(on, b.w)
            for r in b.rs:
                self._wait(on, r)

    @staticmethod
    def _compact(rs):
        best = {}
        for ev in rs:
            k = ev[:2]
            if k not in best or best[k][2] < ev[2]:
                best[k] = ev
        return list(best.values())

    def _record(self, ev, reads, writes):
        for b in reads:
            b.rs.append(ev)
            if len(b.rs) > 12:
                b.rs = self._compact(b.rs)
        for b in writes:
            b.w = ev
            b.rs = []

    def op(self, on, fn, reads=(), writes=(), inc=True):
        self._deps(on, reads, writes)
        ins = fn()
        self.n_inst += 1
        if on == "pe" and not inc:
            self.pe_pending.append((tuple(reads), tuple(writes)))
            return ins
        self.cnt[on] += 1
        ins.then_inc(self.sem[on], 1)
        ev = ("c", on, self.cnt[on])
        groups = [(tuple(reads), tuple(writes))]
        if on == "pe":
            groups += self.pe_pending
            self.pe_pending = []
        for rd, wr in groups:
            self._record(ev, rd, wr)
        return ins

    def dma(self, on, out, in_, reads=(), writes=(), **kw):
        pl = self.dpool[on]
        i = pl[self.dnext[on] % len(pl)]
        self.dnext[on] += 1
        if self.dval[i] > 0:
            self._wait(on, ("d", i, self.dval[i]))
        self._deps(on, reads, writes)
        ins = self.eng[on].dma_start(out=out, in_=in_, **kw)
        self.n_inst += 1
        self.dval[i] += 16
        ins.then_inc(self.dsem[i], 16)
        self._record(("d", i, self.dval[i]), reads, writes)
        return ins

    def barrier(self):
        for on in ("pe", "act", "dve", "pool", "sp"):
            for e in ("pe", "act", "dve", "pool"):
                if e != on and self.cnt[e] > 0:
                    self._wait(on, ("c", e, self.cnt[e]))
            for i, v in enumerate(self.dval):
                if v > 0:
                    self._wait(on, ("d", i, v))

    def finish(self, bufs):
        for i, v in enumerate(self.dval):
            if v > 0:
                self._wait("sp", ("d", i, v))
        for b in bufs:
            self._wait("sp", b.w)


class Ctx:
    pass


def tiles_of(n_rows):
    out = []
    r = 0
    while r < n_rows:
        n = min(128, n_rows - r)
        out.append((r, n))
        r += n
    return out


def blocks_of(lo, hi, maxw=512):
    out = []
    while lo < hi:
        n = min(maxw, hi - lo)
        out.append((lo, n))
        lo += n
    return out


def rstd_ops(C, ssq_ap, out_ap, n, dim, B_stat):
    S, nc = C.S, C.nc
    S.op("act", lambda: nc.scalar.activation(out=out_ap, in_=ssq_ap, func=AF.Ln, scale=1.0 / dim, bias=C.eps_ap(n)),
         reads=[B_stat, C.B_id], writes=[B_stat])
    S.op("act", lambda: nc.scalar.activation(out=out_ap, in_=out_ap, func=AF.Exp, scale=-0.5),
         reads=[B_stat], writes=[B_stat])


def ffn_phase(C, name, x_src, x_dst, tiles, wg_d, wu_d, wd_d, gpre_d, gpost_d, ple=None):
    S, nc = C.S, C.nc
    S.barrier()
    with contextlib.ExitStack() as es:
        def sb(nm, shape, dt):
            return es.enter_context(nc.sbuf_tensor(name + "_" + nm, shape, dt))

        def ps(nm, shape, dt):
            return es.enter_context(nc.psum_tensor(name + "_" + nm, shape, dt))

        ntl = len(tiles)
        if ntl <= 18:
            halves = [tiles[: (ntl + 1) // 2], tiles[(ntl + 1) // 2:]]
        else:
            halves = [tiles[i:i + FFN_PART] for i in range(0, ntl, FFN_PART)]
            if len(halves[-1]) < 4:
                halves[-2] = halves[-2] + halves[-1]
                halves.pop()
        maxtok = max(sum(t[2] for t in h) for h in halves)
        hT = sb("hT", [128, 8, maxtok], BF16)
        aT = sb("aT", [128, NJ, maxtok], BF16)
        wd = sb("wd", [128, NJ, D], BF16)
        wg = [sb("wg%d" % i, [128, 8, GW], BF16) for i in range(2)]
        wu = [sb("wu%d" % i, [128, 8, GW], BF16) for i in range(2)]
        xin = [sb("xin%d" % i, [128, D], F32) for i in range(2)]
        hb = [sb("hb%d" % i, [128, D], BF16) for i in range(2)]
        yo = [sb("yo%d" % i, [128, D], F32) for i in range(2)]
        tmp = [sb("tmp%d" % i, [128, D], F32) for i in range(2)]
        junk = sb("junk", [128, D], BF16)
        gpre = sb("gpre", [128, D], F32)
        gpost = sb("gpost", [128, D], F32)
        sg = [sb("sg%d" % i, [128, 512], F32) for i in range(2)]
        stat = sb("stat", [128, 64], F32)
        pT = [ps("pT%d" % i, [128, 8, 128], BF16) for i in range(2)]
        pg = [ps("pg%d" % i, [128, 512], F32) for i in range(2)]
        pu = [ps("pu%d" % i, [128, 512], F32) for i in range(2)]
        po = ps("po", [128, D], F32)

        B = {}
        for k in ["hT", "aT", "wd", "junk", "gpre", "gpost", "stat", "po"]:
            B[k] = Buf(name + k)
        for k in ["wg", "wu", "xin", "hb", "yo", "tmp", "sg", "pT", "pg", "pu"]:
            for i in range(2):
                B[k, i] = Buf(name + k + str(i))
        if ple is not None:
            gple = sb("gple", [128, D], F32)
            wpg = sb("wpg", [128, 8, D], BF16)
            wpp = sb("wpp", [128, 2, D], BF16)
            pin = [sb("pin%d" % i, [128, PLE], F32) for i in range(2)]
            pb = [sb("pb%d" % i, [128, PLE], BF16) for i in range(2)]
            xT3 = sb("xT3", [128, 8, 128], BF16)
            pT3 = sb("pT3", [128, 2, 128], BF16)
            prod = sb("prod", [128, D], F32)
            for k in ["gple", "wpg", "wpp", "xT3", "pT3", "prod"]:
                B[k] = Buf(name + k)
            for i in range(2):
                B["pin", i] = Buf(name + "pin%d" % i)
                B["pb", i] = Buf(name + "pb%d" % i)

        S.dma("sp", gpre[:], gpre_d.partition_broadcast(128), writes=[B["gpre"]])
        S.dma("sp", gpost[:], gpost_d.partition_broadcast(128), writes=[B["gpost"]])
        S.op("pool", lambda: nc.gpsimd.memset(stat[:], 0.0), writes=[B["stat"]])
        S.op("pool", lambda: nc.gpsimd.tensor_scalar(out=gpost[:], in0=gpost[:], scalar1=0.5, scalar2=None,
                                                     op0=ALU.mult), reads=[B["gpost"]], writes=[B["gpost"]])
        if ple is not None:
            S.dma("sp", gple[:], ple["gain"].partition_broadcast(128), writes=[B["gple"]])
            S.dma("pool", wpg[:], ple["wg"].rearrange("p (k n) -> p k n", k=8), writes=[B["wpg"]])
            S.dma("pool", wpp[:], ple["wp"].rearrange("p (k n) -> p k n", k=2), writes=[B["wpp"]])

        wd_loaded = False
        tcount = 0
        gcount = 0
        bcount = 0
        for half in halves:
            if not half:
                continue
            col = 0
            cols = []
            for (r0, d0, n) in half:
                s = tcount % 2
                st = 8 * (tcount % 8)
                S.dma("sp", xin[s][:n, :], x_src[r0:r0 + n, :], writes=[B["xin", s]])
                S.op("pool", lambda: nc.gpsimd.memset(stat[:, st:st + 8], 0.0), writes=[B["stat"]])
                S.op("act", lambda: nc.scalar.activation(out=junk[:n, :], in_=xin[s][:n, :], func=AF.Square,
                                                         accum_out=stat[:n, st:st + 1]),
                     reads=[B["xin", s], B["stat"]], writes=[B["junk"], B["stat"]])
                rstd_ops(C, stat[:n, st:st + 1], stat[:n, st + 1:st + 2], n, D, B["stat"])
                S.op("dve", lambda: nc.vector.scalar_tensor_tensor(
                    out=hb[s][:n, :], in0=xin[s][:n, :], scalar=stat[:n, st + 1:st + 2], in1=gpre[:n, :],
                    op0=ALU.mult, op1=ALU.mult), reads=[B["xin", s], B["stat"], B["gpre"]], writes=[B["hb", s]])
                for k in range(8):
                    S.op("pe", lambda: nc.tensor.transpose(pT[s][:, k, :n], hb[s][:n, k * 128:(k + 1) * 128],
                                                           C.idb[:n, :n]),
                         reads=[B["hb", s], C.B_id], writes=[B["pT", s]], inc=(k == 7))
                S.op("act", lambda: nc.scalar.copy(out=hT[:, :, col:col + n], in_=pT[s][:, :, :n]),
                     reads=[B["pT", s]], writes=[B["hT"]])
                cols.append(col)
                col += n
                tcount += 1
            ntok = col
            tblocks = blocks_of(0, ntok)
            for g in range(NG):
                s = gcount % 2
                gcount += 1
                S.dma("pool", wg[s][:], wg_d[g].rearrange("p (k c) -> p k c", k=8), writes=[B["wg", s]])
                S.dma("pool", wu[s][:], wu_d[g].rearrange("p (k c) -> p k c", k=8), writes=[B["wu", s]])
                if not wd_loaded and g == 2:
                    for q in range(2):
                        S.dma("pool", wd[:, q * 11:(q + 1) * 11, :],
                              wd_d[:, q * 11 * D:(q + 1) * 11 * D].rearrange("p (j n) -> p j n", j=11),
                              writes=[B["wd"]])
                    wd_loaded = True
                for jj in range(GW // 128):
                    j = g * (GW // 128) + jj
                    for (b0, bn) in tblocks:
                        bs = bcount % 2
                        bcount += 1
                        for k in range(8):
                            S.op("pe", lambda: nc.tensor.matmul(pg[bs][:, :bn], lhsT=wg[s][:, k, jj * 128:(jj + 1) * 128],
                                                                rhs=hT[:, k, b0:b0 + bn], start=(k == 0), stop=(k == 7)),
                                 reads=[B["wg", s], B["hT"]], writes=[B["pg", bs]], inc=(k == 7))
                        for k in range(8):
                            S.op("pe", lambda: nc.tensor.matmul(pu[bs][:, :bn], lhsT=wu[s][:, k, jj * 128:(jj + 1) * 128],
                                                                rhs=hT[:, k, b0:b0 + bn], start=(k == 0), stop=(k == 7)),
                                 reads=[B["wu", s], B["hT"]], writes=[B["pu", bs]], inc=(k == 7))
                        S.op("act", lambda: nc.scalar.activation(out=sg[bs][:, :bn], in_=pg[bs][:, :bn], func=AF.Silu),
                             reads=[B["pg", bs]], writes=[B["sg", bs]])
                        S.op("dve", lambda: nc.vector.tensor_tensor(out=aT[:, j, b0:b0 + bn], in0=sg[bs][:, :bn],
                                                                    in1=pu[bs][:, :bn], op=ALU.mult),
                             reads=[B["sg", bs], B["pu", bs]], writes=[B["aT"]])
            for ti, (r0, d0, n) in enumerate(half):
                c0 = cols[ti]
                s = tcount % 2
                st = 8 * (tcount % 8)
                tcount += 1
                S.dma("sp", xin[s][:n, :], x_src[r0:r0 + n, :], writes=[B["xin", s]])
                S.op("pool", lambda: nc.gpsimd.memset(stat[:, st:st + 8], 0.0), writes=[B["stat"]])
                use_b = (ple is None) and (ti % 2 == 1)
                for nh in range(2):
                    dst = pg[nh][:n, :] if use_b else po[:n, nh * 512:(nh + 1) * 512]
                    wb = B["pg", nh] if use_b else B["po"]
                    for j in range(NJ):
                        S.op("pe", lambda: nc.tensor.matmul(dst, lhsT=aT[:, j, c0:c0 + n],
                                                            rhs=wd[:, j, nh * 512:(nh + 1) * 512],
                                                            start=(j == 0), stop=(j == NJ - 1)),
                             reads=[B["aT"], B["wd"]], writes=[wb], inc=(j == NJ - 1 and (nh == 1 or use_b)))
                if use_b:
                    for nh in range(2):
                        S.op("act", lambda: nc.scalar.activation(out=junk[:n, nh * 512:(nh + 1) * 512], in_=pg[nh][:n, :],
                                                                 func=AF.Square, accum_out=stat[:n, st + 2 + nh:st + 3 + nh]),
                             reads=[B["pg", nh], B["stat"]], writes=[B["junk"], B["stat"]])
                    S.op("dve", lambda: nc.vector.tensor_tensor(out=stat[:n, st:st + 1], in0=stat[:n, st + 2:st + 3],
                                                                in1=stat[:n, st + 3:st + 4], op=ALU.add),
                         reads=[B["stat"]], writes=[B["stat"]])
                    rstd_ops(C, stat[:n, st:st + 1], stat[:n, st + 1:st + 2], n, D, B["stat"])
                    for nh in range(2):
                        S.op("dve", lambda: nc.vector.scalar_tensor_tensor(
                            out=tmp[s][:n, nh * 512:(nh + 1) * 512], in0=pg[nh][:n, :], scalar=stat[:n, st + 1:st + 2],
                            in1=gpost[:n, nh * 512:(nh + 1) * 512], op0=ALU.mult, op1=ALU.mult),
                            reads=[B["pg", nh], B["stat"], B["gpost"]], writes=[B["tmp", s]])
                else:
                    S.op("act", lambda: nc.scalar.activation(out=junk[:n, :], in_=po[:n, :], func=AF.Square,
                                                             accum_out=stat[:n, st:st + 1]),
                         reads=[B["po"], B["stat"]], writes=[B["junk"], B["stat"]])
                    rstd_ops(C, stat[:n, st:st + 1], stat[:n, st + 1:st + 2], n, D, B["stat"])
                    S.op("dve", lambda: nc.vector.scalar_tensor_tensor(
                        out=tmp[s][:n, :], in0=po[:n, :], scalar=stat[:n, st + 1:st + 2], in1=gpost[:n, :],
                        op0=ALU.mult, op1=ALU.mult), reads=[B["po"], B["stat"], B["gpost"]], writes=[B["tmp", s]])
                S.op("pool", lambda: nc.gpsimd.tensor_tensor(out=yo[s][:n, :], in0=tmp[s][:n, :], in1=xin[s][:n, :],
                                                             op=ALU.add),
                     reads=[B["tmp", s], B["xin", s]], writes=[B["yo", s]])
                if ple is None:
                    S.dma("sp", x_dst[d0:d0 + n, :], yo[s][:n, :], reads=[B["yo", s]], writes=[C.B_dram[x_dst.tensor.name]])
                    continue
                pr0 = ple["prow"](d0)
                S.dma("sp", pin[s][:n, :], ple["p"][pr0:pr0 + n, :], writes=[B["pin", s]])
                S.op("act", lambda: nc.scalar.copy(out=hb[s][:n, :], in_=yo[s][:n, :]),
                     reads=[B["yo", s]], writes=[B["hb", s]])
                S.op("pool", lambda: nc.gpsimd.tensor_copy(out=pb[s][:n, :], in_=pin[s][:n, :]),
                     reads=[B["pin", s]], writes=[B["pb", s]])
                for k in range(8):
                    S.op("pe", lambda: nc.tensor.transpose(pT[0][:, k, :n], hb[s][:n, k * 128:(k + 1) * 128],
                                                           C.idb[:n, :n]),
                         reads=[B["hb", s], C.B_id], writes=[B["pT", 0]], inc=(k == 7))
                S.op("dve", lambda: nc.vector.tensor_copy(out=xT3[:, :, :n], in_=pT[0][:, :, :n]),
                     reads=[B["pT", 0]], writes=[B["xT3"]])
                for k in range(2):
                    S.op("pe", lambda: nc.tensor.transpose(pT[1][:, k, :n], pb[s][:n, k * 128:(k + 1) * 128],
                                                           C.idb[:n, :n]),
                         reads=[B["pb", s], C.B_id], writes=[B["pT", 1]], inc=(k == 1))
                S.op("dve", lambda: nc.vector.tensor_copy(out=pT3[:, :, :n], in_=pT[1][:, 0:2, :n]),
                     reads=[B["pT", 1]], writes=[B["pT3"]])
                for nh in range(2):
                    for k in range(8):
                        S.op("pe", lambda: nc.tensor.matmul(pg[nh][:n, :], lhsT=xT3[:, k, :n],
                                                            rhs=wpg[:, k, nh * 512:(nh + 1) * 512],
                                                            start=(k == 0), stop=(k == 7)),
                             reads=[B["xT3"], B["wpg"]], writes=[B["pg", nh]], inc=(k == 7))
                    for k in range(2):
                        S.op("pe", lambda: nc.tensor.matmul(pu[nh][:n, :], lhsT=pT3[:, k, :n],
                                                            rhs=wpp[:, k, nh * 512:(nh + 1) * 512],
                                                            start=(k == 0), stop=(k == 1)),
                             reads=[B["pT3"], B["wpp"]], writes=[B["pu", nh]], inc=(k == 1))
                    S.op("act", lambda: nc.scalar.activation(out=sg[nh][:n, :], in_=pg[nh][:n, :],
                                                             func=AF.Sigmoid),
                         reads=[B["pg", nh]], writes=[B["sg", nh]])
                    S.op("dve", lambda: nc.vector.tensor_tensor(out=prod[:n, nh * 512:(nh + 1) * 512],
                                                                in0=sg[nh][:n, :],
                                                                in1=pu[nh][:n, :], op=ALU.mult),
                         reads=[B["sg", nh], B["pu", nh]], writes=[B["prod"]])
                S.op("act", lambda: nc.scalar.activation(out=junk[:n, :], in_=prod[:n, :], func=AF.Square,
                                                         accum_out=stat[:n, st + 2:st + 3]),
                     reads=[B["prod"], B["stat"]], writes=[B["junk"], B["stat"]])
                rstd_ops(C, stat[:n, st + 2:st + 3], stat[:n, st + 3:st + 4], n, D, B["stat"])
                S.op("dve", lambda: nc.vector.scalar_tensor_tensor(
                    out=tmp[s][:n, :], in0=prod[:n, :], scalar=stat[:n, st + 3:st + 4], in1=gple[:n, :],
                    op0=ALU.mult, op1=ALU.mult), reads=[B["prod"], B["stat"], B["gple"]], writes=[B["tmp", s]])
                S.op("pool", lambda: nc.gpsimd.tensor_tensor(out=tmp[s][:n, :], in0=tmp[s][:n, :], in1=yo[s][:n, :],
                                                             op=ALU.add),
                     reads=[B["tmp", s], B["yo", s]], writes=[B["tmp", s]])
                S.dma("sp", x_dst[d0:d0 + n, :], tmp[s][:n, :], reads=[B["tmp", s]], writes=[C.B_dram[x_dst.tensor.name]])


O_Z, O_QS, O_KS, O_VS, O_B, O_A, NPROJ = 1536, 2048, 2560, 2688, 2816, 2820, 2824
NEG = -30000.0


DBG_STOP = None
MIXW = (3, 1, 1)
FFN_PART = 12
MIXP = ((0, 1, 2), (3, 4), (5, 6))
FILLER = False


class _Stop(Exception):
    pass


def _ck(tag):
    if DBG_STOP == tag:
        raise _Stop()


def mix_phase(C, d, part="AB"):
    try:
        _mix_phase(C, d, part)
    except _Stop:
        pass


def _mix_phase(C, d, part="AB"):
    S, nc = C.S, C.nc
    do1 = part in ("A", "AB", "F")
    do2 = part in ("B", "AB", "F")
    FUS = part == "F"
    PRE0 = NPRE if FUS else NHALO
    DVW = 128 if FUS else 256
    NT = NMAIN // 128
    S.barrier()
    with contextlib.ExitStack() as es:
        pre = {}
        for nm_, shape_, dt_ in [("cm", [128, 4, 128], F32), ("gmpre", [128, D], F32), ("gmpost", [128, D], F32),
                                 ("gdnn", [128, 128], F32), ("sm", [128, 32], F32), ("xt", [128, D], F32),
                                 ("junk", [128, D], BF16), ("st", [128, 64], F32), ("zsb", [128, 512], F32),
                                 ("oloc", [128, 4, 128], F32), ("oPT", [128, 4, 128], BF16),
                                 ("swaT", [64, 8, 128], BF16), ("swaTs", [64, 8, 64], BF16),
                                 ("og", [128, 4, 128], F32), ("og2", [128, 4, 128], F32), ("sz", [128, 512], F32),
                                 ("ogb", [128, 512], BF16), ("mixT", [128, 4, 128], BF16), ("mtmp", [128, D], F32)]:
            pre[nm_] = es.enter_context(nc.sbuf_tensor("m_" + nm_, shape_, dt_))
        esA = es.enter_context(contextlib.ExitStack())

        def sb(nm, shape, dt=F32):
            if nm in pre:
                return pre[nm]
            return es.enter_context(nc.sbuf_tensor("m_" + nm, shape, dt))

        def sbA(nm, shape, dt=F32):
            return esA.enter_context(nc.sbuf_tensor("m_" + nm, shape, dt))

        def ps(nm, shape, dt=F32):
            return es.enter_context(nc.psum_tensor("m_" + nm, shape, dt))

        Bd = {}

        def B(k):
            if k not in Bd:
                Bd[k] = Buf("m" + str(k))
            return Bd[k]

        def DB(ap):
            return C.B_dram[ap.tensor.name]

        def dve(fn, r, w):
            return S.op("dve", fn, [B(x) if not isinstance(x, Buf) else x for x in r],
                        [B(x) if not isinstance(x, Buf) else x for x in w])

        def act(fn, r, w):
            return S.op("act", fn, [B(x) if not isinstance(x, Buf) else x for x in r],
                        [B(x) if not isinstance(x, Buf) else x for x in w])

        def pool(fn, r, w):
            return S.op("pool", fn, [B(x) if not isinstance(x, Buf) else x for x in r],
                        [B(x) if not isinstance(x, Buf) else x for x in w])

        def pe(fn, r, w, inc=True):
            ins = S.op("pe", fn, [B(x) if not isinstance(x, Buf) else x for x in r],
                       [B(x) if not isinstance(x, Buf) else x for x in w], inc=inc)
            if inc and FILLER and fill_on[0]:
                nc.tensor.matmul(dummy_bank[:, :], lhsT=win[:, 0, 0:128], rhs=win[:, 1, 0:512], start=True, stop=True)
            return ins

        def dma(on, out, in_, r, w, **kw):
            return S.dma(on, out, in_, [B(x) if not isinstance(x, Buf) else x for x in r],
                         [B(x) if not isinstance(x, Buf) else x for x in w], **kw)

        pTb = ps("pTb", [128, 8, 128], BF16)
        NBK = 6 if FILLER else 7
        banks = [ps("bk%d" % i, [128, 512], F32) for i in range(NBK)]
        dummy_bank = ps("bkdummy", [128, 512], F32) if FILLER else None
        bstate = [0]
        fill_on = [False]

        bpool = [tuple(range(NBK))]
        bpos = {}

        def bank():
            pl = bpool[0]
            k_ = bpos.get(pl, 0)
            bpos[pl] = k_ + 1
            i = pl[k_ % len(pl)]
            return banks[i], "bk%d" % i

        def v3(t, n, a):
            return t[:n, :].rearrange("p (a b) -> p a b", a=a)

        win = sbA("win", [128, 8, NPROJ], BF16)
        for k in range(8):
            dma("pool", win[:, k, :], d["win"][:, k * NPROJ:(k + 1) * NPROJ], [], ["win"])
        cw = sbA("cw", [128, 12, 4])
        dma("sp", cw[:], d["convw"].rearrange("p (c j) -> p c j", c=12), [], ["cw"])
        cm = sb("cm", [128, 4, 128])
        dma("sp", cm[:], d["cmask"].rearrange("p (c j) -> p c j", c=4), [], ["cm"])
        TRI, INCL, STRICT, ONES = cm[:, 0, :], cm[:, 1, :], cm[:, 2, :], cm[:, 3, :]
        sel65 = sbA("sel65", [65, 64])
        dma("sp", sel65[:], d["sel65"], [], ["sel65"])
        bprev = sbA("bprev", [128, 8, 128]); bcur = sbA("bcur", [128, 8, 128])
        dma("sp", bprev[:], d["bprev1"].rearrange("p (h q) -> p h q", h=8), [], ["bprev"])
        dma("sp", bcur[:], d["bcur"].rearrange("p (h q) -> p h q", h=8), [], ["bcur"])
        bsc = sbA("bsc", [128, 8, 4]); bsn = sbA("bsn", [4, 8, 4])
        dma("sp", bsc[:], d["bs_cache"].rearrange("p (h q) -> p h q", h=8), [], ["bsc"])
        dma("sp", bsn[:], d["bs_new"].rearrange("p (h q) -> p h q", h=8), [], ["bsn"])
        gmpre = sb("gmpre", [128, D]); gmpost = sb("gmpost", [128, D])
        dma("sp", gmpre[:], d["gmpre"].partition_broadcast(128), [], ["gmpre"])
        dma("sp", gmpost[:], d["gmpost"].partition_broadcast(128), [], ["gmpost"])
        gdnn = sb("gdnn", [128, 128])
        dma("sp", gdnn[:], d["gdnn"].partition_broadcast(128), [], ["gdnn"])
        sm = sb("sm", [128, 32])
        dma("sp", sm[:, 0:4], d["alog"].partition_broadcast(128), [], ["sm"])
        dma("sp", sm[:, 4:8], d["dtb"].partition_broadcast(128), [], ["sm"])
        dma("sp", sm[:, 8:16], d["sinks"].partition_broadcast(128), [], ["sm"])
        dma("sp", sm[:, 16:24], d["rmask"].partition_broadcast(128), [], ["sm"])
        act(lambda: nc.scalar.activation(out=sm[:, 0:4], in_=sm[:, 0:4], func=AF.Exp), ["sm"], ["sm"])
        dve(lambda: nc.vector.tensor_scalar(out=sm[:, 0:4], in0=sm[:, 0:4], scalar1=-1.0, scalar2=None, op0=ALU.mult),
            ["sm"], ["sm"])
        act(lambda: nc.scalar.activation(out=sm[:, 8:16], in_=sm[:, 8:16], func=AF.Exp), ["sm"], ["sm"])
        negA, dtb, rmask = sm[:, 0:4], sm[:, 4:8], sm[:, 16:24]
        idf, idb = C.idf, C.idb
        ID = C.B_id

        xt = sb("xt", [128, D]); junk = sb("junk", [128, D], BF16)
        hbm = sbA("hbm", [128, D], BF16)
        st = sb("st", [128, 64])
        pool(lambda: nc.gpsimd.memset(st[:], 0.0), [], ["st"])
        hT = sbA("hT", [128, 8, 128], BF16)
        rawx = [sbA("rawx%d" % i, [128, 12, 131]) for i in range(2)]
        rawxs = rawx[1][:, :, 0:112].rearrange("p c (s t) -> p c s t", t=7)
        ctmp = sbA("ctmp", [128, 12, 128])
        sq = ctmp[:, 0:8, :]
        rn = sbA("rn", [128, 8, 128])
        zsb = sb("zsb", [128, 512])
        ksT = [sbA("ksT%d" % i, [64, 2, 128], BF16) for i in range(4)]
        vaug = [sbA("vaug%d" % i, [128, 2, 65], BF16) for i in range(4)]
        for i in range(4):
            pool(lambda: nc.gpsimd.memset(vaug[i][:], 1.0), [], ["vaug%d" % i])

        class Slot:
            pass
        PS = []
        for i in range(2):
            P_ = Slot()
            P_.nm = {}
            for nm_, shape_, dt_ in [("qkf", [128, 8, 128], F32), ("qkb", [128, 8, 128], BF16),
                                     ("cacc", [128, 12, 128], F32), ("misc", [128, 264], F32),
                                     ("qsT", [64, 8, 128], BF16), ("qdecT", [128, 4, 128], BF16),
                                     ("qkT", [128, 4, 128], BF16), ("kdec", [128, 4, 128], BF16),
                                     ("wT", [128, 4, 128], BF16), ("uaug", [128, 4, DVW], F32), ("glb", [128, 8], F32)]:
                setattr(P_, nm_, sbA("%s_%d" % (nm_, i), shape_, dt_))
                P_.nm[nm_] = "%s_%d" % (nm_, i)
            P_.ys = P_.cacc
            pool(lambda: nc.gpsimd.memset(P_.uaug[:], 0.0), [], [P_.nm["uaug"]])
            PS.append(P_)
        qsT3 = [PS[0].qsT, PS[1].qsT, sbA("qsT_2", [64, 8, 128], BF16)]
        _slots = {}

        def slot(ti):
            key = (ti % 2, ti % 3)
            if key not in _slots:
                Q = Slot()
                Q.__dict__.update(PS[ti % 2].__dict__)
                Q.nm = dict(PS[ti % 2].nm)
                Q.qsT = qsT3[ti % 3]
                Q.nm["qsT"] = "qsT_%d" % (ti % 3)
                _slots[key] = Q
            return _slots[key]
        g4 = sbA("g4", [128, 48])
        pool(lambda: nc.gpsimd.memset(g4[:], 0.0), [], ["g4"])
        diag = sbA("t1", [128, 4, 128]); erow = sbA("erow", [128, 4, 128])
        t1 = diag; dec = sbA("dec", [128, 4, 128]); decT = sbA("decT", [128, 4, 128])
        ktm = sbA("ktm", [128, 4, 128]); vtm = sbA("vtm", [128, 4, 128])
        kbg = sbA("kbg", [128, 4, 128], BF16)
        vb = sbA("vb", [128, 4, 128], BF16)
        X = [sbA("X%d" % i, [128, 4, 128]) for i in range(2)]
        XT = [sbA("XT%d" % i, [128, 4, 128]) for i in range(2)]
        Nm = X[1]
        TT = sbA("TT", [128, 4, 128]); TTb = sbA("TTb", [128, 4, 128], BF16)
        vnew = sbA("vnew", [128, 4, DVW], BF16)
        Sf = sbA("Sf", [128, 4, DVW]); Sb_ = sbA("Sb", [128, 4, DVW], BF16)
        oloc = sb("oloc", [128, 4, 128]); oPT = sb("oPT", [128, 4, 128], BF16)
        tp = sbA("tp", [128, 512]); pTp = sbA("pTp", [128, 512], BF16)
        tc_ = sbA("tc", [128, 512]); pTc = sbA("pTc", [128, 512], BF16)
        oTa = sbA("oTa", [65, 512]); rden = sbA("rden", [64, 512])

        def drain(g):
            for _ in g:
                pass

        def interleave(*gens, weights=None, pools=None):
            ws = list(weights) if weights else [1] * len(gens)
            ps_ = list(pools) if pools else [tuple(range(NBK))] * len(gens)
            gw = [(g, w_, p_) for g, w_, p_ in zip(gens, ws, ps_) if g is not None]
            full = bpool[0]
            while gw:
                for g, w_, p_ in list(gw):
                    bpool[0] = tuple(p_)
                    for _ in range(w_):
                        try:
                            next(g)
                        except StopIteration:
                            gw.remove((g, w_, p_))
                            break
            bpool[0] = full
        swaT = sb("swaT", [64, 8, 128], BF16); swaTs = sb("swaTs", [64, 8, 64], BF16)
        kcf = sbA("kcf", [128, 128]); vcf = sbA("vcf", [128, 128])
        og = sb("og", [128, 4, 128]); og2 = sb("og2", [128, 4, 128]); sz = sb("sz", [128, 512])
        ogb = sb("ogb", [128, 512], BF16)
        mixT = sb("mixT", [128, 4, 128], BF16)
        mtmp = sb("mtmp", [128, D])
        cvo = sbA("cvo", [48, 512])
        stc = [0]

        def newstat(k=4):
            c = stc[0]
            stc[0] += k
            assert stc[0] <= 64
            return c

        def load_norm_T(r0, n):
            dma("sp", xt[:n, :], d["x1"][r0:r0 + n, :], [DB(d["x1"])], ["xt"])
            pool(lambda: nc.gpsimd.memset(st[:, 0:4], 0.0), [], ["st"])
            yield
            act(lambda: nc.scalar.activation(out=junk[:n, :], in_=xt[:n, :], func=AF.Square, accum_out=st[:n, 0:1]),
                ["xt", "st"], ["junk", "st"])
            yield
            rstd_ops(C, st[:n, 0:1], st[:n, 1:2], n, D, B("st"))
            dve(lambda: nc.vector.scalar_tensor_tensor(out=hbm[:n, :], in0=xt[:n, :], scalar=st[:n, 1:2],
                                                       in1=gmpre[:n, :], op0=ALU.mult, op1=ALU.mult),
                ["xt", "st", "gmpre"], ["hbm"])
            yield
            for k in range(8):
                pe(lambda: nc.tensor.transpose(pTb[:, k, :n], hbm[:n, k * 128:(k + 1) * 128], idb[:n, :n]),
                   ["hbm", ID], ["pTb"], inc=(k == 7))
            act(lambda: nc.scalar.copy(out=hT[:, :, :n], in_=pTb[:, :, :n]), ["pTb"], ["hT"])
            yield

        def proj_feat(P, n, raw_dst, with_qs=True, with_q=True):
            traw = ctmp[:, :, :].rearrange("p c t -> p (c t)")
            nb0 = 0 if with_q else 1
            for nb in range(nb0, 3):
                bk, bn = bank()
                for k in range(8):
                    pe(lambda: nc.tensor.matmul(bk[:n, :], lhsT=hT[:, k, :n], rhs=win[:, k, nb * 512:(nb + 1) * 512],
                                                start=(k == 0), stop=(k == 7)), ["win", "hT"], [bn], inc=(k == 7))
                act(lambda: nc.scalar.copy(out=traw[:n, nb * 512:(nb + 1) * 512], in_=bk[:n, :]), [bn], ["ctmp"])
                yield
            for g in range(nb0, 3):
                bk, bn = bank()
                for cc in range(4):
                    c = g * 4 + cc
                    pe(lambda: nc.tensor.transpose(bk[:, cc * 128:cc * 128 + n], traw[:n, c * 128:(c + 1) * 128], idf[:n, :n]),
                       ["ctmp", ID], [bn], inc=(cc == 3))
                raw_dst(g, v3(bk, 128, 4)[:, :, :n], bn)
                yield
        def proj_qs(P, n, with_qs=True):
            if with_qs:
                bk, bn = bank()
                for k in range(8):
                    pe(lambda: nc.tensor.matmul(bk[:n, :], lhsT=hT[:, k, :n], rhs=win[:, k, O_QS:O_QS + 512],
                                                start=(k == 0), stop=(k == 7)), ["win", "hT"], [bn], inc=(k == 7))
                act(lambda: nc.scalar.copy(out=hbm[:n, 0:512], in_=bk[:n, :]), [bn], ["hbm"])
                yield
                for h in range(8):
                    pe(lambda: nc.tensor.transpose(pTb[:64, h, :n], hbm[:n, h * 64:(h + 1) * 64], idb[:n, :n]),
                       ["hbm", ID], ["pTb"], inc=(h == 7))
                act(lambda: nc.scalar.copy(out=P.qsT[:, :, :n], in_=pTb[:64, :, :n]), ["pTb"], [P.nm["qsT"]])
                yield

        def proj_ksT(P, n, dst, dname):
            act(lambda: nc.scalar.copy(out=hbm[:n, 512:640], in_=P.misc[:n, 0:128]), [P.nm["misc"]], ["hbm"])
            yield
            for kh in range(2):
                pe(lambda: nc.tensor.transpose(pTb[:64, kh, :n], hbm[:n, 512 + kh * 64:512 + (kh + 1) * 64], idb[:n, :n]),
                   ["hbm", ID], ["pTb"], inc=(kh == 1))
            act(lambda: nc.scalar.copy(out=dst[:, :, :n], in_=pTb[:64, 0:2, :n]), ["pTb"], [dname])
            yield

        def proj_tok(P, c0, n, with_z=True, zdst=None, zname="zsb"):
            zdst = zsb if zdst is None else zdst
            if with_z:
                bk, bn = bank()
                for k in range(8):
                    pe(lambda: nc.tensor.matmul(bk[:n, :], lhsT=hT[:, k, c0:c0 + n], rhs=win[:, k, O_Z:O_Z + 512],
                                                start=(k == 0), stop=(k == 7)), ["win", "hT"], [bn], inc=(k == 7))
                act(lambda: nc.scalar.copy(out=zdst[:n, :], in_=bk[:n, :]), [bn], [zname])
                yield
            bk2, bn2 = bank()
            for k in range(8):
                pe(lambda: nc.tensor.matmul(bk2[:n, 0:264], lhsT=hT[:, k, c0:c0 + n], rhs=win[:, k, O_KS:O_KS + 264],
                                            start=(k == 0), stop=(k == 7)), ["win", "hT"], [bn2], inc=(k == 7))
            dve(lambda: nc.vector.tensor_copy(out=P.misc[:n, :], in_=bk2[:n, 0:264]), [bn2], [P.nm["misc"]])
            yield

        def conv_l2(P, n, taps, ys_v, full=True):
            c_lo = 0 if full else 4
            cwv = cw[:, c_lo:12, :]
            cw_b = lambda j, shape: cwv[:, :, j:j + 1].to_broadcast(shape) if len(shape) == 3 else \
                cwv[:, :, j:j + 1].unsqueeze(3).to_broadcast(shape)
            tp_ = lambda j: taps(j)[:, c_lo:12]
            shape = list(tp_(0).shape)
            accv = P.cacc[:, c_lo:12, :n] if len(shape) == 3 else P.cacc[:, c_lo:12, :n].rearrange("p c (s t) -> p c s t", t=4)
            tmpv = ctmp[:, c_lo:12, :n] if len(shape) == 3 else ctmp[:, c_lo:12, :n].rearrange("p c (s t) -> p c s t", t=4)
            dve(lambda: nc.vector.tensor_tensor(out=accv, in0=tp_(0), in1=cw_b(0, shape), op=ALU.mult),
                ["rawcur", "cw"], [P.nm["cacc"]])
            yield
            for j in range(1, 4):
                dve(lambda: nc.vector.tensor_tensor(out=tmpv, in0=tp_(j), in1=cw_b(j, shape), op=ALU.mult),
                    ["rawcur", "cw"], ["ctmp"])
                yield
                dve(lambda: nc.vector.tensor_tensor(out=accv, in0=accv, in1=tmpv, op=ALU.add), [P.nm["cacc"], "ctmp"], [P.nm["cacc"]])
                yield
            act(lambda: nc.scalar.activation(out=P.ys[:, c_lo:12, :n], in_=P.cacc[:, c_lo:12, :n], func=AF.Silu), [P.nm["cacc"]], [P.nm["cacc"]])
            yield
            pool(lambda: nc.gpsimd.tensor_tensor(out=sq[:, c_lo:8, :n], in0=P.ys[:, c_lo:8, :n], in1=P.ys[:, c_lo:8, :n], op=ALU.mult),
                 [P.nm["cacc"]], ["ctmp"])
            yield
            for g in range(0 if full else 1, 2):
                bk, bn = bank()
                pe(lambda: nc.tensor.matmul(bk[:, :4 * n], lhsT=ONES, rhs=sq[:, g * 4:(g + 1) * 4, :n],
                                            start=True, stop=True), ["ctmp", "cm"], [bn])
                act(lambda: nc.scalar.activation(out=rn[:, g * 4:(g + 1) * 4, :n],
                                                 in_=bk[:, :4 * n].rearrange("p (a b) -> p a b", a=4),
                                                 func=AF.Ln, bias=C.epsc[:, 0:1]), [bn, ID], ["rn"])
                yield
            act(lambda: nc.scalar.activation(out=rn[:, c_lo:8, :n], in_=rn[:, c_lo:8, :n], func=AF.Exp, scale=-0.5),
                ["rn"], ["rn"])
            yield
            if full:
                dve(lambda: nc.vector.scalar_tensor_tensor(out=P.qkf[:, 0:4, :n], in0=P.ys[:, 0:4, :n], scalar=128.0 ** -0.5,
                                                           in1=rn[:, 0:4, :n], op0=ALU.mult, op1=ALU.mult),
                    [P.nm["cacc"], "rn"], [P.nm["qkf"]])
                yield
            dve(lambda: nc.vector.tensor_tensor(out=P.qkf[:, 4:8, :n], in0=P.ys[:, 4:8, :n], in1=rn[:, 4:8, :n], op=ALU.mult),
                [P.nm["cacc"], "rn"], [P.nm["qkf"]])
            yield
            act(lambda: nc.scalar.copy(out=P.qkb[:, c_lo:8, :n], in_=P.qkf[:, c_lo:8, :n]), [P.nm["qkf"]], [P.nm["qkb"]])
            yield

        def gdn_pre(P, n, c0, nlev, full=True):
            bcol = lambda ap: ap.unsqueeze(2).to_broadcast([n, 4, 128])
            bcn = lambda ap: ap.unsqueeze(2).to_broadcast([n, 4, n])
            G = g4
            dve(lambda: nc.vector.tensor_tensor(out=G[:n, 0:4], in0=P.misc[:n, 260:264], in1=dtb[:n, :], op=ALU.add),
                [P.nm["misc"], "sm"], ["g4"])
            yield
            dve(lambda: nc.vector.scalar_tensor_tensor(out=G[:n, 4:8], in0=G[:n, 0:4], scalar=-1.0, in1=G[:n, 0:4],
                                                       op0=ALU.mult, op1=ALU.max), ["g4"], ["g4"])
            yield
            act(lambda: nc.scalar.activation(out=G[:n, 4:8], in_=G[:n, 4:8], func=AF.Exp, scale=-1.0), ["g4"], ["g4"])
            yield
            act(lambda: nc.scalar.activation(out=G[:n, 4:8], in_=G[:n, 4:8], func=AF.Ln, bias=C.epsc[:n, 1:2]),
                ["g4", ID], ["g4"])
            yield
            dve(lambda: nc.vector.scalar_tensor_tensor(out=G[:n, 8:12], in0=G[:n, 0:4], scalar=0.0, in1=G[:n, 4:8],
                                                       op0=ALU.max, op1=ALU.add), ["g4"], ["g4"])
            yield
            dve(lambda: nc.vector.tensor_tensor(out=G[:n, 12:16], in0=G[:n, 8:12], in1=negA[:n, :], op=ALU.mult),
                ["g4", "sm"], ["g4"])
            yield
            act(lambda: nc.scalar.activation(out=G[:n, 16:20], in_=P.misc[:n, 256:260], func=AF.Exp, scale=-1.0),
                [P.nm["misc"]], ["g4"])
            yield
            dve(lambda: nc.vector.tensor_scalar(out=G[:n, 16:20], in0=G[:n, 16:20], scalar1=1.0, scalar2=None,
                                                op0=ALU.add), ["g4"], ["g4"])
            yield
            dve(lambda: nc.vector.reciprocal(out=G[:n, 16:20], in_=G[:n, 16:20]), ["g4"], ["g4"])
            yield
            dve(lambda: nc.vector.tensor_scalar(out=G[:n, 20:24], in0=G[:n, 16:20], scalar1=-1.0, scalar2=None,
                                                op0=ALU.mult), ["g4"], ["g4"])
            yield
            gcol, beta, nbeta = G[:n, 12:16], G[:n, 16:20], G[:n, 20:24]
            _ck("p1")
            bk, bn = bank()
            pe(lambda: nc.tensor.matmul(bk[:n, 0:4], lhsT=TRI[:n, :n], rhs=gcol, start=True, stop=True),
               ["g4", "cm"], [bn], inc=False)
            pe(lambda: nc.tensor.matmul(bk[:, 4:8], lhsT=ONES[:n, :], rhs=gcol, start=True, stop=True),
               ["g4", "cm"], [bn])
            dve(lambda: nc.vector.tensor_copy(out=G[:n, 24:28], in_=bk[:n, 0:4]), [bn], ["g4"])
            yield
            dve(lambda: nc.vector.tensor_copy(out=P.glb[:, 0:4], in_=bk[:, 4:8]), [bn], [P.nm["glb"]])
            yield
            gc = G[:n, 24:28]
            act(lambda: nc.scalar.activation(out=G[:n, 28:32], in_=gc, func=AF.Exp), ["g4"], ["g4"])
            yield
            dve(lambda: nc.vector.tensor_tensor(out=G[:n, 32:36], in0=P.glb[:n, 0:4], in1=gc, op=ALU.subtract),
                ["g4", P.nm["glb"]], ["g4"])
            yield
            act(lambda: nc.scalar.activation(out=G[:n, 32:36], in_=G[:n, 32:36], func=AF.Exp), ["g4"], ["g4"])
            yield
            act(lambda: nc.scalar.activation(out=P.glb[:, 4:8], in_=P.glb[:, 0:4], func=AF.Exp), [P.nm["glb"]], [P.nm["glb"]])
            yield
            eg, ekl = G[:n, 28:32], G[:n, 32:36]
            _ck("p2")
            for h in range(4):
                dve(lambda: nc.vector.tensor_scalar(out=diag[:n, h, :n], in0=idf[:n, :n], scalar1=gc[:, h:h + 1],
                                                    scalar2=None, op0=ALU.mult), ["g4", ID], ["t1"])
                yield
            gbk, gbn = bank()
            pe(lambda: nc.tensor.matmul(gbk[:, :4 * n], lhsT=ONES[:n, :], rhs=diag[:n, :, :n],
                                        start=True, stop=True), ["t1", "cm"], [gbn])
            grow = gbk[:, :4 * n].rearrange("p (a b) -> p a b", a=4)
            _ck("p2a")
            if full:
                act(lambda: nc.scalar.activation(out=erow[:, :, :n], in_=grow[:, :, :n], func=AF.Exp), [gbn], ["erow"])
                yield
            _ck("p2b")
            dve(lambda: nc.vector.tensor_scalar(out=G[:n, 44:48], in0=gc, scalar1=-1.0, scalar2=None, op0=ALU.mult),
                ["g4"], ["g4"])
            yield
            for h in range(4):
                act(lambda: nc.scalar.activation(out=dec[:n, h, :n], in_=grow[:n, h, :n], func=AF.Exp, scale=-1.0,
                                                 bias=gc[:, h:h + 1]), [gbn, "g4"], ["dec"])
                yield
            _ck("p2c")
            dve(lambda: nc.vector.scalar_tensor_tensor(out=dec[:n, :, :n], in0=dec[:n, :, :n], scalar=1.0,
                                                       in1=STRICT[:n, :n].unsqueeze(1).to_broadcast([n, 4, n]),
                                                       op0=ALU.min, op1=ALU.mult), ["dec", "cm"], ["dec"])
            yield
            pool(lambda: nc.gpsimd.tensor_tensor(out=dec[:n, :, :n], in0=dec[:n, :, :n], in1=bcn(nbeta), op=ALU.mult),
                 ["dec", "g4"], ["dec"])
            yield
            for h in range(4 if full else 0):
                act(lambda: nc.scalar.activation(out=decT[:n, h, :n], in_=grow[:n, h, :n], func=AF.Exp, scale=1.0,
                                                 bias=G[:n, 44 + h:45 + h]), [gbn, "g4"], ["decT"])
                yield
            if full:
                dve(lambda: nc.vector.scalar_tensor_tensor(out=decT[:n, :, :n], in0=decT[:n, :, :n], scalar=1.0,
                                                           in1=TRI[:n, :n].unsqueeze(1).to_broadcast([n, 4, n]),
                                                           op0=ALU.min, op1=ALU.mult), ["decT", "cm"], ["decT"])
                yield
                dve(lambda: nc.vector.tensor_tensor(out=P.qdecT[:, :, :n], in0=P.qkf[:, 0:4, c0:c0 + n], in1=erow[:, :, :n],
                                                    op=ALU.mult), [P.nm["qkf"], "erow"], [P.nm["qdecT"]])
                yield
            _ck("p3")
            bk, bn = bank()
            for h in range(4):
                pe(lambda: nc.tensor.transpose(bk[:n, h * 128:(h + 1) * 128], P.qkf[:, 4 + h, c0:c0 + n], idf[:, :]),
                   [P.nm["qkf"], ID], [bn], inc=(h == 3))
            act(lambda: nc.scalar.copy(out=ktm[:n, :, :], in_=v3(bk, n, 4)), [bn], ["ktm"])
            yield
            bk, bn = bank()
            for h in range(4):
                pe(lambda: nc.tensor.transpose(bk[:n, h * 128:(h + 1) * 128], P.ys[:, 8 + h, c0:c0 + n], idf[:, :]),
                   [P.nm["cacc"], ID], [bn], inc=(h == 3))
            act(lambda: nc.scalar.copy(out=vtm[:n, :, :], in_=v3(bk, n, 4)), [bn], ["vtm"])
            yield
            dve(lambda: nc.vector.tensor_tensor(out=G[:n, 36:40], in0=beta, in1=eg, op=ALU.mult), ["g4"], ["g4"])
            yield
            dve(lambda: nc.vector.tensor_tensor(out=kbg[:n], in0=ktm[:n], in1=bcol(G[:n, 36:40]), op=ALU.mult),
                ["ktm", "g4"], ["kbg"])
            yield
            pool(lambda: nc.gpsimd.tensor_tensor(out=P.kdec[:n], in0=ktm[:n], in1=bcol(ekl), op=ALU.mult),
                 ["ktm", "g4"], [P.nm["kdec"]])
            yield
            pool(lambda: nc.gpsimd.tensor_tensor(out=vb[:n], in0=vtm[:n], in1=bcol(beta), op=ALU.mult),
                 ["vtm", "g4"], ["vb"])
            yield
            _ck("p4")
            kbk, kbn = bank()
            for h in range(4):
                pe(lambda: nc.tensor.matmul(kbk[:n, h * 128:h * 128 + n], lhsT=P.qkb[:, 4 + h, c0:c0 + n],
                                            rhs=P.qkb[:, 4 + h, c0:c0 + n], start=True, stop=True),
                   [P.nm["qkb"]], [kbn], inc=(h == 3))
            if full:
                qbk, qbn = bank()
                for h in range(4):
                    pe(lambda: nc.tensor.matmul(qbk[:n, h * 128:h * 128 + n], lhsT=P.qkb[:, 4 + h, c0:c0 + n],
                                                rhs=P.qkb[:, h, c0:c0 + n], start=True, stop=True),
                       [P.nm["qkb"]], [qbn], inc=(h == 3))
            dve(lambda: nc.vector.tensor_tensor(out=X[0][:n, :, :n], in0=v3(kbk, n, 4)[:, :, :n], in1=dec[:n, :, :n],
                                                op=ALU.mult), [kbn, "dec"], ["X0"])
            yield
            if full:
                dve(lambda: nc.vector.tensor_tensor(out=P.qkT[:n, :, :n], in0=v3(qbk, n, 4)[:, :, :n], in1=decT[:n, :, :n],
                                                    op=ALU.mult), [qbn, "decT"], [P.nm["qkT"]])
                yield
            _ck("p5")
            bk, bn = bank()
            for h in range(4):
                pe(lambda: nc.tensor.transpose(bk[:n, h * 128:h * 128 + n], X[0][:n, h, :n], idf[:n, :n]),
                   ["X0", ID], [bn], inc=(h == 3))
            act(lambda: nc.scalar.copy(out=XT[0][:n, :, :n], in_=v3(bk, n, 4)[:, :, :n]), [bn], ["XT0"])
            yield
            dve(lambda: nc.vector.tensor_tensor(out=TT[:n, :, :n], in0=XT[0][:n, :, :n],
                                                in1=idf[:n, :n].unsqueeze(1).to_broadcast([n, 4, n]), op=ALU.add),
                ["XT0", ID], ["TT"])
            yield
            cur = 0
            for lev in range(1, nlev):
                nx = 1 - cur
                b1, b1n = bank()
                for h in range(4):
                    pe(lambda: nc.tensor.matmul(b1[:n, h * 128:h * 128 + n], lhsT=XT[cur][:n, h, :n],
                                                rhs=X[cur][:n, h, :n], start=True, stop=True),
                       ["X%d" % cur, "XT%d" % cur], [b1n], inc=(h == 3))
                act(lambda: nc.scalar.copy(out=X[nx][:n, :, :n], in_=v3(b1, n, 4)[:, :, :n]), [b1n], ["X%d" % nx])
                yield
                if lev < nlev - 1:
                    b2, b2n = bank()
                    for h in range(4):
                        pe(lambda: nc.tensor.matmul(b2[:n, h * 128:h * 128 + n], lhsT=X[cur][:n, h, :n],
                                                    rhs=XT[cur][:n, h, :n], start=True, stop=True),
                           ["X%d" % cur, "XT%d" % cur], [b2n], inc=(h == 3))
                    dve(lambda: nc.vector.tensor_copy(out=XT[nx][:n, :, :n], in_=v3(b2, n, 4)[:, :, :n]),
                        [b2n], ["XT%d" % nx])
                    yield
                b3, b3n = bank()
                for h in range(4):
                    pe(lambda: nc.tensor.matmul(b3[:n, h * 128:h * 128 + n], lhsT=X[nx][:n, h, :n],
                                                rhs=TT[:n, h, :n], start=True, stop=True),
                       ["X%d" % nx, "TT"], [b3n], inc=(h == 3))
                dve(lambda: nc.vector.tensor_tensor(out=TT[:n, :, :n], in0=TT[:n, :, :n], in1=v3(b3, n, 4)[:, :, :n],
                                                    op=ALU.add), ["TT", b3n], ["TT"])
                yield
                cur = nx
            act(lambda: nc.scalar.copy(out=TTb[:n, :, :n], in_=TT[:n, :, :n]), ["TT"], ["TTb"])
            yield
            _ck("p6")
            bk, bn = bank()
            for h in range(4):
                pe(lambda: nc.tensor.matmul(bk[:n, h * 128:(h + 1) * 128], lhsT=TTb[:n, h, :n], rhs=vb[:n, h, :],
                                            start=True, stop=True), ["TTb", "vb"], [bn], inc=(h == 3))
            act(lambda: nc.scalar.copy(out=P.uaug[:n, :, 0:128], in_=v3(bk, n, 4)), [bn], [P.nm["uaug"]])
            yield
            bk, bn = bank()
            for h in range(4):
                pe(lambda: nc.tensor.matmul(bk[:, h * 128:h * 128 + n], lhsT=kbg[:n, h, :], rhs=TTb[:n, h, :n],
                                            start=True, stop=True), ["TTb", "kbg"], [bn], inc=(h == 3))
            dve(lambda: nc.vector.tensor_copy(out=P.wT[:, :, :n], in_=v3(bk, 128, 4)[:, :, :n]), [bn], [P.nm["wT"]])
            yield

        def gdn_scan(P, n, dvw, state_only=False):
            aug = dvw == 256
            pb = [bank() for _ in range(2 if aug else 1)]
            per = 2 if aug else 4

            def reg(pbk, h):
                return pbk[h // per][0][:, (h % per) * dvw:(h % per + 1) * dvw]

            for h in range(4):
                pe(lambda: nc.tensor.matmul(reg(pb, h)[:n, :], lhsT=P.wT[:, h, :n], rhs=Sb_[:, h, :dvw],
                                            start=True, stop=True), [P.nm["wT"], "Sb"], [pb[h // per][1]],
                   inc=(h % per == per - 1))
            for i, (bk, bn) in enumerate(pb):
                dve(lambda: nc.vector.tensor_tensor(out=vnew[:n, i * per:(i + 1) * per, :dvw],
                                                    in0=P.uaug[:n, i * per:(i + 1) * per, :dvw],
                                                    in1=bk[:n, :per * dvw].rearrange("p (a b) -> p a b", a=per),
                                                    op=ALU.subtract), [P.nm["uaug"], bn], ["vnew"])
                yield
            if not state_only:
                obk, obn = bank()
                for h in range(4):
                    pe(lambda: nc.tensor.matmul(obk[:n, h * 128:(h + 1) * 128], lhsT=P.qdecT[:, h, :n], rhs=Sb_[:, h, 0:128],
                                                start=True, stop=False), [P.nm["qdecT"], "Sb"], [obn], inc=False)
                    pe(lambda: nc.tensor.matmul(obk[:n, h * 128:(h + 1) * 128], lhsT=P.qkT[:n, h, :n], rhs=vnew[:n, h, 0:128],
                                                start=False, stop=True), [P.nm["qkT"], "vnew"], [obn], inc=(h == 3))
                act(lambda: nc.scalar.copy(out=oloc[:n], in_=v3(obk, n, 4)), [obn], ["oloc"])
                yield
            if aug:
                pbk, pbn = bank()
                for h in range(4):
                    pe(lambda: nc.tensor.matmul(pbk[:, h * 128:h * 128 + n], lhsT=Sb_[:, h, 128:256], rhs=P.qdecT[:, h, :n],
                                                start=True, stop=False), [P.nm["qdecT"], "Sb"], [pbn], inc=False)
                    pe(lambda: nc.tensor.matmul(pbk[:, h * 128:h * 128 + n], lhsT=vnew[:n, h, 128:256], rhs=P.qkT[:n, h, :n],
                                                start=False, stop=True), [P.nm["qkT"], "vnew"], [pbn], inc=(h == 3))
                dve(lambda: nc.vector.tensor_copy(out=oPT[:, :, :n], in_=v3(pbk, 128, 4)[:, :, :n]), [pbn], ["oPT"])
                yield
            sbk = [bank() for _ in range(2 if aug else 1)]
            for h in range(4):
                pe(lambda: nc.tensor.matmul(reg(sbk, h), lhsT=P.kdec[:n, h, :], rhs=vnew[:n, h, :dvw],
                                            start=True, stop=True), [P.nm["kdec"], "vnew"], [sbk[h // per][1]],
                   inc=(h % per == per - 1))
            for h in range(4):
                dve(lambda: nc.vector.scalar_tensor_tensor(out=Sf[:, h, :dvw], in0=Sf[:, h, :dvw], scalar=P.glb[:, 4 + h:5 + h],
                                                           in1=reg(sbk, h), op0=ALU.mult, op1=ALU.add),
                    ["Sf", P.nm["glb"], sbk[h // per][1]], ["Sf"])
                yield
            act(lambda: nc.scalar.copy(out=Sb_[:, :, :dvw], in_=Sf[:, :, :dvw]), ["Sf"], ["Sb"])
            yield

        def swa(P, n, qc0, kprevT, kpn, vprev, vpn, kcurT, kcn, vcur, vcn, bp, bpn, bc, bcn_, dst, dstn, dst_c0):
            for kh in range(2):
                hs = slice(kh * 4, (kh + 1) * 4)
                pb_, pbn = bank()
                pe(lambda: nc.tensor.matmul(pb_[:, :4 * n], lhsT=kprevT[:, kh, :], rhs=P.qsT[:, hs, qc0:qc0 + n],
                                            start=True, stop=True), [kpn, P.nm["qsT"]], [pbn])
                cb_, cbn = bank()
                pe(lambda: nc.tensor.matmul(cb_[:n, :4 * n], lhsT=kcurT[:, kh, qc0:qc0 + n], rhs=P.qsT[:, hs, qc0:qc0 + n],
                                            start=True, stop=True), [kcn, P.nm["qsT"]], [cbn])
                dve(lambda: nc.vector.scalar_tensor_tensor(out=tp[:, :4 * n].rearrange("p (a b) -> p a b", a=4),
                                                           in0=pb_[:, :4 * n].rearrange("p (a b) -> p a b", a=4),
                                                           scalar=0.125, in1=bp[:, hs, :n], op0=ALU.mult, op1=ALU.add),
                    [pbn, bpn], ["tp"])
                yield
                act(lambda: nc.scalar.activation(out=pTp[:, :4 * n], in_=tp[:, :4 * n], func=AF.Exp), ["tp"], ["pTp"])
                yield
                dve(lambda: nc.vector.scalar_tensor_tensor(out=tc_[:n, :4 * n].rearrange("p (a b) -> p a b", a=4),
                                                           in0=cb_[:n, :4 * n].rearrange("p (a b) -> p a b", a=4),
                                                           scalar=0.125, in1=bc[:n, hs, :n], op0=ALU.mult, op1=ALU.add),
                    [cbn, bcn_], ["tc"])
                yield
                act(lambda: nc.scalar.activation(out=pTc[:n, :4 * n], in_=tc_[:n, :4 * n], func=AF.Exp), ["tc"], ["pTc"])
                yield
                ob_, obn = bank()
                pe(lambda: nc.tensor.matmul(ob_[:65, :4 * n], lhsT=vprev[:, kh, :], rhs=pTp[:, :4 * n],
                                            start=True, stop=False), [vpn, "pTp"], [obn], inc=False)
                pe(lambda: nc.tensor.matmul(ob_[:65, :4 * n], lhsT=vcur[:n, kh, :], rhs=pTc[:n, :4 * n],
                                            start=False, stop=True), [vcn, "pTc"], [obn])
                act(lambda: nc.scalar.copy(out=oTa[:, :4 * n], in_=ob_[:65, :4 * n]), [obn], ["oTa"])
                yield
                db_, dbn = bank()
                pe(lambda: nc.tensor.matmul(db_[:64, :4 * n], lhsT=sel65[:, :], rhs=oTa[:, :4 * n], start=True, stop=True),
                   ["oTa", "sel65"], [dbn])
                dve(lambda: nc.vector.tensor_tensor(out=rden[:, :4 * n].rearrange("p (a b) -> p a b", a=4),
                                                    in0=db_[:64, :4 * n].rearrange("p (a b) -> p a b", a=4),
                                                    in1=sm[:64, 8 + kh * 4:12 + kh * 4].unsqueeze(2).to_broadcast([64, 4, n]),
                                                    op=ALU.add), [dbn, "sm"], ["rden"])
                yield
                act(lambda: nc.scalar.activation(out=rden[:, :4 * n], in_=rden[:, :4 * n], func=AF.Ln), ["rden"], ["rden"])
                yield
                act(lambda: nc.scalar.activation(out=rden[:, :4 * n], in_=rden[:, :4 * n], func=AF.Exp, scale=-1.0),
                    ["rden"], ["rden"])
                yield
                dve(lambda: nc.vector.tensor_tensor(out=dst[:, hs, dst_c0:dst_c0 + n],
                                                    in0=oTa[0:64, :4 * n].rearrange("p (a b) -> p a b", a=4),
                                                    in1=rden[:, :4 * n].rearrange("p (a b) -> p a b", a=4), op=ALU.mult),
                    ["oTa", "rden"], [dstn])
                yield

        def gate_cols(n, o_ap, on, z_ap, zn, dst_c0):
            c = 8
            pool(lambda: nc.gpsimd.tensor_tensor(out=og2[:n], in0=o_ap, in1=o_ap, op=ALU.mult), [on], ["og2"])
            dve(lambda: nc.vector.tensor_reduce(out=st[:n, c:c + 4], in_=og2[:n], axis=AX.X, op=ALU.add),
                ["og2"], ["st"])
            act(lambda: nc.scalar.activation(out=st[:n, c + 4:c + 8], in_=st[:n, c:c + 4], func=AF.Ln, scale=1.0 / 128,
                                             bias=C.epsc[:n, 0:1]), ["st", ID], ["st"])
            act(lambda: nc.scalar.activation(out=st[:n, c + 4:c + 8], in_=st[:n, c + 4:c + 8], func=AF.Exp, scale=-0.5),
                ["st"], ["st"])
            dve(lambda: nc.vector.tensor_tensor(out=og[:n], in0=o_ap,
                                                in1=st[:n, c + 4:c + 8].unsqueeze(2).to_broadcast([n, 4, 128]),
                                                op=ALU.mult), [on, "st"], ["og"])
            pool(lambda: nc.gpsimd.tensor_tensor(out=og[:n], in0=og[:n],
                                                 in1=gdnn[:n, :].unsqueeze(1).to_broadcast([n, 4, 128]), op=ALU.mult),
                 ["og", "gdnn"], ["og"])
            act(lambda: nc.scalar.activation(out=sz[:n, :], in_=z_ap, func=AF.Silu), [zn], ["sz"])
            dve(lambda: nc.vector.tensor_tensor(out=ogb[:n, :], in0=og[:n].rearrange("p a b -> p (a b)"), in1=sz[:n, :],
                                                op=ALU.mult), ["og", "sz"], ["ogb"])
            for cc in range(4):
                pe(lambda: nc.tensor.transpose(pTb[:, cc, :n], ogb[:n, cc * 128:(cc + 1) * 128], idb[:n, :n]),
                   ["ogb", ID], ["pTb"], inc=(cc == 3))
            act(lambda: nc.scalar.copy(out=mixT[:, :, dst_c0:dst_c0 + n], in_=pTb[:, 0:4, :n]), ["pTb"], ["mixT"])

        def out_proj(n, swa_ap, swn, r1, r2):
            dma("sp", xt[:n, :], d["x1"][r1:r1 + n, :], [DB(d["x1"])], ["xt"])
            bks = [bank(), bank()]
            for nh in range(2):
                bk, bn = bks[nh]
                for cc in range(4):
                    pe(lambda: nc.tensor.matmul(bk[:n, :], lhsT=mixT[:, cc, :n], rhs=wog[:, cc, nh * 512:(nh + 1) * 512],
                                                start=(cc == 0), stop=False), ["mixT", "wog"], [bn], inc=False)
                for h in range(8):
                    pe(lambda: nc.tensor.matmul(bk[:n, :], lhsT=swa_ap[:, h, :n], rhs=wos[:, h, nh * 512:(nh + 1) * 512],
                                                start=False, stop=(h == 7)), [swn, "wos"], [bn], inc=(h == 7))
                act(lambda: nc.scalar.copy(out=mtmp[:n, nh * 512:(nh + 1) * 512], in_=bk[:n, :]), [bn], ["mtmp"])
            pool(lambda: nc.gpsimd.memset(st[:, 16:20], 0.0), [], ["st"])
            act(lambda: nc.scalar.activation(out=junk[:n, :], in_=mtmp[:n, :], func=AF.Square, accum_out=st[:n, 16:17]),
                ["mtmp", "st"], ["junk", "st"])
            rstd_ops(C, st[:n, 16:17], st[:n, 17:18], n, D, B("st"))
            dve(lambda: nc.vector.scalar_tensor_tensor(out=mtmp[:n, :], in0=mtmp[:n, :], scalar=st[:n, 17:18],
                                                       in1=gmpost[:n, :], op0=ALU.mult, op1=ALU.mult),
                ["mtmp", "st", "gmpost"], ["mtmp"])
            pool(lambda: nc.gpsimd.tensor_tensor(out=mtmp[:n, :], in0=mtmp[:n, :], in1=xt[:n, :], op=ALU.add),
                 ["mtmp", "xt"], ["mtmp"])
            dma("sp", d["x2"][r2:r2 + n, :], mtmp[:n, :], ["mtmp"], [DB(d["x2"])])

        def conv_state_out(src_view, ncols, dst, srcn="rawcur"):
            for g in range(3):
                bk, bn = bank()
                for cc in range(4):
                    pe(lambda: nc.tensor.transpose(bk[:ncols, cc * 128:(cc + 1) * 128], src_view(g * 4 + cc), idf[:, :]),
                       [srcn, ID], [bn], inc=(cc == 3))
                act(lambda: nc.scalar.copy(out=cvo[:ncols, :], in_=bk[:ncols, :]), [bn], ["cvo"])
                dma("sp", dst[:, g * 512:(g + 1) * 512], cvo[:ncols, :], ["cvo"], [DB(dst)])

        if do1:
            pool(lambda: nc.gpsimd.memset(Sf[:], 0.0), [], ["Sf"])
            if not FUS:
                dve(lambda: nc.vector.tensor_tensor(out=Sf[:, :, 128:256], in0=Sf[:, :, 128:256],
                                                    in1=idf[:, :].unsqueeze(1).to_broadcast([128, 4, 128]), op=ALU.add),
                    ["Sf", ID], ["Sf"])
            act(lambda: nc.scalar.copy(out=Sb_[:], in_=Sf[:]), ["Sf"], ["Sb"])
            pool(lambda: nc.gpsimd.memset(rawx[0][:], 0.0), [], ["rawcur"])
            pool(lambda: nc.gpsimd.memset(rawx[1][:], 0.0), [], ["rawcur"])

            NT = NMAIN // 128
            L = list(range(-(NPRE // 128) if FUS else -1, NT))

            def F1(ti):
                P = slot(ti)
                cur = ti % 2
                prv = 1 - cur
                k3 = ti % 4
                r0 = PRE0 + ti * 128
                yield from load_norm_T(r0, 128)
                pool(lambda: nc.gpsimd.tensor_copy(out=rawx[cur][:, :, 0:3], in_=rawx[prv][:, :, 128:131]),
                     ["rawcur"], ["rawcur"])

                def raw_dst(g, src, bn):
                    act(lambda: nc.scalar.copy(out=rawx[cur][:, g * 4:(g + 1) * 4, 3:131], in_=src), [bn], ["rawcur"])
                yield from proj_feat(P, 128, raw_dst, with_qs=(ti >= 0), with_q=(ti >= -1))
                if ti == NT - 1:
                    conv_state_out(lambda c: rawx[cur][:, c, 128:131], 3, d["conv_p"])
                if FUS or ti >= 0:
                    yield from conv_l2(P, 128, lambda j: rawx[cur][:, :, j:j + 128], None, full=(ti >= 0))
                yield from proj_tok(P, 0, 128, with_z=(ti >= 0))
                if ti >= 0:
                    dma("sp", d["z_s"][ti], zsb[:, :], ["zsb"], [DB(d["z_s"])])
                yield from proj_qs(P, 128, with_qs=(ti >= 0))
                if ti >= -1:
                    yield from proj_ksT(P, 128, ksT[k3], "ksT%d" % k3)
                    pool(lambda: nc.gpsimd.tensor_copy(out=vaug[k3][:, :, 0:64],
                                                       in_=P.misc[:, 128:256].rearrange("p (a b) -> p a b", a=2)),
                         [P.nm["misc"]], ["vaug%d" % k3])
                    yield
                if ti == NT - 1:
                    dma("sp", d["swak_p"], P.misc[:, 0:128], [P.nm["misc"]], [DB(d["swak_p"])])
                    dma("sp", d["swav_p"], P.misc[:, 128:256], [P.nm["misc"]], [DB(d["swav_p"])])

            def F2(ti):
                P = slot(ti)
                if FUS or ti >= 0:
                    yield from gdn_pre(P, 128, 0, 7, full=(ti >= 0))

            def Bk(ti):
                P = slot(ti)
                if not (FUS or ti >= 0):
                    return
                yield from gdn_scan(P, 128, DVW, state_only=(ti < 0))
                if ti < 0:
                    return
                dma("sp", d["oloc_s"][ti], oloc[:].rearrange("p a b -> p (a b)"), ["oloc"], [DB(d["oloc_s"])])
                if not FUS:
                    dma("sp", d["opt_s"][ti], oPT[:].rearrange("p a b -> p (a b)"), ["oPT"], [DB(d["opt_s"])])
                kc, kp = ti % 4, (ti - 1) % 4
                yield from swa(P, 128, 0, ksT[kp], "ksT%d" % kp, vaug[kp], "vaug%d" % kp, ksT[kc], "ksT%d" % kc,
                               vaug[kc], "vaug%d" % kc, bprev, "bprev", bcur, "bcur", swaT, "swaT", 0)
                dma("sp", d["swat_s"][ti], swaT[:].rearrange("p a b -> p (a b)"), ["swaT"], [DB(d["swat_s"])])
                if ti == 0:
                    dma("sp", bprev[:], d["bprev"].rearrange("p (h q) -> p h q", h=8), [], ["bprev"])

            fill_on[0] = True
            drain(F1(L[0]))
            interleave(F2(L[0]), F1(L[1]), pools=MIXP[0:3:2])
            for idx in range(len(L)):
                interleave(F2(L[idx + 1]) if idx + 1 < len(L) else None, Bk(L[idx]),
                           F1(L[idx + 2]) if idx + 2 < len(L) else None, weights=MIXW, pools=MIXP)

            fill_on[0] = False
            if FUS:
                dma("sp", d["gdn_p"].rearrange("h k v -> k h v"), Sf[:, :, 0:128], ["Sf"], [DB(d["gdn_p"])])
            else:
                dma("sp", d["cc_send"], Sf[:].rearrange("p a b -> p (a b)"), ["Sf"], [DB(d["cc_send"])])
            if part == "AB":
                S._deps("pool", [DB(d["cc_send"])], [DB(d["cc_recv"])])
                ccs = nc.alloc_semaphore("ccsem")
                cc = nc.gpsimd.collective_compute("AllGather", ALU.bypass, replica_groups=[list(range(NCORES))],
                                                  ins=[d["cc_send"]], outs=[d["cc_recv"]])
                cc.then_inc(ccs, 16)
                S.dsem.append(ccs); S.dval.append(16)
                DB(d["cc_recv"]).w = ("d", len(S.dsem) - 1, 16)

            _ck("main")
            P = PS[0]
            r0 = PRE0 + NMAIN
            drain(load_norm_T(r0, NSAMP))
            for g in range(3):
                dma("sp", cvo[:, :], d["state_conv"][:, g * 512:(g + 1) * 512], [], ["cvo"])
                bk, bn = bank()
                for cc_ in range(4):
                    pe(lambda: nc.tensor.transpose(bk[:, cc_ * 128:cc_ * 128 + 48], cvo[:48, cc_ * 128:(cc_ + 1) * 128], idf[:48, :48]),
                       ["cvo", ID], [bn], inc=(cc_ == 3))
                act(lambda: nc.scalar.copy(out=rawxs[:, g * 4:(g + 1) * 4, :, 0:3],
                                           in_=v3(bk, 128, 4)[:, :, 0:48].rearrange("p a (s r) -> p a s r", r=3)),
                    [bn], ["rawcur"])

            def raw_dst_s(g, src, bn):
                act(lambda: nc.scalar.copy(out=rawxs[:, g * 4:(g + 1) * 4, :, 3:7],
                                           in_=src.rearrange("p a (s t) -> p a s t", t=4)), [bn], ["rawcur"])
            drain(proj_feat(P, NSAMP, raw_dst_s))
            drain(proj_qs(P, NSAMP))
            ksTs = ksT[0]
            drain(proj_tok(P, 0, NSAMP, with_z=False))
            drain(proj_ksT(P, NSAMP, ksTs, "ksT0"))
            drain(conv_l2(P, NSAMP, lambda j: rawxs[:, :, :, j:j + 4], None))
            pool(lambda: nc.gpsimd.tensor_copy(out=ctmp[:, :, 0:48].rearrange("p c (s r) -> p c s r", r=3),
                                               in_=rawxs[:, :, :, 4:7]), ["rawcur"], ["ctmp"])
            conv_state_out(lambda c: ctmp[:, c, 0:48], 48, d["conv_s"], srcn="ctmp")
            _ck("sconv")
            zs2 = [(zsb, "zsb"), (PS[1].cacc[:, 0:4, :].rearrange("p a b -> p (a b)"), PS[1].nm["cacc"])]
            _sq = {}

            def sslot(sq_):
                if sq_ % 2 not in _sq:
                    Q = Slot()
                    Q.__dict__.update(PS[0].__dict__)
                    Q.nm = dict(PS[0].nm)
                    for nm_ in ("misc", "qdecT", "qkT", "kdec", "wT", "uaug", "glb"):
                        setattr(Q, nm_, getattr(PS[sq_ % 2], nm_))
                        Q.nm[nm_] = PS[sq_ % 2].nm[nm_]
                    _sq[sq_ % 2] = Q
                return _sq[sq_ % 2]

            def s_front(sq_):
                Q = sslot(sq_)
                zd, zn = zs2[sq_ % 2]
                yield from proj_tok(Q, sq_ * 4, 4, zdst=zd, zname=zn)
                yield from gdn_pre(Q, 4, sq_ * 4, 2)

            def s_back(sq_):
                Q = sslot(sq_)
                zd, zn = zs2[sq_ % 2]
                c0 = sq_ * 4
                dma("sp", Sf[:, :, 0:128], d["state_gdn"][sq_].rearrange("h k v -> k h v"), [], ["Sf"])
                act(lambda: nc.scalar.copy(out=Sb_[:, :, 0:128], in_=Sf[:, :, 0:128]), ["Sf"], ["Sb"])
                yield
                yield from gdn_scan(Q, 4, 128)
                dma("sp", d["gdn_s"][sq_].rearrange("h k v -> k h v"), Sf[:, :, 0:128], ["Sf"], [DB(d["gdn_s"])])
                gate_cols(4, oloc[:4], "oloc", zd[:4, :], zn, c0)
                yield
                dma("sp", kcf[:, :], d["cache_k"][sq_], [], ["kcf"])
                dma("sp", vcf[:, :], d["cache_v"][sq_], [], ["vcf"])
                bk, bn = bank()
                for kh in range(2):
                    pe(lambda: nc.tensor.transpose(bk[:64, kh * 128:(kh + 1) * 128], kcf[:, kh * 64:(kh + 1) * 64], idf[:, :]),
                       ["kcf", ID], [bn], inc=(kh == 1))
                act(lambda: nc.scalar.copy(out=ksT[1][:, :, :], in_=v3(bk, 64, 4)[:, 0:2, :]), [bn], ["ksT1"])
                yield
                pool(lambda: nc.gpsimd.tensor_copy(out=vaug[1][:, :, 0:64], in_=vcf[:, :].rearrange("p (a b) -> p a b", a=2)),
                     ["vcf"], ["vaug1"])
                pool(lambda: nc.gpsimd.tensor_copy(out=vaug[0][:4, :, 0:64],
                                                   in_=Q.misc[:4, 128:256].rearrange("p (a b) -> p a b", a=2)),
                     [Q.nm["misc"]], ["vaug0"])
                yield
                yield from swa(Q, 4, c0, ksT[1], "ksT1", vaug[1], "vaug1", ksT[0], "ksT0", vaug[0], "vaug0",
                               bsc, "bsc", bsn, "bsn", swaTs, "swaTs", c0)
                dma("sp", d["swak_s"][sq_, 0:124, :], d["cache_k"][sq_, 4:128, :], [], [DB(d["swak_s"])])
                dma("sp", d["swav_s"][sq_, 0:124, :], d["cache_v"][sq_, 4:128, :], [], [DB(d["swav_s"])])
                dma("sp", d["swak_s"][sq_, 124:128, :], Q.misc[:4, 0:128], [Q.nm["misc"]], [DB(d["swak_s"])])
                dma("sp", d["swav_s"][sq_, 124:128, :], Q.misc[:4, 128:256], [Q.nm["misc"]], [DB(d["swav_s"])])

            drain(s_front(0))
            for sq_ in range(16):
                interleave(s_back(sq_), s_front(sq_ + 1) if sq_ + 1 < 16 else None,
                           pools=((3, 4, 5, 6), (0, 1, 2)))
        S.barrier()
        esA.close()
        wog = sb("wog", [128, 4, D], BF16)
        dma("pool", wog[:], d["wout_g"].rearrange("p (c n) -> p c n", c=4), [], ["wog"])
        wos = sb("wos", [64, 8, D], BF16)
        dma("pool", wos[:], d["wout_s"].rearrange("p (c n) -> p c n", c=8), [], ["wos"])
        Pr = sb("Pr", [128, 4, 256]); PmT = sb("PmT", [128, 4, 128]); Sin = sb("Sin", [128, 4, 128])
        Sinb = sb("Sinb", [128, 4, 128], BF16); cand = sb("cand", [128, 4, 128])
        if do1:
            out_proj(NSAMP, swaTs, "swaTs", PRE0 + NMAIN, NMAIN)
        if not do2:
            return
        if not do1:
            dma("sp", d["x2"][NMAIN:NMAIN + NSAMP, :], d["x2in"][NMAIN:NMAIN + NSAMP, :], [], [DB(d["x2"])])

        if not FUS:
            pool(lambda: nc.gpsimd.memset(Sin[:], 0.0), [], ["Sin"])

        def apply_P(Pbuf, Pn):
            bk, bn = bank()
            for h in range(4):
                pe(lambda: nc.tensor.transpose(bk[:, h * 128:(h + 1) * 128], Pbuf[:, h, 128:256], idf[:, :]),
                   [Pn, ID], [bn], inc=(h == 3))
            act(lambda: nc.scalar.copy(out=PmT[:], in_=v3(bk, 128, 4)), [bn], ["PmT"])
            bk2, bn2 = bank()
            for h in range(4):
                pe(lambda: nc.tensor.matmul(bk2[:, h * 128:(h + 1) * 128], lhsT=PmT[:, h, :], rhs=Sin[:, h, :],
                                            start=True, stop=True), ["PmT", "Sin"], [bn2], inc=(h == 3))
            dve(lambda: nc.vector.tensor_tensor(out=cand[:], in0=v3(bk2, 128, 4), in1=Pbuf[:, :, 0:128], op=ALU.add),
                [bn2, Pn], ["cand"])

        for r in range(0 if FUS else NCORES):
            dma("sp", Pr[:].rearrange("p a b -> p (a b)"), d["cc_recv"][r * 128:(r + 1) * 128, :],
                [DB(d["cc_recv"])], ["Pr"])
            apply_P(Pr, "Pr")
            dve(lambda: nc.vector.tensor_tensor(out=cand[:], in0=cand[:], in1=Sin[:], op=ALU.subtract),
                ["cand", "Sin"], ["cand"])
            for h in range(4):
                dve(lambda: nc.vector.scalar_tensor_tensor(out=Sin[:, h, :], in0=cand[:, h, :], scalar=rmask[:, r:r + 1],
                                                           in1=Sin[:, h, :], op0=ALU.mult, op1=ALU.add),
                    ["cand", "Sin", "sm"], ["Sin"])
        if not FUS:
            act(lambda: nc.scalar.copy(out=Sinb[:], in_=Sin[:]), ["Sin"], ["Sinb"])
            dma("sp", Pr[:].rearrange("p a b -> p (a b)"), d["cc_send"], [DB(d["cc_send"])], ["Pr"])
            apply_P(Pr, "Pr")
            dma("sp", d["gdn_p"].rearrange("h k v -> k h v"), cand[:], ["cand"], [DB(d["gdn_p"])])

        for ti in range(NT):
            dma("sp", oloc[:].rearrange("p a b -> p (a b)"), d["oloc_s"][ti], [DB(d["oloc_s"])], ["oloc"])
            dma("sp", zsb[:, :], d["z_s"][ti], [DB(d["z_s"])], ["zsb"])
            dma("sp", swaT[:].rearrange("p a b -> p (a b)"), d["swat_s"][ti], [DB(d["swat_s"])], ["swaT"])
            if not FUS:
                dma("sp", oPT[:].rearrange("p a b -> p (a b)"), d["opt_s"][ti], [DB(d["opt_s"])], ["oPT"])
                bk, bn = bank()
                for h in range(4):
                    pe(lambda: nc.tensor.matmul(bk[:, h * 128:(h + 1) * 128], lhsT=oPT[:, h, :], rhs=Sinb[:, h, :],
                                                start=True, stop=True), ["oPT", "Sinb"], [bn], inc=(h == 3))
                dve(lambda: nc.vector.tensor_tensor(out=oloc[:], in0=oloc[:], in1=v3(bk, 128, 4), op=ALU.add),
                    ["oloc", bn], ["oloc"])
            gate_cols(128, oloc[:], "oloc", zsb[:, :], "zsb", 0)
            out_proj(128, swaT, "swaT", PRE0 + ti * 128, ti * 128)


def build_program(dbg=False, part="AB"):
    nc = bass.Bass("TRN2", target_bir_lowering=False)
    C = Ctx()
    C.nc = nc
    C.S = Sched(nc)
    S = C.S
    C.B_dram = {}
    A, Bp = part in ("A", "AB", "F"), part in ("B", "AB", "F")
    FUS = part == "F"
    NT1 = NTOKF if FUS else NTOK1

    def dt_(nm, shape, dt, kind):
        t = nc.dram_tensor(nm, list(shape), dt, kind=kind).ap()
        C.B_dram[nm] = Buf(nm)
        return t

    def din(nm, shape, dt=F32):
        return dt_(nm, shape, dt, "ExternalInput")

    def dout(nm, shape, cond=True):
        return dt_(nm, shape, F32, "ExternalOutput" if cond else "Internal")

    def dlink(nm, shape, dt=F32):
        kind = "Internal" if part in ("AB", "F") else ("ExternalOutput" if part == "A" else "ExternalInput")
        if dbg and part == "AB" and dt == F32:
            kind = "ExternalOutput"
        return dt_(nm, shape, dt, kind)

    xin = din("xin", [NT1, D])
    pin = din("pin", [NTOK2, PLE])
    wg1 = din("wg1", [NG, 128, 8 * GW]); wu1 = din("wu1", [NG, 128, 8 * GW]); wd1 = din("wd1", [128, NJ * D])
    wg2 = din("wg2", [NG, 128, 8 * GW]); wu2 = din("wu2", [NG, 128, 8 * GW]); wd2 = din("wd2", [128, NJ * D])
    gains = {k: din(k, [1, D]) for k in ("g1pre", "g1post", "gmpre", "gmpost", "g2pre", "g2post", "gple")}
    wpg = din("wpg", [128, 8 * D])
    wpp = din("wpp", [128, 2 * D])
    d = dict(gmpre=gains["gmpre"][0:1, :], gmpost=gains["gmpost"][0:1, :])
    d["win"] = din("win", [128, 8 * NPROJ])
    d["convw"] = din("convw", [128, 48])
    d["alog"] = din("alog", [1, 4])[0:1, :]; d["dtb"] = din("dtb", [1, 4])[0:1, :]
    d["gdnn"] = din("gdnn", [1, 128])[0:1, :]; d["sinks"] = din("sinks", [1, 8])[0:1, :]
    d["rmask"] = din("rmask", [1, 8])[0:1, :]
    d["wout_g"] = din("wout_g", [128, 4 * D]); d["wout_s"] = din("wout_s", [64, 8 * D])
    for k in ("bprev", "bcur", "bprev1"):
        d[k] = din(k, [128, 8 * 128])
    d["bs_cache"] = din("bs_cache", [128, 32]); d["bs_new"] = din("bs_new", [4, 32])
    d["cmask"] = din("cmask", [128, 4 * 128]); d["sel65"] = din("sel65", [65, 64])
    d["state_conv"] = din("state_conv", [48, 1536]); d["state_gdn"] = din("state_gdn", [16, 4, 128, 128])
    d["cache_k"] = din("cache_k", [16, 128, 128]); d["cache_v"] = din("cache_v", [16, 128, 128])
    y_out = dout("y", [NTOK2, D], Bp)
    d["gdn_p"] = dout("gdn_p", [4, 128, 128], Bp)
    d["conv_p"] = dout("conv_p", [3, 1536], A)
    d["swak_p"] = dout("swak_p", [128, 128], A); d["swav_p"] = dout("swav_p", [128, 128], A)
    d["conv_s"] = dout("conv_s", [48, 1536], A); d["gdn_s"] = dout("gdn_s", [16, 4, 128, 128], A)
    d["swak_s"] = dout("swak_s", [16, 128, 128], A); d["swav_s"] = dout("swav_s", [16, 128, 128], A)
    x1 = dlink("x1", [NT1, D])
    if part == "B":
        d["x2in"] = din("x2in", [NTOK2, D])
        x2 = dt_("x2", [NTOK2, D], F32, "Internal")
    elif part == "A":
        x2 = dt_("x2in", [NTOK2, D], F32, "ExternalOutput")
    else:
        x2 = dt_("x2", [NTOK2, D], F32, "ExternalOutput" if dbg else "Internal")
    d["x1"] = x1; d["x2"] = x2
    NT = NMAIN // 128
    d["oloc_s"] = dlink("oloc_s", [NT, 128, 512]); d["opt_s"] = dlink("opt_s", [NT, 128, 512], BF16)
    d["z_s"] = dlink("z_s", [NT, 128, 512]); d["swat_s"] = dlink("swat_s", [NT, 64, 1024], BF16)
    d["cc_send"] = dlink("cc_send", [128, 1024])
    d["cc_recv"] = dt_("cc_recv", [NCORES * 128, 1024], F32, "ExternalInput" if part == "B" else "Internal")

    C.idf = nc.alloc_sbuf_tensor("idf", [128, 128], F32)
    C.idb = nc.alloc_sbuf_tensor("idb", [128, 128], BF16)
    C.B_id = Buf("id")
    C.epsc = nc.alloc_sbuf_tensor("epsc", [128, 2], F32)
    S.op("pool", lambda: nc.gpsimd.memset(C.epsc[:, 0:1], EPS), writes=[C.B_id])
    S.op("pool", lambda: nc.gpsimd.memset(C.epsc[:, 1:2], 1.0), writes=[C.B_id])
    C.eps_ap = lambda n: C.epsc[:n, 0:1]
    S.op("pool", lambda: nc.gpsimd.memset(C.idf[:], 0.0), writes=[C.B_id])
    S.op("pool", lambda: nc.gpsimd.affine_select(out=C.idf[:], in_=C.idf[:], pattern=[[-1, 128]],
                                                 compare_op=ALU.not_equal, fill=1.0, base=0, channel_multiplier=1),
         reads=[C.B_id], writes=[C.B_id])
    S.op("dve", lambda: nc.vector.tensor_copy(out=C.idb[:], in_=C.idf[:]), reads=[C.B_id], writes=[C.B_id])

    outs = []
    if A:
        npm = (NPRE if FUS else NHALO) + NMAIN
        t1 = [(r0, r0, n) for (r0, n) in tiles_of(npm)] + [(npm, npm, NSAMP)]
        ffn_phase(C, "f1", xin, x1, t1, wg1, wu1, wd1, gains["g1pre"][0:1, :], gains["g1post"][0:1, :])
        outs += ["conv_p", "swak_p", "swav_p", "conv_s", "gdn_s", "swak_s", "swav_s"]
        if part == "A":
            outs += ["x1", "x2in", "oloc_s", "opt_s", "z_s", "swat_s", "cc_send"]
    mix_phase(C, d, part)
    if Bp:
        t2 = [(r0, r0, n) for (r0, n) in tiles_of(NMAIN)] + [(NMAIN, NMAIN, NSAMP)]
        ffn_phase(C, "f2", x2, y_out, t2, wg2, wu2, wd2, gains["g2pre"][0:1, :], gains["g2post"][0:1, :],
                  ple=dict(gain=gains["gple"][0:1, :], wg=wpg, wp=wpp, p=pin, prow=lambda d0: d0))
        outs += ["y", "gdn_p"]
    S.finish([C.B_dram[k] for k in outs])
    return nc, C


def _lay_gu(w):
    return np.ascontiguousarray(w.reshape(8, 128, NG, GW).transpose(2, 1, 0, 3).reshape(NG, 128, 8 * GW))


def _lay_rows(w, nk):
    n = w.shape[1]
    return np.ascontiguousarray(w.reshape(nk, 128, n).transpose(1, 0, 2).reshape(128, nk * n))


def _bucket_table():
    dd = np.arange(128)
    lr = np.log(np.maximum(dd, 1).astype(np.float32) / np.float32(16)) / np.float32(np.log(128 / 16))
    large = np.minimum(16 + (lr.astype(np.float32) * np.float32(16)).astype(np.int32), 31)
    return np.where(dd < 16, dd, large)


def _bias_tables(rel_bias):
    bt = _bucket_table()
    bv = rel_bias[bt, :]
    k = np.arange(128)[:, None]; q = np.arange(128)[None, :]
    dcur = q - k
    dprev = 128 + q - k
    bcur = np.full((128, 8, 128), NEG, np.float32); bprev = np.full((128, 8, 128), NEG, np.float32)
    for h in range(8):
        t = bv[np.clip(dcur, 0, 127), h]
        bcur[:, h, :] = np.where(dcur >= 0, t, NEG)
        t = bv[np.clip(dprev, 0, 127), h]
        bprev[:, h, :] = np.where(dprev < 128, t, NEG)
    j = np.arange(128)[:, None]; t4 = np.arange(4)[None, :]
    dc = 128 + t4 - j
    bsc = np.full((128, 8, 4), NEG, np.float32)
    kk = np.arange(4)[:, None]
    dn = t4 - kk
    bsn = np.full((4, 8, 4), NEG, np.float32)
    for h in range(8):
        bsc[:, h, :] = np.where(dc < 128, bv[np.clip(dc, 0, 127), h], NEG)
        bsn[:, h, :] = np.where(dn >= 0, bv[np.clip(dn, 0, 127), h], NEG)
    return bprev.reshape(128, -1), bcur.reshape(128, -1), bsc.reshape(128, -1), bsn.reshape(4, -1)


def make_in_maps(inp, fused=False):
    f = lambda a: np.ascontiguousarray(np.asarray(a, dtype=np.float32))
    w_in = f(inp["w_in"][0])
    perm = np.concatenate([np.arange(0, 1536), np.arange(1536, 2048), np.arange(2056, 2568), np.arange(2568, 2696),
                           np.arange(2696, 2824), np.arange(2048, 2052), np.arange(2052, 2056)])
    w_out = f(inp["w_out"][0])
    conv_w = f(inp["conv_w"][0])
    bprev, bcur, bsc, bsn = _bias_tables(f(inp["rel_bias"]))
    ii = np.arange(128)
    tri = (ii[:, None] <= ii[None, :]).astype(np.float32)
    cmask = np.stack([tri, tri.T, (ii[:, None] > ii[None, :]).astype(np.float32), np.ones((128, 128), np.float32)], 1)
    sel65 = np.zeros((65, 64), np.float32); sel65[64, :] = 1.0
    shared = {
        "wg1": _lay_gu(f(inp["ffn1_w_gate"][0])), "wu1": _lay_gu(f(inp["ffn1_w_up"][0])),
        "wd1": _lay_rows(f(inp["ffn1_w_down"][0]), NJ),
        "wg2": _lay_gu(f(inp["ffn2_w_gate"][0])), "wu2": _lay_gu(f(inp["ffn2_w_up"][0])),
        "wd2": _lay_rows(f(inp["ffn2_w_down"][0]), NJ),
        "g1pre": f(inp["norm_ffn1_pre"]), "g1post": f(inp["norm_ffn1_post"]),
        "gmpre": f(inp["norm_mix_pre"]), "gmpost": f(inp["norm_mix_post"]),
        "g2pre": f(inp["norm_ffn2_pre"]), "g2post": f(inp["norm_ffn2_post"]),
        "gple": f(inp["norm_ple_post"]),
        "wpg": _lay_rows(f(inp["ple_gate"][0]), 8), "wpp": _lay_rows(f(inp["ple_proj"][0]), 2),
        "win": _lay_rows(np.ascontiguousarray(w_in[:, perm]), 8),
        "convw": np.ascontiguousarray(conv_w.T.reshape(12, 128, 4).transpose(1, 0, 2).reshape(128, 48)),
        "alog": f(inp["gdn_a_log"]), "dtb": f(inp["gdn_dt_bias"]), "gdnn": f(inp["gdn_norm"]),
        "sinks": f(inp["swa_sinks"]),
        "wout_g": _lay_rows(w_out[:512], 4),
        "wout_s": np.ascontiguousarray(w_out[512:].reshape(8, 64, D).transpose(1, 0, 2).reshape(64, 8 * D)),
        "bprev": bprev, "bcur": bcur, "bs_cache": bsc, "bs_new": bsn,
        "cmask": np.ascontiguousarray(cmask.reshape(128, 512)), "sel65": sel65,
    }
    xp = f(inp["x_prompt"]); xs = f(inp["x_sample"]).reshape(-1, D)
    pp = f(inp["p_prompt"][0]); psm = f(inp["p_sample"][0]).reshape(-1, PLE)
    sconv = f(inp["state_conv"][0]); sgdn = f(inp["state_gdn"][0])
    ck = f(inp["cache_swa_k"][0]).reshape(128, 128, 128); cv = f(inp["cache_swa_v"][0]).reshape(128, 128, 128)
    maps = []
    for c in range(NCORES):
        b, q = c // 4, c % 4
        t0 = q * NMAIN
        if fused:
            halo = np.zeros((NPRE, D), np.float32)
            if t0 > 0:
                halo[NPRE - t0:] = xp[b, 0:t0]
        else:
            halo = xp[b, t0 - NHALO:t0] if q > 0 else np.zeros((NHALO, D), np.float32)
        m = dict(shared)
        m["xin"] = np.ascontiguousarray(np.concatenate([halo, xp[b, t0:t0 + NMAIN], xs[c * NSAMP:(c + 1) * NSAMP]], 0))
        m["pin"] = np.ascontiguousarray(np.concatenate([pp[b, t0:t0 + NMAIN], psm[c * NSAMP:(c + 1) * NSAMP]], 0))
        m["bprev1"] = bprev if q > 0 else np.full_like(bprev, NEG)
        rm = np.zeros((1, 8), np.float32)
        for r in range(NCORES):
            if r // 4 == b and r % 4 < q:
                rm[0, r] = 1.0
        m["rmask"] = rm
        m["state_conv"] = np.ascontiguousarray(sconv[c * 16:(c + 1) * 16].reshape(48, 1536))
        m["state_gdn"] = np.ascontiguousarray(sgdn[c * 16:(c + 1) * 16])
        m["cache_k"] = np.ascontiguousarray(ck[c * 16:(c + 1) * 16]); m["cache_v"] = np.ascontiguousarray(cv[c * 16:(c + 1) * 16])
        maps.append(m)
    return maps


def assemble(R):
    yp = np.zeros((2, 8192, D), np.float32); ys = np.zeros((128, 4, D), np.float32)
    conv_p = np.zeros((1, 2, 3, 1536), np.float32); gdn_p = np.zeros((1, 2, 4, 128, 128), np.float32)
    kp = np.zeros((1, 2, 128, 2, 64), np.float32); vp = np.zeros((1, 2, 128, 2, 64), np.float32)
    conv_s = np.zeros((1, 128, 3, 1536), np.float32); gdn_s = np.zeros((1, 128, 4, 128, 128), np.float32)
    ks = np.zeros((1, 128, 128, 2, 64), np.float32); vs = np.zeros((1, 128, 128, 2, 64), np.float32)
    for c in range(NCORES):
        b, q = c // 4, c % 4
        r = R[c]
        yp[b, q * NMAIN:(q + 1) * NMAIN] = r["y"][:NMAIN]
        ys[c * 16:(c + 1) * 16] = r["y"][NMAIN:].reshape(16, 4, D)
        if q == 3:
            conv_p[0, b] = r["conv_p"]; gdn_p[0, b] = r["gdn_p"]
            kp[0, b] = r["swak_p"].reshape(128, 2, 64); vp[0, b] = r["swav_p"].reshape(128, 2, 64)
        conv_s[0, c * 16:(c + 1) * 16] = r["conv_s"].reshape(16, 3, 1536)
        gdn_s[0, c * 16:(c + 1) * 16] = r["gdn_s"]
        ks[0, c * 16:(c + 1) * 16] = r["swak_s"].reshape(16, 128, 2, 64)
        vs[0, c * 16:(c + 1) * 16] = r["swav_s"].reshape(16, 128, 2, 64)
    return (yp, ys, conv_p, gdn_p, kp, vp, conv_s, gdn_s, ks, vs)


_CACHE = {}
A_KEYS = ("x1", "x2in", "oloc_s", "opt_s", "z_s", "swat_s", "cc_send")


def kernel(**inputs):
    if "F" not in _CACHE:
        _CACHE["F"] = build_program(False, "F")[0]
    maps = make_in_maps(inputs, fused=True)
    res = run_bass_kernel_spmd(_CACHE["F"], maps, core_ids=list(range(NCORES)))
    return assemble(res.results)
```

```python
import contextlib
import numpy as np
import concourse.bass as bass
import concourse.mybir as mybir
from concourse.bass_utils import run_bass_kernel_spmd

F32 = mybir.dt.float32
BF16 = mybir.dt.bfloat16
AF = mybir.ActivationFunctionType
ALU = mybir.AluOpType
AX = mybir.AxisListType

NCORES = 8
D = 1024
FF = 2816
NJ = FF // 128
PLE = 256
EPS = 1e-6
NMAIN = 2048
NHALO = 128
NSAMP = 64
NTOK1 = NHALO + NMAIN + NSAMP
NPRE = 6144
NTOKF = NPRE + NMAIN + NSAMP
NTOK2 = NMAIN + NSAMP
GW = 256
NG = FF // GW


class Buf:
    __slots__ = ("name", "w", "rs")

    def __init__(self, name):
        self.name = name
        self.w = None
        self.rs = []


class Sched:
    def __init__(self, nc, n_dma_sems=24, same_engine_sync=True):
        self.nc = nc
        self.eng = {"pe": nc.tensor, "act": nc.scalar, "dve": nc.vector,
                    "pool": nc.gpsimd, "sp": nc.sync}
        self.sem = {k: nc.alloc_semaphore("cs_" + k) for k in ("pe", "act", "dve", "pool")}
        self.cnt = {k: 0 for k in self.sem}
        self.seen = {}
        self.dsem = [nc.alloc_semaphore("ds%d" % i) for i in range(n_dma_sems)]
        self.dval = [0] * n_dma_sems
        self.dpool = {"sp": list(range(0, n_dma_sems - 8)), "pool": list(range(n_dma_sems - 8, n_dma_sems))}
        self.dnext = {"sp": 0, "pool": 0}
        self.ses = same_engine_sync
        self.pe_pending = []
        self.n_inst = 0

    def _wait(self, on, ev):
        if ev is None:
            return
        if ev[0] == "c":
            _, e, v = ev
            if e == on and (not self.ses or e == "pe"):
                return
            key = (on, e)
            if self.seen.get(key, 0) >= v:
                return
            self.seen[key] = v
            self.eng[on].wait_ge(self.sem[e], v)
        else:
            _, i, v = ev
            key = (on, "d", i)
            if self.seen.get(key, 0) >= v:
                return
            self.seen[key] = v
            self.eng[on].wait_ge(self.dsem[i], v)

    def _deps(self, on, reads, writes):
        for b in reads:
            self._wait(on, b.w)
        for b in writes:
            self._wait(on, b.w)
            for r in b.rs:
                self._wait(on, r)

    @staticmethod
    def _compact(rs):
        best = {}
        for ev in rs:
            k = ev[:2]
            if k not in best or best[k][2] < ev[2]:
                best[k] = ev
        return list(best.values())

    def _record(self, ev, reads, writes):
        for b in reads:
            b.rs.append(ev)
            if len(b.rs) > 12:
                b.rs = self._compact(b.rs)
        for b in writes:
            b.w = ev
            b.rs = []

    def op(self, on, fn, reads=(), writes=(), inc=True):
        self._deps(on, reads, writes)
        ins = fn()
        self.n_inst += 1
        if on == "pe" and not inc:
            self.pe_pending.append((tuple(reads), tuple(writes)))
            return ins
        self.cnt[on] += 1
        ins.then_inc(self.sem[on], 1)
        ev = ("c", on, self.cnt[on])
        groups = [(tuple(reads), tuple(writes))]
        if on == "pe":
            groups += self.pe_pending
            self.pe_pending = []
        for rd, wr in groups:
            self._record(ev, rd, wr)
        return ins

    def dma(self, on, out, in_, reads=(), writes=(), **kw):
        pl = self.dpool[on]
        i = pl[self.dnext[on] % len(pl)]
        self.dnext[on] += 1
        if self.dval[i] > 0:
            self._wait(on, ("d", i, self.dval[i]))
        self._deps(on, reads, writes)
        ins = self.eng[on].dma_start(out=out, in_=in_, **kw)
        self.n_inst += 1
        self.dval[i] += 16
        ins.then_inc(self.dsem[i], 16)
        self._record(("d", i, self.dval[i]), reads, writes)
        return ins

    def barrier(self):
        for on in ("pe", "act", "dve", "pool", "sp"):
            for e in ("pe", "act", "dve", "pool"):
                if e != on and self.cnt[e] > 0:
                    self._wait(on, ("c", e, self.cnt[e]))
            for i, v in enumerate(self.dval):
                if v > 0:
                    self._wait(on, ("d", i, v))

    def finish(self, bufs):
        for i, v in enumerate(self.dval):
            if v > 0:
                self._wait("sp", ("d", i, v))
        for b in bufs:
            self._wait("sp", b.w)


class Ctx:
    pass


def tiles_of(n_rows):
    out = []
    r = 0
    while r < n_rows:
        n = min(128, n_rows - r)
        out.append((r, n))
        r += n
    return out


def blocks_of(lo, hi, maxw=512):
    out = []
    while lo < hi:
        n = min(maxw, hi - lo)
        out.append((lo, n))
        lo += n
    return out


def rstd_ops(C, ssq_ap, out_ap, n, dim, B_stat):
    S, nc = C.S, C.nc
    S.op("act", lambda: nc.scalar.activation(out=out_ap, in_=ssq_ap, func=AF.Ln, scale=1.0 / dim, bias=C.eps_ap(n)),
         reads=[B_stat, C.B_id], writes=[B_stat])
    S.op("act", lambda: nc.scalar.activation(out=out_ap, in_=out_ap, func=AF.Exp, scale=-0.5),
         reads=[B_stat], writes=[B_stat])


def ffn_phase(C, name, x_src, x_dst, tiles, wg_d, wu_d, wd_d, gpre_d, gpost_d, ple=None):
    S, nc = C.S, C.nc
    S.barrier()
    with contextlib.ExitStack() as es:
        def sb(nm, shape, dt):
            return es.enter_context(nc.sbuf_tensor(name + "_" + nm, shape, dt))

        def ps(nm, shape, dt):
            return es.enter_context(nc.psum_tensor(name + "_" + nm, shape, dt))

        ntl = len(tiles)
        if ntl <= 18:
            halves = [tiles[: (ntl + 1) // 2], tiles[(ntl + 1) // 2:]]
        else:
            halves = [tiles[i:i + FFN_PART] for i in range(0, ntl, FFN_PART)]
            if len(halves[-1]) < 4:
                halves[-2] = halves[-2] + halves[-1]
                halves.pop()
        maxtok = max(sum(t[2] for t in h) for h in halves)
        hT = sb("hT", [128, 8, maxtok], BF16)
        aT = sb("aT", [128, NJ, maxtok], BF16)
        wd = sb("wd", [128, NJ, D], BF16)
        wg = [sb("wg%d" % i, [128, 8, GW], BF16) for i in range(2)]
        wu = [sb("wu%d" % i, [128, 8, GW], BF16) for i in range(2)]
        xin = [sb("xin%d" % i, [128, D], F32) for i in range(2)]
        hb = [sb("hb%d" % i, [128, D], BF16) for i in range(2)]
        yo = [sb("yo%d" % i, [128, D], F32) for i in range(2)]
        tmp = [sb("tmp%d" % i, [128, D], F32) for i in range(2)]
        junk = sb("junk", [128, D], BF16)
        gpre = sb("gpre", [128, D], F32)
        gpost = sb("gpost", [128, D], F32)
        sg = [sb("sg%d" % i, [128, 512], F32) for i in range(2)]
        stat = sb("stat", [128, 64], F32)
        pT = [ps("pT%d" % i, [128, 8, 128], BF16) for i in range(2)]
        pg = [ps("pg%d" % i, [128, 512], F32) for i in range(2)]
        pu = [ps("pu%d" % i, [128, 512], F32) for i in range(2)]
        po = ps("po", [128, D], F32)

        B = {}
        for k in ["hT", "aT", "wd", "junk", "gpre", "gpost", "stat", "po"]:
            B[k] = Buf(name + k)
        for k in ["wg", "wu", "xin", "hb", "yo", "tmp", "sg", "pT", "pg", "pu"]:
            for i in range(2):
                B[k, i] = Buf(name + k + str(i))
        if ple is not None:
            gple = sb("gple", [128, D], F32)
            wpg = sb("wpg", [128, 8, D], BF16)
            wpp = sb("wpp", [128, 2, D], BF16)
            pin = [sb("pin%d" % i, [128, PLE], F32) for i in range(2)]
            pb = [sb("pb%d" % i, [128, PLE], BF16) for i in range(2)]
            xT3 = sb("xT3", [128, 8, 128], BF16)
            pT3 = sb("pT3", [128, 2, 128], BF16)
            prod = sb("prod", [128, D], F32)
            for k in ["gple", "wpg", "wpp", "xT3", "pT3", "prod"]:
                B[k] = Buf(name + k)
            for i in range(2):
                B["pin", i] = Buf(name + "pin%d" % i)
                B["pb", i] = Buf(name + "pb%d" % i)

        S.dma("sp", gpre[:], gpre_d.partition_broadcast(128), writes=[B["gpre"]])
        S.dma("sp", gpost[:], gpost_d.partition_broadcast(128), writes=[B["gpost"]])
        S.op("pool", lambda: nc.gpsimd.memset(stat[:], 0.0), writes=[B["stat"]])
        S.op("pool", lambda: nc.gpsimd.tensor_scalar(out=gpost[:], in0=gpost[:], scalar1=0.5, scalar2=None,
                                                     op0=ALU.mult), reads=[B["gpost"]], writes=[B["gpost"]])
        if ple is not None:
            S.dma("sp", gple[:], ple["gain"].partition_broadcast(128), writes=[B["gple"]])
            S.dma("pool", wpg[:], ple["wg"].rearrange("p (k n) -> p k n", k=8), writes=[B["wpg"]])
            S.dma("pool", wpp[:], ple["wp"].rearrange("p (k n) -> p k n", k=2), writes=[B["wpp"]])

        wd_loaded = False
        tcount = 0
        gcount = 0
        bcount = 0
        for half in halves:
            if not half:
                continue
            col = 0
            cols = []
            for (r0, d0, n) in half:
                s = tcount % 2
                st = 8 * (tcount % 8)
                S.dma("sp", xin[s][:n, :], x_src[r0:r0 + n, :], writes=[B["xin", s]])
                S.op("pool", lambda: nc.gpsimd.memset(stat[:, st:st + 8], 0.0), writes=[B["stat"]])
                S.op("act", lambda: nc.scalar.activation(out=junk[:n, :], in_=xin[s][:n, :], func=AF.Square,
                                                         accum_out=stat[:n, st:st + 1]),
                     reads=[B["xin", s], B["stat"]], writes=[B["junk"], B["stat"]])
                rstd_ops(C, stat[:n, st:st + 1], stat[:n, st + 1:st + 2], n, D, B["stat"])
                S.op("dve", lambda: nc.vector.scalar_tensor_tensor(
                    out=hb[s][:n, :], in0=xin[s][:n, :], scalar=stat[:n, st + 1:st + 2], in1=gpre[:n, :],
                    op0=ALU.mult, op1=ALU.mult), reads=[B["xin", s], B["stat"], B["gpre"]], writes=[B["hb", s]])
                for k in range(8):
                    S.op("pe", lambda: nc.tensor.transpose(pT[s][:, k, :n], hb[s][:n, k * 128:(k + 1) * 128],
                                                           C.idb[:n, :n]),
                         reads=[B["hb", s], C.B_id], writes=[B["pT", s]], inc=(k == 7))
                S.op("act", lambda: nc.scalar.copy(out=hT[:, :, col:col + n], in_=pT[s][:, :, :n]),
                     reads=[B["pT", s]], writes=[B["hT"]])
                cols.append(col)
                col += n
                tcount += 1
            ntok = col
            tblocks = blocks_of(0, ntok)
            for g in range(NG):
                s = gcount % 2
                gcount += 1
                S.dma("pool", wg[s][:], wg_d[g].rearrange("p (k c) -> p k c", k=8), writes=[B["wg", s]])
                S.dma("pool", wu[s][:], wu_d[g].rearrange("p (k c) -> p k c", k=8), writes=[B["wu", s]])
                if not wd_loaded and g == 2:
                    for q in range(2):
                        S.dma("pool", wd[:, q * 11:(q + 1) * 11, :],
                              wd_d[:, q * 11 * D:(q + 1) * 11 * D].rearrange("p (j n) -> p j n", j=11),
                              writes=[B["wd"]])
                    wd_loaded = True
                for jj in range(GW // 128):
                    j = g * (GW // 128) + jj
                    for (b0, bn) in tblocks:
                        bs = bcount % 2
                        bcount += 1
                        for k in range(8):
                            S.op("pe", lambda: nc.tensor.matmul(pg[bs][:, :bn], lhsT=wg[s][:, k, jj * 128:(jj + 1) * 128],
                                                                rhs=hT[:, k, b0:b0 + bn], start=(k == 0), stop=(k == 7)),
                                 reads=[B["wg", s], B["hT"]], writes=[B["pg", bs]], inc=(k == 7))
                        for k in range(8):
                            S.op("pe", lambda: nc.tensor.matmul(pu[bs][:, :bn], lhsT=wu[s][:, k, jj * 128:(jj + 1) * 128],
                                                                rhs=hT[:, k, b0:b0 + bn], start=(k == 0), stop=(k == 7)),
                                 reads=[B["wu", s], B["hT"]], writes=[B["pu", bs]], inc=(k == 7))
                        S.op("act", lambda: nc.scalar.activation(out=sg[bs][:, :bn], in_=pg[bs][:, :bn], func=AF.Silu),
                             reads=[B["pg", bs]], writes=[B["sg", bs]])
                        S.op("dve", lambda: nc.vector.tensor_tensor(out=aT[:, j, b0:b0 + bn], in0=sg[bs][:, :bn],
                                                                    in1=pu[bs][:, :bn], op=ALU.mult),
                             reads=[B["sg", bs], B["pu", bs]], writes=[B["aT"]])
            for ti, (r0, d0, n) in enumerate(half):
                c0 = cols[ti]
                s = tcount % 2
                st = 8 * (tcount % 8)
                tcount += 1
                S.dma("sp", xin[s][:n, :], x_src[r0:r0 + n, :], writes=[B["xin", s]])
                S.op("pool", lambda: nc.gpsimd.memset(stat[:, st:st + 8], 0.0), writes=[B["stat"]])
                use_b = (ple is None) and (ti % 2 == 1)
                for nh in range(2):
                    dst = pg[nh][:n, :] if use_b else po[:n, nh * 512:(nh + 1) * 512]
                    wb = B["pg", nh] if use_b else B["po"]
                    for j in range(NJ):
                        S.op("pe", lambda: nc.tensor.matmul(dst, lhsT=aT[:, j, c0:c0 + n],
                                                            rhs=wd[:, j, nh * 512:(nh + 1) * 512],
                                                            start=(j == 0), stop=(j == NJ - 1)),
                             reads=[B["aT"], B["wd"]], writes=[wb], inc=(j == NJ - 1 and (nh == 1 or use_b)))
                if use_b:
                    for nh in range(2):
                        S.op("act", lambda: nc.scalar.activation(out=junk[:n, nh * 512:(nh + 1) * 512], in_=pg[nh][:n, :],
                                                                 func=AF.Square, accum_out=stat[:n, st + 2 + nh:st + 3 + nh]),
                             reads=[B["pg", nh], B["stat"]], writes=[B["junk"], B["stat"]])
                    S.op("dve", lambda: nc.vector.tensor_tensor(out=stat[:n, st:st + 1], in0=stat[:n, st + 2:st + 3],
                                                                in1=stat[:n, st + 3:st + 4], op=ALU.add),
                         reads=[B["stat"]], writes=[B["stat"]])
                    rstd_ops(C, stat[:n, st:st + 1], stat[:n, st + 1:st + 2], n, D, B["stat"])
                    for nh in range(2):
                        S.op("dve", lambda: nc.vector.scalar_tensor_tensor(
                            out=tmp[s][:n, nh * 512:(nh + 1) * 512], in0=pg[nh][:n, :], scalar=stat[:n, st + 1:st + 2],
                            in1=gpost[:n, nh * 512:(nh + 1) * 512], op0=ALU.mult, op1=ALU.mult),
                            reads=[B["pg", nh], B["stat"], B["gpost"]], writes=[B["tmp", s]])
                else:
                    S.op("act", lambda: nc.scalar.activation(out=junk[:n, :], in_=po[:n, :], func=AF.Square,
                                                             accum_out=stat[:n, st:st + 1]),
                         reads=[B["po"], B["stat"]], writes=[B["junk"], B["stat"]])
                    rstd_ops(C, stat[:n, st:st + 1], stat[:n, st + 1:st + 2], n, D, B["stat"])
                    S.op("dve", lambda: nc.vector.scalar_tensor_tensor(
                        out=tmp[s][:n, :], in0=po[:n, :], scalar=stat[:n, st + 1:st + 2], in1=gpost[:n, :],
                        op0=ALU.mult, op1=ALU.mult), reads=[B["po"], B["stat"], B["gpost"]], writes=[B["tmp", s]])
                S.op("pool", lambda: nc.gpsimd.tensor_tensor(out=yo[s][:n, :], in0=tmp[s][:n, :], in1=xin[s][:n, :],
                                                             op=ALU.add),
                     reads=[B["tmp", s], B["xin", s]], writes=[B["yo", s]])
                if ple is None:
                    S.dma("sp", x_dst[d0:d0 + n, :], yo[s][:n, :], reads=[B["yo", s]], writes=[C.B_dram[x_dst.tensor.name]])
                    continue
                pr0 = ple["prow"](d0)
                S.dma("sp", pin[s][:n, :], ple["p"][pr0:pr0 + n, :], writes=[B["pin", s]])
                S.op("act", lambda: nc.scalar.copy(out=hb[s][:n, :], in_=yo[s][:n, :]),
                     reads=[B["yo", s]], writes=[B["hb", s]])
                S.op("pool", lambda: nc.gpsimd.tensor_copy(out=pb[s][:n, :], in_=pin[s][:n, :]),
                     reads=[B["pin", s]], writes=[B["pb", s]])
                for k in range(8):
                    S.op("pe", lambda: nc.tensor.transpose(pT[0][:, k, :n], hb[s][:n, k * 128:(k + 1) * 128],
                                                           C.idb[:n, :n]),
                         reads=[B["hb", s], C.B_id], writes=[B["pT", 0]], inc=(k == 7))
                S.op("dve", lambda: nc.vector.tensor_copy(out=xT3[:, :, :n], in_=pT[0][:, :, :n]),
                     reads=[B["pT", 0]], writes=[B["xT3"]])
                for k in range(2):
                    S.op("pe", lambda: nc.tensor.transpose(pT[1][:, k, :n], pb[s][:n, k * 128:(k + 1) * 128],
                                                           C.idb[:n, :n]),
                         reads=[B["pb", s], C.B_id], writes=[B["pT", 1]], inc=(k == 1))
                S.op("dve", lambda: nc.vector.tensor_copy(out=pT3[:, :, :n], in_=pT[1][:, 0:2, :n]),
                     reads=[B["pT", 1]], writes=[B["pT3"]])
                for nh in range(2):
                    for k in range(8):
                        S.op("pe", lambda: nc.tensor.matmul(pg[nh][:n, :], lhsT=xT3[:, k, :n],
                                                            rhs=wpg[:, k, nh * 512:(nh + 1) * 512],
                                                            start=(k == 0), stop=(k == 7)),
                             reads=[B["xT3"], B["wpg"]], writes=[B["pg", nh]], inc=(k == 7))
                    for k in range(2):
                        S.op("pe", lambda: nc.tensor.matmul(pu[nh][:n, :], lhsT=pT3[:, k, :n],
                                                            rhs=wpp[:, k, nh * 512:(nh + 1) * 512],
                                                            start=(k == 0), stop=(k == 1)),
                             reads=[B["pT3"], B["wpp"]], writes=[B["pu", nh]], inc=(k == 1))
                    S.op("act", lambda: nc.scalar.activation(out=sg[nh][:n, :], in_=pg[nh][:n, :],
                                                             func=AF.Sigmoid),
                         reads=[B["pg", nh]], writes=[B["sg", nh]])
                    S.op("dve", lambda: nc.vector.tensor_tensor(out=prod[:n, nh * 512:(nh + 1) * 512],
                                                                in0=sg[nh][:n, :],
                                                                in1=pu[nh][:n, :], op=ALU.mult),
                         reads=[B["sg", nh], B["pu", nh]], writes=[B["prod"]])
                S.op("act", lambda: nc.scalar.activation(out=junk[:n, :], in_=prod[:n, :], func=AF.Square,
                                                         accum_out=stat[:n, st + 2:st + 3]),
                     reads=[B["prod"], B["stat"]], writes=[B["junk"], B["stat"]])
                rstd_ops(C, stat[:n, st + 2:st + 3], stat[:n, st + 3:st + 4], n, D, B["stat"])
                S.op("dve", lambda: nc.vector.scalar_tensor_tensor(
                    out=tmp[s][:n, :], in0=prod[:n, :], scalar=stat[:n, st + 3:st + 4], in1=gple[:n, :],
                    op0=ALU.mult, op1=ALU.mult), reads=[B["prod"], B["stat"], B["gple"]], writes=[B["tmp", s]])
                S.op("pool", lambda: nc.gpsimd.tensor_tensor(out=tmp[s][:n, :], in0=tmp[s][:n, :], in1=yo[s][:n, :],
                                                             op=ALU.add),
                     reads=[B["tmp", s], B["yo", s]], writes=[B["tmp", s]])
                S.dma("sp", x_dst[d0:d0 + n, :], tmp[s][:n, :], reads=[B["tmp", s]], writes=[C.B_dram[x_dst.tensor.name]])


O_Z, O_QS, O_KS, O_VS, O_B, O_A, NPROJ = 1536, 2048, 2560, 2688, 2816, 2820, 2824
NEG = -30000.0


DBG_STOP = None
MIXW = (3, 1, 1)
FFN_PART = 12
MIXP = ((0, 1, 2), (3, 4), (5, 6))
FILLER = False


class _Stop(Exception):
    pass


def _ck(tag):
    if DBG_STOP == tag:
        raise _Stop()


def mix_phase(C, d, part="AB"):
    try:
        _mix_phase(C, d, part)
    except _Stop:
        pass


def _mix_phase(C, d, part="AB"):
    S, nc = C.S, C.nc
    do1 = part in ("A", "AB", "F")
    do2 = part in ("B", "AB", "F")
    FUS = part == "F"
    PRE0 = NPRE if FUS else NHALO
    DVW = 128 if FUS else 256
    NT = NMAIN // 128
    S.barrier()
    with contextlib.ExitStack() as es:
        pre = {}
        for nm_, shape_, dt_ in [("cm", [128, 4, 128], F32), ("gmpre", [128, D], F32), ("gmpost", [128, D], F32),
                                 ("gdnn", [128, 128], F32), ("sm", [128, 32], F32), ("xt", [128, D], F32),
                                 ("junk", [128, D], BF16), ("st", [128, 64], F32), ("zsb", [128, 512], F32),
                                 ("oloc", [128, 4, 128], F32), ("oPT", [128, 4, 128], BF16),
                                 ("swaT", [64, 8, 128], BF16), ("swaTs", [64, 8, 64], BF16),
                                 ("og", [128, 4, 128], F32), ("og2", [128, 4, 128], F32), ("sz", [128, 512], F32),
                                 ("ogb", [128, 512], BF16), ("mixT", [128, 4, 128], BF16), ("mtmp", [128, D], F32)]:
            pre[nm_] = es.enter_context(nc.sbuf_tensor("m_" + nm_, shape_, dt_))
        esA = es.enter_context(contextlib.ExitStack())

        def sb(nm, shape, dt=F32):
            if nm in pre:
                return pre[nm]
            return es.enter_context(nc.sbuf_tensor("m_" + nm, shape, dt))

        def sbA(nm, shape, dt=F32):
            return esA.enter_context(nc.sbuf_tensor("m_" + nm, shape, dt))

        def ps(nm, shape, dt=F32):
            return es.enter_context(nc.psum_tensor("m_" + nm, shape, dt))

        Bd = {}

        def B(k):
            if k not in Bd:
                Bd[k] = Buf("m" + str(k))
            return Bd[k]

        def DB(ap):
            return C.B_dram[ap.tensor.name]

        def dve(fn, r, w):
            return S.op("dve", fn, [B(x) if not isinstance(x, Buf) else x for x in r],
                        [B(x) if not isinstance(x, Buf) else x for x in w])

        def act(fn, r, w):
            return S.op("act", fn, [B(x) if not isinstance(x, Buf) else x for x in r],
                        [B(x) if not isinstance(x, Buf) else x for x in w])

        def pool(fn, r, w):
            return S.op("pool", fn, [B(x) if not isinstance(x, Buf) else x for x in r],
                        [B(x) if not isinstance(x, Buf) else x for x in w])

        def pe(fn, r, w, inc=True):
            ins = S.op("pe", fn, [B(x) if not isinstance(x, Buf) else x for x in r],
                       [B(x) if not isinstance(x, Buf) else x for x in w], inc=inc)
            if inc and FILLER and fill_on[0]:
                nc.tensor.matmul(dummy_bank[:, :], lhsT=win[:, 0, 0:128], rhs=win[:, 1, 0:512], start=True, stop=True)
            return ins

        def dma(on, out, in_, r, w, **kw):
            return S.dma(on, out, in_, [B(x) if not isinstance(x, Buf) else x for x in r],
                         [B(x) if not isinstance(x, Buf) else x for x in w], **kw)

        pTb = ps("pTb", [128, 8, 128], BF16)
        NBK = 6 if FILLER else 7
        banks = [ps("bk%d" % i, [128, 512], F32) for i in range(NBK)]
        dummy_bank = ps("bkdummy", [128, 512], F32) if FILLER else None
        bstate = [0]
        fill_on = [False]

        bpool = [tuple(range(NBK))]
        bpos = {}

        def bank():
            pl = bpool[0]
            k_ = bpos.get(pl, 0)
            bpos[pl] = k_ + 1
            i = pl[k_ % len(pl)]
            return banks[i], "bk%d" % i

        def v3(t, n, a):
            return t[:n, :].rearrange("p (a b) -> p a b", a=a)

        win = sbA("win", [128, 8, NPROJ], BF16)
        for k in range(8):
            dma("pool", win[:, k, :], d["win"][:, k * NPROJ:(k + 1) * NPROJ], [], ["win"])
        cw = sbA("cw", [128, 12, 4])
        dma("sp", cw[:], d["convw"].rearrange("p (c j) -> p c j", c=12), [], ["cw"])
        cm = sb("cm", [128, 4, 128])
        dma("sp", cm[:], d["cmask"].rearrange("p (c j) -> p c j", c=4), [], ["cm"])
        TRI, INCL, STRICT, ONES = cm[:, 0, :], cm[:, 1, :], cm[:, 2, :], cm[:, 3, :]
        sel65 = sbA("sel65", [65, 64])
        dma("sp", sel65[:], d["sel65"], [], ["sel65"])
        bprev = sbA("bprev", [128, 8, 128]); bcur = sbA("bcur", [128, 8, 128])
        dma("sp", bprev[:], d["bprev1"].rearrange("p (h q) -> p h q", h=8), [], ["bprev"])
        dma("sp", bcur[:], d["bcur"].rearrange("p (h q) -> p h q", h=8), [], ["bcur"])
        bsc = sbA("bsc", [128, 8, 4]); bsn = sbA("bsn", [4, 8, 4])
        dma("sp", bsc[:], d["bs_cache"].rearrange("p (h q) -> p h q", h=8), [], ["bsc"])
        dma("sp", bsn[:], d["bs_new"].rearrange("p (h q) -> p h q", h=8), [], ["bsn"])
        gmpre = sb("gmpre", [128, D]); gmpost = sb("gmpost", [128, D])
        dma("sp", gmpre[:], d["gmpre"].partition_broadcast(128), [], ["gmpre"])
        dma("sp", gmpost[:], d["gmpost"].partition_broadcast(128), [], ["gmpost"])
        gdnn = sb("gdnn", [128, 128])
        dma("sp", gdnn[:], d["gdnn"].partition_broadcast(128), [], ["gdnn"])
        sm = sb("sm", [128, 32])
        dma("sp", sm[:, 0:4], d["alog"].partition_broadcast(128), [], ["sm"])
        dma("sp", sm[:, 4:8], d["dtb"].partition_broadcast(128), [], ["sm"])
        dma("sp", sm[:, 8:16], d["sinks"].partition_broadcast(128), [], ["sm"])
        dma("sp", sm[:, 16:24], d["rmask"].partition_broadcast(128), [], ["sm"])
        act(lambda: nc.scalar.activation(out=sm[:, 0:4], in_=sm[:, 0:4], func=AF.Exp), ["sm"], ["sm"])
        dve(lambda: nc.vector.tensor_scalar(out=sm[:, 0:4], in0=sm[:, 0:4], scalar1=-1.0, scalar2=None, op0=ALU.mult),
            ["sm"], ["sm"])
        act(lambda: nc.scalar.activation(out=sm[:, 8:16], in_=sm[:, 8:16], func=AF.Exp), ["sm"], ["sm"])
        negA, dtb, rmask = sm[:, 0:4], sm[:, 4:8], sm[:, 16:24]
        idf, idb = C.idf, C.idb
        ID = C.B_id

        xt = sb("xt", [128, D]); junk = sb("junk", [128, D], BF16)
        hbm = sbA("hbm", [128, D], BF16)
        st = sb("st", [128, 64])
        pool(lambda: nc.gpsimd.memset(st[:], 0.0), [], ["st"])
        hT = sbA("hT", [128, 8, 128], BF16)
        rawx = [sbA("rawx%d" % i, [128, 12, 131]) for i in range(2)]
        rawxs = rawx[1][:, :, 0:112].rearrange("p c (s t) -> p c s t", t=7)
        ctmp = sbA("ctmp", [128, 12, 128])
        sq = ctmp[:, 0:8, :]
        rn = sbA("rn", [128, 8, 128])
        zsb = sb("zsb", [128, 512])
        ksT = [sbA("ksT%d" % i, [64, 2, 128], BF16) for i in range(4)]
        vaug = [sbA("vaug%d" % i, [128, 2, 65], BF16) for i in range(4)]
        for i in range(4):
            pool(lambda: nc.gpsimd.memset(vaug[i][:], 1.0), [], ["vaug%d" % i])

        class Slot:
            pass
        PS = []
        for i in range(2):
            P_ = Slot()
            P_.nm = {}
            for nm_, shape_, dt_ in [("qkf", [128, 8, 128], F32), ("qkb", [128, 8, 128], BF16),
                                     ("cacc", [128, 12, 128], F32), ("misc", [128, 264], F32),
                                     ("qsT", [64, 8, 128], BF16), ("qdecT", [128, 4, 128], BF16),
                                     ("qkT", [128, 4, 128], BF16), ("kdec", [128, 4, 128], BF16),
                                     ("wT", [128, 4, 128], BF16), ("uaug", [128, 4, DVW], F32), ("glb", [128, 8], F32)]:
                setattr(P_, nm_, sbA("%s_%d" % (nm_, i), shape_, dt_))
                P_.nm[nm_] = "%s_%d" % (nm_, i)
            P_.ys = P_.cacc
            pool(lambda: nc.gpsimd.memset(P_.uaug[:], 0.0), [], [P_.nm["uaug"]])
            PS.append(P_)
        qsT3 = [PS[0].qsT, PS[1].qsT, sbA("qsT_2", [64, 8, 128], BF16)]
        _slots = {}

        def slot(ti):
            key = (ti % 2, ti % 3)
            if key not in _slots:
                Q = Slot()
                Q.__dict__.update(PS[ti % 2].__dict__)
                Q.nm = dict(PS[ti % 2].nm)
                Q.qsT = qsT3[ti % 3]
                Q.nm["qsT"] = "qsT_%d" % (ti % 3)
                _slots[key] = Q
            return _slots[key]
        g4 = sbA("g4", [128, 48])
        pool(lambda: nc.gpsimd.memset(g4[:], 0.0), [], ["g4"])
        diag = sbA("t1", [128, 4, 128]); erow = sbA("erow", [128, 4, 128])
        t1 = diag; dec = sbA("dec", [128, 4, 128]); decT = sbA("decT", [128, 4, 128])
        ktm = sbA("ktm", [128, 4, 128]); vtm = sbA("vtm", [128, 4, 128])
        kbg = sbA("kbg", [128, 4, 128], BF16)
        vb = sbA("vb", [128, 4, 128], BF16)
        X = [sbA("X%d" % i, [128, 4, 128]) for i in range(2)]
        XT = [sbA("XT%d" % i, [128, 4, 128]) for i in range(2)]
        Nm = X[1]
        TT = sbA("TT", [128, 4, 128]); TTb = sbA("TTb", [128, 4, 128], BF16)
        vnew = sbA("vnew", [128, 4, DVW], BF16)
        Sf = sbA("Sf", [128, 4, DVW]); Sb_ = sbA("Sb", [128, 4, DVW], BF16)
        oloc = sb("oloc", [128, 4, 128]); oPT = sb("oPT", [128, 4, 128], BF16)
        tp = sbA("tp", [128, 512]); pTp = sbA("pTp", [128, 512], BF16)
        tc_ = sbA("tc", [128, 512]); pTc = sbA("pTc", [128, 512], BF16)
        oTa = sbA("oTa", [65, 512]); rden = sbA("rden", [64, 512])

        def drain(g):
            for _ in g:
                pass

        def interleave(*gens, weights=None, pools=None):
            ws = list(weights) if weights else [1] * len(gens)
            ps_ = list(pools) if pools else [tuple(range(NBK))] * len(gens)
            gw = [(g, w_, p_) for g, w_, p_ in zip(gens, ws, ps_) if g is not None]
            full = bpool[0]
            while gw:
                for g, w_, p_ in list(gw):
                    bpool[0] = tuple(p_)
                    for _ in range(w_):
                        try:
                            next(g)
                        except StopIteration:
                            gw.remove((g, w_, p_))
                            break
            bpool[0] = full
        swaT = sb("swaT", [64, 8, 128], BF16); swaTs = sb("swaTs", [64, 8, 64], BF16)
        kcf = sbA("kcf", [128, 128]); vcf = sbA("vcf", [128, 128])
        og = sb("og", [128, 4, 128]); og2 = sb("og2", [128, 4, 128]); sz = sb("sz", [128, 512])
        ogb = sb("ogb", [128, 512], BF16)
        mixT = sb("mixT", [128, 4, 128], BF16)
        mtmp = sb("mtmp", [128, D])
        cvo = sbA("cvo", [48, 512])
        stc = [0]

        def newstat(k=4):
            c = stc[0]
            stc[0] += k
            assert stc[0] <= 64
            return c

        def load_norm_T(r0, n):
            dma("sp", xt[:n, :], d["x1"][r0:r0 + n, :], [DB(d["x1"])], ["xt"])
            pool(lambda: nc.gpsimd.memset(st[:, 0:4], 0.0), [], ["st"])
            yield
            act(lambda: nc.scalar.activation(out=junk[:n, :], in_=xt[:n, :], func=AF.Square, accum_out=st[:n, 0:1]),
                ["xt", "st"], ["junk", "st"])
            yield
            rstd_ops(C, st[:n, 0:1], st[:n, 1:2], n, D, B("st"))
            dve(lambda: nc.vector.scalar_tensor_tensor(out=hbm[:n, :], in0=xt[:n, :], scalar=st[:n, 1:2],
                                                       in1=gmpre[:n, :], op0=ALU.mult, op1=ALU.mult),
                ["xt", "st", "gmpre"], ["hbm"])
            yield
            for k in range(8):
                pe(lambda: nc.tensor.transpose(pTb[:, k, :n], hbm[:n, k * 128:(k + 1) * 128], idb[:n, :n]),
                   ["hbm", ID], ["pTb"], inc=(k == 7))
            act(lambda: nc.scalar.copy(out=hT[:, :, :n], in_=pTb[:, :, :n]), ["pTb"], ["hT"])
            yield

        def proj_feat(P, n, raw_dst, with_qs=True, with_q=True):
            traw = ctmp[:, :, :].rearrange("p c t -> p (c t)")
            nb0 = 0 if with_q else 1
            for nb in range(nb0, 3):
                bk, bn = bank()
                for k in range(8):
                    pe(lambda: nc.tensor.matmul(bk[:n, :], lhsT=hT[:, k, :n], rhs=win[:, k, nb * 512:(nb + 1) * 512],
                                                start=(k == 0), stop=(k == 7)), ["win", "hT"], [bn], inc=(k == 7))
                act(lambda: nc.scalar.copy(out=traw[:n, nb * 512:(nb + 1) * 512], in_=bk[:n, :]), [bn], ["ctmp"])
                yield
            for g in range(nb0, 3):
                bk, bn = bank()
                for cc in range(4):
                    c = g * 4 + cc
                    pe(lambda: nc.tensor.transpose(bk[:, cc * 128:cc * 128 + n], traw[:n, c * 128:(c + 1) * 128], idf[:n, :n]),
                       ["ctmp", ID], [bn], inc=(cc == 3))
                raw_dst(g, v3(bk, 128, 4)[:, :, :n], bn)
                yield
        def proj_qs(P, n, with_qs=True):
            if with_qs:
                bk, bn = bank()
                for k in range(8):
                    pe(lambda: nc.tensor.matmul(bk[:n, :], lhsT=hT[:, k, :n], rhs=win[:, k, O_QS:O_QS + 512],
                                                start=(k == 0), stop=(k == 7)), ["win", "hT"], [bn], inc=(k == 7))
                act(lambda: nc.scalar.copy(out=hbm[:n, 0:512], in_=bk[:n, :]), [bn], ["hbm"])
                yield
                for h in range(8):
                    pe(lambda: nc.tensor.transpose(pTb[:64, h, :n], hbm[:n, h * 64:(h + 1) * 64], idb[:n, :n]),
                       ["hbm", ID], ["pTb"], inc=(h == 7))
                act(lambda: nc.scalar.copy(out=P.qsT[:, :, :n], in_=pTb[:64, :, :n]), ["pTb"], [P.nm["qsT"]])
                yield

        def proj_ksT(P, n, dst, dname):
            act(lambda: nc.scalar.copy(out=hbm[:n, 512:640], in_=P.misc[:n, 0:128]), [P.nm["misc"]], ["hbm"])
            yield
            for kh in range(2):
                pe(lambda: nc.tensor.transpose(pTb[:64, kh, :n], hbm[:n, 512 + kh * 64:512 + (kh + 1) * 64], idb[:n, :n]),
                   ["hbm", ID], ["pTb"], inc=(kh == 1))
            act(lambda: nc.scalar.copy(out=dst[:, :, :n], in_=pTb[:64, 0:2, :n]), ["pTb"], [dname])
            yield

        def proj_tok(P, c0, n, with_z=True, zdst=None, zname="zsb"):
            zdst = zsb if zdst is None else zdst
            if with_z:
                bk, bn = bank()
                for k in range(8):
                    pe(lambda: nc.tensor.matmul(bk[:n, :], lhsT=hT[:, k, c0:c0 + n], rhs=win[:, k, O_Z:O_Z + 512],
                                                start=(k == 0), stop=(k == 7)), ["win", "hT"], [bn], inc=(k == 7))
                act(lambda: nc.scalar.copy(out=zdst[:n, :], in_=bk[:n, :]), [bn], [zname])
                yield
            bk2, bn2 = bank()
            for k in range(8):
                pe(lambda: nc.tensor.matmul(bk2[:n, 0:264], lhsT=hT[:, k, c0:c0 + n], rhs=win[:, k, O_KS:O_KS + 264],
                                            start=(k == 0), stop=(k == 7)), ["win", "hT"], [bn2], inc=(k == 7))
            dve(lambda: nc.vector.tensor_copy(out=P.misc[:n, :], in_=bk2[:n, 0:264]), [bn2], [P.nm["misc"]])
            yield

        def conv_l2(P, n, taps, ys_v, full=True):
            c_lo = 0 if full else 4
            cwv = cw[:, c_lo:12, :]
            cw_b = lambda j, shape: cwv[:, :, j:j + 1].to_broadcast(shape) if len(shape) == 3 else \
                cwv[:, :, j:j + 1].unsqueeze(3).to_broadcast(shape)
            tp_ = lambda j: taps(j)[:, c_lo:12]
            shape = list(tp_(0).shape)
            accv = P.cacc[:, c_lo:12, :n] if len(shape) == 3 else P.cacc[:, c_lo:12, :n].rearrange("p c (s t) -> p c s t", t=4)
            tmpv = ctmp[:, c_lo:12, :n] if len(shape) == 3 else ctmp[:, c_lo:12, :n].rearrange("p c (s t) -> p c s t", t=4)
            dve(lambda: nc.vector.tensor_tensor(out=accv, in0=tp_(0), in1=cw_b(0, shape), op=ALU.mult),
                ["rawcur", "cw"], [P.nm["cacc"]])
            yield
            for j in range(1, 4):
                dve(lambda: nc.vector.tensor_tensor(out=tmpv, in0=tp_(j), in1=cw_b(j, shape), op=ALU.mult),
                    ["rawcur", "cw"], ["ctmp"])
                yield
                dve(lambda: nc.vector.tensor_tensor(out=accv, in0=accv, in1=tmpv, op=ALU.add), [P.nm["cacc"], "ctmp"], [P.nm["cacc"]])
                yield
            act(lambda: nc.scalar.activation(out=P.ys[:, c_lo:12, :n], in_=P.cacc[:, c_lo:12, :n], func=AF.Silu), [P.nm["cacc"]], [P.nm["cacc"]])
            yield
            pool(lambda: nc.gpsimd.tensor_tensor(out=sq[:, c_lo:8, :n], in0=P.ys[:, c_lo:8, :n], in1=P.ys[:, c_lo:8, :n], op=ALU.mult),
                 [P.nm["cacc"]], ["ctmp"])
            yield
            for g in range(0 if full else 1, 2):
                bk, bn = bank()
                pe(lambda: nc.tensor.matmul(bk[:, :4 * n], lhsT=ONES, rhs=sq[:, g * 4:(g + 1) * 4, :n],
                                            start=True, stop=True), ["ctmp", "cm"], [bn])
                act(lambda: nc.scalar.activation(out=rn[:, g * 4:(g + 1) * 4, :n],
                                                 in_=bk[:, :4 * n].rearrange("p (a b) -> p a b", a=4),
                                                 func=AF.Ln, bias=C.epsc[:, 0:1]), [bn, ID], ["rn"])
                yield
            act(lambda: nc.scalar.activation(out=rn[:, c_lo:8, :n], in_=rn[:, c_lo:8, :n], func=AF.Exp, scale=-0.5),
                ["rn"], ["rn"])
            yield
            if full:
                dve(lambda: nc.vector.scalar_tensor_tensor(out=P.qkf[:, 0:4, :n], in0=P.ys[:, 0:4, :n], scalar=128.0 ** -0.5,
                                                           in1=rn[:, 0:4, :n], op0=ALU.mult, op1=ALU.mult),
                    [P.nm["cacc"], "rn"], [P.nm["qkf"]])
                yield
            dve(lambda: nc.vector.tensor_tensor(out=P.qkf[:, 4:8, :n], in0=P.ys[:, 4:8, :n], in1=rn[:, 4:8, :n], op=ALU.mult),
                [P.nm["cacc"], "rn"], [P.nm["qkf"]])
            yield
            act(lambda: nc.scalar.copy(out=P.qkb[:, c_lo:8, :n], in_=P.qkf[:, c_lo:8, :n]), [P.nm["qkf"]], [P.nm["qkb"]])
            yield

        def gdn_pre(P, n, c0, nlev, full=True):
            bcol = lambda ap: ap.unsqueeze(2).to_broadcast([n, 4, 128])
            bcn = lambda ap: ap.unsqueeze(2).to_broadcast([n, 4, n])
            G = g4
            dve(lambda: nc.vector.tensor_tensor(out=G[:n, 0:4], in0=P.misc[:n, 260:264], in1=dtb[:n, :], op=ALU.add),
                [P.nm["misc"], "sm"], ["g4"])
            yield
            dve(lambda: nc.vector.scalar_tensor_tensor(out=G[:n, 4:8], in0=G[:n, 0:4], scalar=-1.0, in1=G[:n, 0:4],
                                                       op0=ALU.mult, op1=ALU.max), ["g4"], ["g4"])
            yield
            act(lambda: nc.scalar.activation(out=G[:n, 4:8], in_=G[:n, 4:8], func=AF.Exp, scale=-1.0), ["g4"], ["g4"])
            yield
            act(lambda: nc.scalar.activation(out=G[:n, 4:8], in_=G[:n, 4:8], func=AF.Ln, bias=C.epsc[:n, 1:2]),
                ["g4", ID], ["g4"])
            yield
            dve(lambda: nc.vector.scalar_tensor_tensor(out=G[:n, 8:12], in0=G[:n, 0:4], scalar=0.0, in1=G[:n, 4:8],
                                                       op0=ALU.max, op1=ALU.add), ["g4"], ["g4"])
            yield
            dve(lambda: nc.vector.tensor_tensor(out=G[:n, 12:16], in0=G[:n, 8:12], in1=negA[:n, :], op=ALU.mult),
                ["g4", "sm"], ["g4"])
            yield
            act(lambda: nc.scalar.activation(out=G[:n, 16:20], in_=P.misc[:n, 256:260], func=AF.Exp, scale=-1.0),
                [P.nm["misc"]], ["g4"])
            yield
            dve(lambda: nc.vector.tensor_scalar(out=G[:n, 16:20], in0=G[:n, 16:20], scalar1=1.0, scalar2=None,
                                                op0=ALU.add), ["g4"], ["g4"])
            yield
            dve(lambda: nc.vector.reciprocal(out=G[:n, 16:20], in_=G[:n, 16:20]), ["g4"], ["g4"])
            yield
            dve(lambda: nc.vector.tensor_scalar(out=G[:n, 20:24], in0=G[:n, 16:20], scalar1=-1.0, scalar2=None,
                                                op0=ALU.mult), ["g4"], ["g4"])
            yield
            gcol, beta, nbeta = G[:n, 12:16], G[:n, 16:20], G[:n, 20:24]
            _ck("p1")
            bk, bn = bank()
            pe(lambda: nc.tensor.matmul(bk[:n, 0:4], lhsT=TRI[:n, :n], rhs=gcol, start=True, stop=True),
               ["g4", "cm"], [bn], inc=False)
            pe(lambda: nc.tensor.matmul(bk[:, 4:8], lhsT=ONES[:n, :], rhs=gcol, start=True, stop=True),
               ["g4", "cm"], [bn])
            dve(lambda: nc.vector.tensor_copy(out=G[:n, 24:28], in_=bk[:n, 0:4]), [bn], ["g4"])
            yield
            dve(lambda: nc.vector.tensor_copy(out=P.glb[:, 0:4], in_=bk[:, 4:8]), [bn], [P.nm["glb"]])
            yield
            gc = G[:n, 24:28]
            act(lambda: nc.scalar.activation(out=G[:n, 28:32], in_=gc, func=AF.Exp), ["g4"], ["g4"])
            yield
            dve(lambda: nc.vector.tensor_tensor(out=G[:n, 32:36], in0=P.glb[:n, 0:4], in1=gc, op=ALU.subtract),
                ["g4", P.nm["glb"]], ["g4"])
            yield
            act(lambda: nc.scalar.activation(out=G[:n, 32:36], in_=G[:n, 32:36], func=AF.Exp), ["g4"], ["g4"])
            yield
            act(lambda: nc.scalar.activation(out=P.glb[:, 4:8], in_=P.glb[:, 0:4], func=AF.Exp), [P.nm["glb"]], [P.nm["glb"]])
            yield
            eg, ekl = G[:n, 28:32], G[:n, 32:36]
            _ck("p2")
            dve(lambda: nc.vector.tensor_tensor(out=diag[:n, :, :n], in0=idf[:n, :n].unsqueeze(1).to_broadcast([n, 4, n]),
                                                in1=gc.unsqueeze(2).to_broadcast([n, 4, n]), op=ALU.mult),
                ["g4", ID], ["t1"])
            yield
            gbk, gbn = bank()
            pe(lambda: nc.tensor.matmul(gbk[:, :4 * n], lhsT=ONES[:n, :], rhs=diag[:n, :, :n],
                                        start=True, stop=True), ["t1", "cm"], [gbn])
            grow = gbk[:, :4 * n].rearrange("p (a b) -> p a b", a=4)
            _ck("p2a")
            if full:
                act(lambda: nc.scalar.activation(out=erow[:, :, :n], in_=grow[:, :, :n], func=AF.Exp), [gbn], ["erow"])
                yield
            _ck("p2b")
            if full:
                dve(lambda: nc.vector.tensor_scalar(out=G[:n, 44:48], in0=gc, scalar1=-1.0, scalar2=None, op0=ALU.mult),
                    ["g4"], ["g4"])
                yield
            for h in range(4):
                act(lambda: nc.scalar.activation(out=dec[:n, h, :n], in_=grow[:n, h, :n], func=AF.Exp, scale=-1.0,
                                                 bias=gc[:, h:h + 1]), [gbn, "g4"], ["dec"])
                yield
            _ck("p2c")
            dve(lambda: nc.vector.scalar_tensor_tensor(out=dec[:n, :, :n], in0=dec[:n, :, :n], scalar=1.0,
                                                       in1=STRICT[:n, :n].unsqueeze(1).to_broadcast([n, 4, n]),
                                                       op0=ALU.min, op1=ALU.mult), ["dec", "cm"], ["dec"])
            yield
            pool(lambda: nc.gpsimd.tensor_tensor(out=dec[:n, :, :n], in0=dec[:n, :, :n], in1=bcn(nbeta), op=ALU.mult),
                 ["dec", "g4"], ["dec"])
            yield
            for h in range(4 if full else 0):
                act(lambda: nc.scalar.activation(out=decT[:n, h, :n], in_=grow[:n, h, :n], func=AF.Exp, scale=1.0,
                                                 bias=G[:n, 44 + h:45 + h]), [gbn, "g4"], ["decT"])
                yield
            if full:
                dve(lambda: nc.vector.scalar_tensor_tensor(out=decT[:n, :, :n], in0=decT[:n, :, :n], scalar=1.0,
                                                           in1=TRI[:n, :n].unsqueeze(1).to_broadcast([n, 4, n]),
                                                           op0=ALU.min, op1=ALU.mult), ["decT", "cm"], ["decT"])
                yield
                dve(lambda: nc.vector.tensor_tensor(out=P.qdecT[:, :, :n], in0=P.qkf[:, 0:4, c0:c0 + n], in1=erow[:, :, :n],
                                                    op=ALU.mult), [P.nm["qkf"], "erow"], [P.nm["qdecT"]])
                yield
            _ck("p3")
            bk, bn = bank()
            for h in range(4):
                pe(lambda: nc.tensor.transpose(bk[:n, h * 128:(h + 1) * 128], P.qkf[:, 4 + h, c0:c0 + n], idf[:, :]),
                   [P.nm["qkf"], ID], [bn], inc=(h == 3))
            act(lambda: nc.scalar.copy(out=ktm[:n, :, :], in_=v3(bk, n, 4)), [bn], ["ktm"])
            yield
            bk, bn = bank()
            for h in range(4):
                pe(lambda: nc.tensor.transpose(bk[:n, h * 128:(h + 1) * 128], P.ys[:, 8 + h, c0:c0 + n], idf[:, :]),
                   [P.nm["cacc"], ID], [bn], inc=(h == 3))
            act(lambda: nc.scalar.copy(out=vtm[:n, :, :], in_=v3(bk, n, 4)), [bn], ["vtm"])
            yield
            dve(lambda: nc.vector.tensor_tensor(out=G[:n, 36:40], in0=beta, in1=eg, op=ALU.mult), ["g4"], ["g4"])
            yield
            dve(lambda: nc.vector.tensor_tensor(out=kbg[:n], in0=ktm[:n], in1=bcol(G[:n, 36:40]), op=ALU.mult),
                ["ktm", "g4"], ["kbg"])
            yield
            pool(lambda: nc.gpsimd.tensor_tensor(out=P.kdec[:n], in0=ktm[:n], in1=bcol(ekl), op=ALU.mult),
                 ["ktm", "g4"], [P.nm["kdec"]])
            yield
            pool(lambda: nc.gpsimd.tensor_tensor(out=vb[:n], in0=vtm[:n], in1=bcol(beta), op=ALU.mult),
                 ["vtm", "g4"], ["vb"])
            yield
            _ck("p4")
            kbk, kbn = bank()
            for h in range(4):
                pe(lambda: nc.tensor.matmul(kbk[:n, h * 128:h * 128 + n], lhsT=P.qkb[:, 4 + h, c0:c0 + n],
                                            rhs=P.qkb[:, 4 + h, c0:c0 + n], start=True, stop=True),
                   [P.nm["qkb"]], [kbn], inc=(h == 3))
            if full:
                qbk, qbn = bank()
                for h in range(4):
                    pe(lambda: nc.tensor.matmul(qbk[:n, h * 128:h * 128 + n], lhsT=P.qkb[:, 4 + h, c0:c0 + n],
                                                rhs=P.qkb[:, h, c0:c0 + n], start=True, stop=True),
                       [P.nm["qkb"]], [qbn], inc=(h == 3))
            dve(lambda: nc.vector.tensor_tensor(out=X[0][:n, :, :n], in0=v3(kbk, n, 4)[:, :, :n], in1=dec[:n, :, :n],
                                                op=ALU.mult), [kbn, "dec"], ["X0"])
            yield
            if full:
                dve(lambda: nc.vector.tensor_tensor(out=P.qkT[:n, :, :n], in0=v3(qbk, n, 4)[:, :, :n], in1=decT[:n, :, :n],
                                                    op=ALU.mult), [qbn, "decT"], [P.nm["qkT"]])
                yield
            _ck("p5")
            bk, bn = bank()
            for h in range(4):
                pe(lambda: nc.tensor.transpose(bk[:n, h * 128:h * 128 + n], X[0][:n, h, :n], idf[:n, :n]),
                   ["X0", ID], [bn], inc=(h == 3))
            act(lambda: nc.scalar.copy(out=XT[0][:n, :, :n], in_=v3(bk, n, 4)[:, :, :n]), [bn], ["XT0"])
            yield
            dve(lambda: nc.vector.tensor_tensor(out=TT[:n, :, :n], in0=XT[0][:n, :, :n],
                                                in1=idf[:n, :n].unsqueeze(1).to_broadcast([n, 4, n]), op=ALU.add),
                ["XT0", ID], ["TT"])
            yield
            cur = 0
            for lev in range(1, nlev):
                nx = 1 - cur
                b1, b1n = bank()
                for h in range(4):
                    pe(lambda: nc.tensor.matmul(b1[:n, h * 128:h * 128 + n], lhsT=XT[cur][:n, h, :n],
                                                rhs=X[cur][:n, h, :n], start=True, stop=True),
                       ["X%d" % cur, "XT%d" % cur], [b1n], inc=(h == 3))
                act(lambda: nc.scalar.copy(out=X[nx][:n, :, :n], in_=v3(b1, n, 4)[:, :, :n]), [b1n], ["X%d" % nx])
                yield
                if lev < nlev - 1:
                    b2, b2n = bank()
                    for h in range(4):
                        pe(lambda: nc.tensor.matmul(b2[:n, h * 128:h * 128 + n], lhsT=X[cur][:n, h, :n],
                                                    rhs=XT[cur][:n, h, :n], start=True, stop=True),
                           ["X%d" % cur, "XT%d" % cur], [b2n], inc=(h == 3))
                    dve(lambda: nc.vector.tensor_copy(out=XT[nx][:n, :, :n], in_=v3(b2, n, 4)[:, :, :n]),
                        [b2n], ["XT%d" % nx])
                    yield
                b3, b3n = bank()
                for h in range(4):
                    pe(lambda: nc.tensor.matmul(b3[:n, h * 128:h * 128 + n], lhsT=X[nx][:n, h, :n],
                                                rhs=TT[:n, h, :n], start=True, stop=True),
                       ["X%d" % nx, "TT"], [b3n], inc=(h == 3))
                dve(lambda: nc.vector.tensor_tensor(out=TT[:n, :, :n], in0=TT[:n, :, :n], in1=v3(b3, n, 4)[:, :, :n],
                                                    op=ALU.add), ["TT", b3n], ["TT"])
                yield
                cur = nx
            act(lambda: nc.scalar.copy(out=TTb[:n, :, :n], in_=TT[:n, :, :n]), ["TT"], ["TTb"])
            yield
            _ck("p6")
            bk, bn = bank()
            for h in range(4):
                pe(lambda: nc.tensor.matmul(bk[:n, h * 128:(h + 1) * 128], lhsT=TTb[:n, h, :n], rhs=vb[:n, h, :],
                                            start=True, stop=True), ["TTb", "vb"], [bn], inc=(h == 3))
            act(lambda: nc.scalar.copy(out=P.uaug[:n, :, 0:128], in_=v3(bk, n, 4)), [bn], [P.nm["uaug"]])
            yield
            bk, bn = bank()
            for h in range(4):
                pe(lambda: nc.tensor.matmul(bk[:, h * 128:h * 128 + n], lhsT=kbg[:n, h, :], rhs=TTb[:n, h, :n],
                                            start=True, stop=True), ["TTb", "kbg"], [bn], inc=(h == 3))
            dve(lambda: nc.vector.tensor_copy(out=P.wT[:, :, :n], in_=v3(bk, 128, 4)[:, :, :n]), [bn], [P.nm["wT"]])
            yield

        def gdn_scan(P, n, dvw, state_only=False):
            aug = dvw == 256
            pb = [bank() for _ in range(2 if aug else 1)]
            per = 2 if aug else 4

            def reg(pbk, h):
                return pbk[h // per][0][:, (h % per) * dvw:(h % per + 1) * dvw]

            for h in range(4):
                pe(lambda: nc.tensor.matmul(reg(pb, h)[:n, :], lhsT=P.wT[:, h, :n], rhs=Sb_[:, h, :dvw],
                                            start=True, stop=True), [P.nm["wT"], "Sb"], [pb[h // per][1]],
                   inc=(h % per == per - 1))
            for i, (bk, bn) in enumerate(pb):
                dve(lambda: nc.vector.tensor_tensor(out=vnew[:n, i * per:(i + 1) * per, :dvw],
                                                    in0=P.uaug[:n, i * per:(i + 1) * per, :dvw],
                                                    in1=bk[:n, :per * dvw].rearrange("p (a b) -> p a b", a=per),
                                                    op=ALU.subtract), [P.nm["uaug"], bn], ["vnew"])
                yield
            if not state_only:
                obk, obn = bank()
                for h in range(4):
                    pe(lambda: nc.tensor.matmul(obk[:n, h * 128:(h + 1) * 128], lhsT=P.qdecT[:, h, :n], rhs=Sb_[:, h, 0:128],
                                                start=True, stop=False), [P.nm["qdecT"], "Sb"], [obn], inc=False)
                    pe(lambda: nc.tensor.matmul(obk[:n, h * 128:(h + 1) * 128], lhsT=P.qkT[:n, h, :n], rhs=vnew[:n, h, 0:128],
                                                start=False, stop=True), [P.nm["qkT"], "vnew"], [obn], inc=(h == 3))
                act(lambda: nc.scalar.copy(out=oloc[:n], in_=v3(obk, n, 4)), [obn], ["oloc"])
                yield
            if aug:
                pbk, pbn = bank()
                for h in range(4):
                    pe(lambda: nc.tensor.matmul(pbk[:, h * 128:h * 128 + n], lhsT=Sb_[:, h, 128:256], rhs=P.qdecT[:, h, :n],
                                                start=True, stop=False), [P.nm["qdecT"], "Sb"], [pbn], inc=False)
                    pe(lambda: nc.tensor.matmul(pbk[:, h * 128:h * 128 + n], lhsT=vnew[:n, h, 128:256], rhs=P.qkT[:n, h, :n],
                                                start=False, stop=True), [P.nm["qkT"], "vnew"], [pbn], inc=(h == 3))
                dve(lambda: nc.vector.tensor_copy(out=oPT[:, :, :n], in_=v3(pbk, 128, 4)[:, :, :n]), [pbn], ["oPT"])
                yield
            sbk = [bank() for _ in range(2 if aug else 1)]
            for h in range(4):
                pe(lambda: nc.tensor.matmul(reg(sbk, h), lhsT=P.kdec[:n, h, :], rhs=vnew[:n, h, :dvw],
                                            start=True, stop=True), [P.nm["kdec"], "vnew"], [sbk[h // per][1]],
                   inc=(h % per == per - 1))
            for h in range(4):
                dve(lambda: nc.vector.scalar_tensor_tensor(out=Sf[:, h, :dvw], in0=Sf[:, h, :dvw], scalar=P.glb[:, 4 + h:5 + h],
                                                           in1=reg(sbk, h), op0=ALU.mult, op1=ALU.add),
                    ["Sf", P.nm["glb"], sbk[h // per][1]], ["Sf"])
                yield
            act(lambda: nc.scalar.copy(out=Sb_[:, :, :dvw], in_=Sf[:, :, :dvw]), ["Sf"], ["Sb"])
            yield

        def swa(P, n, qc0, kprevT, kpn, vprev, vpn, kcurT, kcn, vcur, vcn, bp, bpn, bc, bcn_, dst, dstn, dst_c0):
            for kh in range(2):
                hs = slice(kh * 4, (kh + 1) * 4)
                pb_, pbn = bank()
                pe(lambda: nc.tensor.matmul(pb_[:, :4 * n], lhsT=kprevT[:, kh, :], rhs=P.qsT[:, hs, qc0:qc0 + n],
                                            start=True, stop=True), [kpn, P.nm["qsT"]], [pbn])
                cb_, cbn = bank()
                pe(lambda: nc.tensor.matmul(cb_[:n, :4 * n], lhsT=kcurT[:, kh, qc0:qc0 + n], rhs=P.qsT[:, hs, qc0:qc0 + n],
                                            start=True, stop=True), [kcn, P.nm["qsT"]], [cbn])
                dve(lambda: nc.vector.scalar_tensor_tensor(out=tp[:, :4 * n].rearrange("p (a b) -> p a b", a=4),
                                                           in0=pb_[:, :4 * n].rearrange("p (a b) -> p a b", a=4),
                                                           scalar=0.125, in1=bp[:, hs, :n], op0=ALU.mult, op1=ALU.add),
                    [pbn, bpn], ["tp"])
                yield
                act(lambda: nc.scalar.activation(out=pTp[:, :4 * n], in_=tp[:, :4 * n], func=AF.Exp), ["tp"], ["pTp"])
                yield
                dve(lambda: nc.vector.scalar_tensor_tensor(out=tc_[:n, :4 * n].rearrange("p (a b) -> p a b", a=4),
                                                           in0=cb_[:n, :4 * n].rearrange("p (a b) -> p a b", a=4),
                                                           scalar=0.125, in1=bc[:n, hs, :n], op0=ALU.mult, op1=ALU.add),
                    [cbn, bcn_], ["tc"])
                yield
                act(lambda: nc.scalar.activation(out=pTc[:n, :4 * n], in_=tc_[:n, :4 * n], func=AF.Exp), ["tc"], ["pTc"])
                yield
                ob_, obn = bank()
                pe(lambda: nc.tensor.matmul(ob_[:65, :4 * n], lhsT=vprev[:, kh, :], rhs=pTp[:, :4 * n],
                                            start=True, stop=False), [vpn, "pTp"], [obn], inc=False)
                pe(lambda: nc.tensor.matmul(ob_[:65, :4 * n], lhsT=vcur[:n, kh, :], rhs=pTc[:n, :4 * n],
                                            start=False, stop=True), [vcn, "pTc"], [obn])
                act(lambda: nc.scalar.copy(out=oTa[:, :4 * n], in_=ob_[:65, :4 * n]), [obn], ["oTa"])
                yield
                db_, dbn = bank()
                pe(lambda: nc.tensor.matmul(db_[:64, :4 * n], lhsT=sel65[:, :], rhs=oTa[:, :4 * n], start=True, stop=True),
                   ["oTa", "sel65"], [dbn])
                dve(lambda: nc.vector.tensor_tensor(out=rden[:, :4 * n].rearrange("p (a b) -> p a b", a=4),
                                                    in0=db_[:64, :4 * n].rearrange("p (a b) -> p a b", a=4),
                                                    in1=sm[:64, 8 + kh * 4:12 + kh * 4].unsqueeze(2).to_broadcast([64, 4, n]),
                                                    op=ALU.add), [dbn, "sm"], ["rden"])
                yield
                act(lambda: nc.scalar.activation(out=rden[:, :4 * n], in_=rden[:, :4 * n], func=AF.Ln), ["rden"], ["rden"])
                yield
                act(lambda: nc.scalar.activation(out=rden[:, :4 * n], in_=rden[:, :4 * n], func=AF.Exp, scale=-1.0),
                    ["rden"], ["rden"])
                yield
                dve(lambda: nc.vector.tensor_tensor(out=dst[:, hs, dst_c0:dst_c0 + n],
                                                    in0=oTa[0:64, :4 * n].rearrange("p (a b) -> p a b", a=4),
                                                    in1=rden[:, :4 * n].rearrange("p (a b) -> p a b", a=4), op=ALU.mult),
                    ["oTa", "rden"], [dstn])
                yield

        def gate_cols(n, o_ap, on, z_ap, zn, dst_c0):
            c = 8
            pool(lambda: nc.gpsimd.tensor_tensor(out=og2[:n], in0=o_ap, in1=o_ap, op=ALU.mult), [on], ["og2"])
            dve(lambda: nc.vector.tensor_reduce(out=st[:n, c:c + 4], in_=og2[:n], axis=AX.X, op=ALU.add),
                ["og2"], ["st"])
            act(lambda: nc.scalar.activation(out=st[:n, c + 4:c + 8], in_=st[:n, c:c + 4], func=AF.Ln, scale=1.0 / 128,
                                             bias=C.epsc[:n, 0:1]), ["st", ID], ["st"])
            act(lambda: nc.scalar.activation(out=st[:n, c + 4:c + 8], in_=st[:n, c + 4:c + 8], func=AF.Exp, scale=-0.5),
                ["st"], ["st"])
            dve(lambda: nc.vector.tensor_tensor(out=og[:n], in0=o_ap,
                                                in1=st[:n, c + 4:c + 8].unsqueeze(2).to_broadcast([n, 4, 128]),
                                                op=ALU.mult), [on, "st"], ["og"])
            pool(lambda: nc.gpsimd.tensor_tensor(out=og[:n], in0=og[:n],
                                                 in1=gdnn[:n, :].unsqueeze(1).to_broadcast([n, 4, 128]), op=ALU.mult),
                 ["og", "gdnn"], ["og"])
            act(lambda: nc.scalar.activation(out=sz[:n, :], in_=z_ap, func=AF.Silu), [zn], ["sz"])
            dve(lambda: nc.vector.tensor_tensor(out=ogb[:n, :], in0=og[:n].rearrange("p a b -> p (a b)"), in1=sz[:n, :],
                                                op=ALU.mult), ["og", "sz"], ["ogb"])
            for cc in range(4):
                pe(lambda: nc.tensor.transpose(pTb[:, cc, :n], ogb[:n, cc * 128:(cc + 1) * 128], idb[:n, :n]),
                   ["ogb", ID], ["pTb"], inc=(cc == 3))
            act(lambda: nc.scalar.copy(out=mixT[:, :, dst_c0:dst_c0 + n], in_=pTb[:, 0:4, :n]), ["pTb"], ["mixT"])

        def out_proj(n, swa_ap, swn, r1, r2):
            dma("sp", xt[:n, :], d["x1"][r1:r1 + n, :], [DB(d["x1"])], ["xt"])
            bks = [bank(), bank()]
            for nh in range(2):
                bk, bn = bks[nh]
                for cc in range(4):
                    pe(lambda: nc.tensor.matmul(bk[:n, :], lhsT=mixT[:, cc, :n], rhs=wog[:, cc, nh * 512:(nh + 1) * 512],
                                                start=(cc == 0), stop=False), ["mixT", "wog"], [bn], inc=False)
                for h in range(8):
                    pe(lambda: nc.tensor.matmul(bk[:n, :], lhsT=swa_ap[:, h, :n], rhs=wos[:, h, nh * 512:(nh + 1) * 512],
                                                start=False, stop=(h == 7)), [swn, "wos"], [bn], inc=(h == 7))
                act(lambda: nc.scalar.copy(out=mtmp[:n, nh * 512:(nh + 1) * 512], in_=bk[:n, :]), [bn], ["mtmp"])
            pool(lambda: nc.gpsimd.memset(st[:, 16:20], 0.0), [], ["st"])
            act(lambda: nc.scalar.activation(out=junk[:n, :], in_=mtmp[:n, :], func=AF.Square, accum_out=st[:n, 16:17]),
                ["mtmp", "st"], ["junk", "st"])
            rstd_ops(C, st[:n, 16:17], st[:n, 17:18], n, D, B("st"))
            dve(lambda: nc.vector.scalar_tensor_tensor(out=mtmp[:n, :], in0=mtmp[:n, :], scalar=st[:n, 17:18],
                                                       in1=gmpost[:n, :], op0=ALU.mult, op1=ALU.mult),
                ["mtmp", "st", "gmpost"], ["mtmp"])
            pool(lambda: nc.gpsimd.tensor_tensor(out=mtmp[:n, :], in0=mtmp[:n, :], in1=xt[:n, :], op=ALU.add),
                 ["mtmp", "xt"], ["mtmp"])
            dma("sp", d["x2"][r2:r2 + n, :], mtmp[:n, :], ["mtmp"], [DB(d["x2"])])

        def conv_state_out(src_view, ncols, dst, srcn="rawcur"):
            for g in range(3):
                bk, bn = bank()
                for cc in range(4):
                    pe(lambda: nc.tensor.transpose(bk[:ncols, cc * 128:(cc + 1) * 128], src_view(g * 4 + cc), idf[:, :]),
                       [srcn, ID], [bn], inc=(cc == 3))
                act(lambda: nc.scalar.copy(out=cvo[:ncols, :], in_=bk[:ncols, :]), [bn], ["cvo"])
                dma("sp", dst[:, g * 512:(g + 1) * 512], cvo[:ncols, :], ["cvo"], [DB(dst)])

        if do1:
            pool(lambda: nc.gpsimd.memset(Sf[:], 0.0), [], ["Sf"])
            if not FUS:
                dve(lambda: nc.vector.tensor_tensor(out=Sf[:, :, 128:256], in0=Sf[:, :, 128:256],
                                                    in1=idf[:, :].unsqueeze(1).to_broadcast([128, 4, 128]), op=ALU.add),
                    ["Sf", ID], ["Sf"])
            act(lambda: nc.scalar.copy(out=Sb_[:], in_=Sf[:]), ["Sf"], ["Sb"])
            pool(lambda: nc.gpsimd.memset(rawx[0][:], 0.0), [], ["rawcur"])
            pool(lambda: nc.gpsimd.memset(rawx[1][:], 0.0), [], ["rawcur"])

            NT = NMAIN // 128
            L = list(range(-(NPRE // 128) if FUS else -1, NT))

            def F1(ti):
                P = slot(ti)
                cur = ti % 2
                prv = 1 - cur
                k3 = ti % 4
                r0 = PRE0 + ti * 128
                yield from load_norm_T(r0, 128)
                pool(lambda: nc.gpsimd.tensor_copy(out=rawx[cur][:, :, 0:3], in_=rawx[prv][:, :, 128:131]),
                     ["rawcur"], ["rawcur"])

                def raw_dst(g, src, bn):
                    act(lambda: nc.scalar.copy(out=rawx[cur][:, g * 4:(g + 1) * 4, 3:131], in_=src), [bn], ["rawcur"])
                yield from proj_feat(P, 128, raw_dst, with_qs=(ti >= 0), with_q=(ti >= -1))
                if ti == NT - 1:
                    conv_state_out(lambda c: rawx[cur][:, c, 128:131], 3, d["conv_p"])
                if FUS or ti >= 0:
                    yield from conv_l2(P, 128, lambda j: rawx[cur][:, :, j:j + 128], None, full=(ti >= 0))
                yield from proj_tok(P, 0, 128, with_z=(ti >= 0))
                if ti >= 0:
                    dma("sp", d["z_s"][ti], zsb[:, :], ["zsb"], [DB(d["z_s"])])
                yield from proj_qs(P, 128, with_qs=(ti >= 0))
                if ti >= -1:
                    yield from proj_ksT(P, 128, ksT[k3], "ksT%d" % k3)
                    pool(lambda: nc.gpsimd.tensor_copy(out=vaug[k3][:, :, 0:64],
                                                       in_=P.misc[:, 128:256].rearrange("p (a b) -> p a b", a=2)),
                         [P.nm["misc"]], ["vaug%d" % k3])
                    yield
                if ti == NT - 1:
                    dma("sp", d["swak_p"], P.misc[:, 0:128], [P.nm["misc"]], [DB(d["swak_p"])])
                    dma("sp", d["swav_p"], P.misc[:, 128:256], [P.nm["misc"]], [DB(d["swav_p"])])

            def F2(ti):
                P = slot(ti)
                if FUS or ti >= 0:
                    yield from gdn_pre(P, 128, 0, 7, full=(ti >= 0))

            def Bk(ti):
                P = slot(ti)
                if not (FUS or ti >= 0):
                    return
                yield from gdn_scan(P, 128, DVW, state_only=(ti < 0))
                if ti < 0:
                    return
                dma("sp", d["oloc_s"][ti], oloc[:].rearrange("p a b -> p (a b)"), ["oloc"], [DB(d["oloc_s"])])
                if not FUS:
                    dma("sp", d["opt_s"][ti], oPT[:].rearrange("p a b -> p (a b)"), ["oPT"], [DB(d["opt_s"])])
                kc, kp = ti % 4, (ti - 1) % 4
                yield from swa(P, 128, 0, ksT[kp], "ksT%d" % kp, vaug[kp], "vaug%d" % kp, ksT[kc], "ksT%d" % kc,
                               vaug[kc], "vaug%d" % kc, bprev, "bprev", bcur, "bcur", swaT, "swaT", 0)
                dma("sp", d["swat_s"][ti], swaT[:].rearrange("p a b -> p (a b)"), ["swaT"], [DB(d["swat_s"])])
                if ti == 0:
                    dma("sp", bprev[:], d["bprev"].rearrange("p (h q) -> p h q", h=8), [], ["bprev"])

            fill_on[0] = True
            drain(F1(L[0]))
            interleave(F2(L[0]), F1(L[1]), pools=MIXP[0:3:2])
            for idx in range(len(L)):
                interleave(F2(L[idx + 1]) if idx + 1 < len(L) else None, Bk(L[idx]),
                           F1(L[idx + 2]) if idx + 2 < len(L) else None, weights=MIXW, pools=MIXP)

            fill_on[0] = False
            if FUS:
                dma("sp", d["gdn_p"].rearrange("h k v -> k h v"), Sf[:, :, 0:128], ["Sf"], [DB(d["gdn_p"])])
            else:
                dma("sp", d["cc_send"], Sf[:].rearrange("p a b -> p (a b)"), ["Sf"], [DB(d["cc_send"])])
            if part == "AB":
                S._deps("pool", [DB(d["cc_send"])], [DB(d["cc_recv"])])
                ccs = nc.alloc_semaphore("ccsem")
                cc = nc.gpsimd.collective_compute("AllGather", ALU.bypass, replica_groups=[list(range(NCORES))],
                                                  ins=[d["cc_send"]], outs=[d["cc_recv"]])
                cc.then_inc(ccs, 16)
                S.dsem.append(ccs); S.dval.append(16)
                DB(d["cc_recv"]).w = ("d", len(S.dsem) - 1, 16)

            _ck("main")
            P = PS[0]
            r0 = PRE0 + NMAIN
            drain(load_norm_T(r0, NSAMP))
            for g in range(3):
                dma("sp", cvo[:, :], d["state_conv"][:, g * 512:(g + 1) * 512], [], ["cvo"])
                bk, bn = bank()
                for cc_ in range(4):
                    pe(lambda: nc.tensor.transpose(bk[:, cc_ * 128:cc_ * 128 + 48], cvo[:48, cc_ * 128:(cc_ + 1) * 128], idf[:48, :48]),
                       ["cvo", ID], [bn], inc=(cc_ == 3))
                act(lambda: nc.scalar.copy(out=rawxs[:, g * 4:(g + 1) * 4, :, 0:3],
                                           in_=v3(bk, 128, 4)[:, :, 0:48].rearrange("p a (s r) -> p a s r", r=3)),
                    [bn], ["rawcur"])

            def raw_dst_s(g, src, bn):
                act(lambda: nc.scalar.copy(out=rawxs[:, g * 4:(g + 1) * 4, :, 3:7],
                                           in_=src.rearrange("p a (s t) -> p a s t", t=4)), [bn], ["rawcur"])
            drain(proj_feat(P, NSAMP, raw_dst_s))
            drain(proj_qs(P, NSAMP))
            ksTs = ksT[0]
            drain(proj_tok(P, 0, NSAMP, with_z=False))
            drain(proj_ksT(P, NSAMP, ksTs, "ksT0"))
            drain(conv_l2(P, NSAMP, lambda j: rawxs[:, :, :, j:j + 4], None))
            pool(lambda: nc.gpsimd.tensor_copy(out=ctmp[:, :, 0:48].rearrange("p c (s r) -> p c s r", r=3),
                                               in_=rawxs[:, :, :, 4:7]), ["rawcur"], ["ctmp"])
            conv_state_out(lambda c: ctmp[:, c, 0:48], 48, d["conv_s"], srcn="ctmp")
            _ck("sconv")
            zs2 = [(zsb, "zsb"), (PS[1].cacc[:, 0:4, :].rearrange("p a b -> p (a b)"), PS[1].nm["cacc"])]
            _sq = {}

            def sslot(sq_):
                if sq_ % 2 not in _sq:
                    Q = Slot()
                    Q.__dict__.update(PS[0].__dict__)
                    Q.nm = dict(PS[0].nm)
                    for nm_ in ("misc", "qdecT", "qkT", "kdec", "wT", "uaug", "glb"):
                        setattr(Q, nm_, getattr(PS[sq_ % 2], nm_))
                        Q.nm[nm_] = PS[sq_ % 2].nm[nm_]
                    _sq[sq_ % 2] = Q
                return _sq[sq_ % 2]

            def s_front(sq_):
                Q = sslot(sq_)
                zd, zn = zs2[sq_ % 2]
                yield from proj_tok(Q, sq_ * 4, 4, zdst=zd, zname=zn)
                yield from gdn_pre(Q, 4, sq_ * 4, 2)

            def s_back(sq_):
                Q = sslot(sq_)
                zd, zn = zs2[sq_ % 2]
                c0 = sq_ * 4
                dma("sp", Sf[:, :, 0:128], d["state_gdn"][sq_].rearrange("h k v -> k h v"), [], ["Sf"])
                act(lambda: nc.scalar.copy(out=Sb_[:, :, 0:128], in_=Sf[:, :, 0:128]), ["Sf"], ["Sb"])
                yield
                yield from gdn_scan(Q, 4, 128)
                dma("sp", d["gdn_s"][sq_].rearrange("h k v -> k h v"), Sf[:, :, 0:128], ["Sf"], [DB(d["gdn_s"])])
                gate_cols(4, oloc[:4], "oloc", zd[:4, :], zn, c0)
                yield
                dma("sp", kcf[:, :], d["cache_k"][sq_], [], ["kcf"])
                dma("sp", vcf[:, :], d["cache_v"][sq_], [], ["vcf"])
                bk, bn = bank()
                for kh in range(2):
                    pe(lambda: nc.tensor.transpose(bk[:64, kh * 128:(kh + 1) * 128], kcf[:, kh * 64:(kh + 1) * 64], idf[:, :]),
                       ["kcf", ID], [bn], inc=(kh == 1))
                act(lambda: nc.scalar.copy(out=ksT[1][:, :, :], in_=v3(bk, 64, 4)[:, 0:2, :]), [bn], ["ksT1"])
                yield
                pool(lambda: nc.gpsimd.tensor_copy(out=vaug[1][:, :, 0:64], in_=vcf[:, :].rearrange("p (a b) -> p a b", a=2)),
                     ["vcf"], ["vaug1"])
                pool(lambda: nc.gpsimd.tensor_copy(out=vaug[0][:4, :, 0:64],
                                                   in_=Q.misc[:4, 128:256].rearrange("p (a b) -> p a b", a=2)),
                     [Q.nm["misc"]], ["vaug0"])
                yield
                yield from swa(Q, 4, c0, ksT[1], "ksT1", vaug[1], "vaug1", ksT[0], "ksT0", vaug[0], "vaug0",
                               bsc, "bsc", bsn, "bsn", swaTs, "swaTs", c0)
                dma("sp", d["swak_s"][sq_, 0:124, :], d["cache_k"][sq_, 4:128, :], [], [DB(d["swak_s"])])
                dma("sp", d["swav_s"][sq_, 0:124, :], d["cache_v"][sq_, 4:128, :], [], [DB(d["swav_s"])])
                dma("sp", d["swak_s"][sq_, 124:128, :], Q.misc[:4, 0:128], [Q.nm["misc"]], [DB(d["swak_s"])])
                dma("sp", d["swav_s"][sq_, 124:128, :], Q.misc[:4, 128:256], [Q.nm["misc"]], [DB(d["swav_s"])])

            drain(s_front(0))
            for sq_ in range(16):
                interleave(s_back(sq_), s_front(sq_ + 1) if sq_ + 1 < 16 else None,
                           pools=((3, 4, 5, 6), (0, 1, 2)))
        S.barrier()
        esA.close()
        wog = sb("wog", [128, 4, D], BF16)
        dma("pool", wog[:], d["wout_g"].rearrange("p (c n) -> p c n", c=4), [], ["wog"])
        wos = sb("wos", [64, 8, D], BF16)
        dma("pool", wos[:], d["wout_s"].rearrange("p (c n) -> p c n", c=8), [], ["wos"])
        Pr = sb("Pr", [128, 4, 256]); PmT = sb("PmT", [128, 4, 128]); Sin = sb("Sin", [128, 4, 128])
        Sinb = sb("Sinb", [128, 4, 128], BF16); cand = sb("cand", [128, 4, 128])
        if do1:
            out_proj(NSAMP, swaTs, "swaTs", PRE0 + NMAIN, NMAIN)
        if not do2:
            return
        if not do1:
            dma("sp", d["x2"][NMAIN:NMAIN + NSAMP, :], d["x2in"][NMAIN:NMAIN + NSAMP, :], [], [DB(d["x2"])])

        if not FUS:
            pool(lambda: nc.gpsimd.memset(Sin[:], 0.0), [], ["Sin"])

        def apply_P(Pbuf, Pn):
            bk, bn = bank()
            for h in range(4):
                pe(lambda: nc.tensor.transpose(bk[:, h * 128:(h + 1) * 128], Pbuf[:, h, 128:256], idf[:, :]),
                   [Pn, ID], [bn], inc=(h == 3))
            act(lambda: nc.scalar.copy(out=PmT[:], in_=v3(bk, 128, 4)), [bn], ["PmT"])
            bk2, bn2 = bank()
            for h in range(4):
                pe(lambda: nc.tensor.matmul(bk2[:, h * 128:(h + 1) * 128], lhsT=PmT[:, h, :], rhs=Sin[:, h, :],
                                            start=True, stop=True), ["PmT", "Sin"], [bn2], inc=(h == 3))
            dve(lambda: nc.vector.tensor_tensor(out=cand[:], in0=v3(bk2, 128, 4), in1=Pbuf[:, :, 0:128], op=ALU.add),
                [bn2, Pn], ["cand"])

        for r in range(0 if FUS else NCORES):
            dma("sp", Pr[:].rearrange("p a b -> p (a b)"), d["cc_recv"][r * 128:(r + 1) * 128, :],
                [DB(d["cc_recv"])], ["Pr"])
            apply_P(Pr, "Pr")
            dve(lambda: nc.vector.tensor_tensor(out=cand[:], in0=cand[:], in1=Sin[:], op=ALU.subtract),
                ["cand", "Sin"], ["cand"])
            for h in range(4):
                dve(lambda: nc.vector.scalar_tensor_tensor(out=Sin[:, h, :], in0=cand[:, h, :], scalar=rmask[:, r:r + 1],
                                                           in1=Sin[:, h, :], op0=ALU.mult, op1=ALU.add),
                    ["cand", "Sin", "sm"], ["Sin"])
        if not FUS:
            act(lambda: nc.scalar.copy(out=Sinb[:], in_=Sin[:]), ["Sin"], ["Sinb"])
            dma("sp", Pr[:].rearrange("p a b -> p (a b)"), d["cc_send"], [DB(d["cc_send"])], ["Pr"])
            apply_P(Pr, "Pr")
            dma("sp", d["gdn_p"].rearrange("h k v -> k h v"), cand[:], ["cand"], [DB(d["gdn_p"])])

        for ti in range(NT):
            dma("sp", oloc[:].rearrange("p a b -> p (a b)"), d["oloc_s"][ti], [DB(d["oloc_s"])], ["oloc"])
            dma("sp", zsb[:, :], d["z_s"][ti], [DB(d["z_s"])], ["zsb"])
            dma("sp", swaT[:].rearrange("p a b -> p (a b)"), d["swat_s"][ti], [DB(d["swat_s"])], ["swaT"])
            if not FUS:
                dma("sp", oPT[:].rearrange("p a b -> p (a b)"), d["opt_s"][ti], [DB(d["opt_s"])], ["oPT"])
                bk, bn = bank()
                for h in range(4):
                    pe(lambda: nc.tensor.matmul(bk[:, h * 128:(h + 1) * 128], lhsT=oPT[:, h, :], rhs=Sinb[:, h, :],
                                                start=True, stop=True), ["oPT", "Sinb"], [bn], inc=(h == 3))
                dve(lambda: nc.vector.tensor_tensor(out=oloc[:], in0=oloc[:], in1=v3(bk, 128, 4), op=ALU.add),
                    ["oloc", bn], ["oloc"])
            gate_cols(128, oloc[:], "oloc", zsb[:, :], "zsb", 0)
            out_proj(128, swaT, "swaT", PRE0 + ti * 128, ti * 128)


def build_program(dbg=False, part="AB"):
    nc = bass.Bass("TRN2", target_bir_lowering=False)
    C = Ctx()
    C.nc = nc
    C.S = Sched(nc)
    S = C.S
    C.B_dram = {}
    A, Bp = part in ("A", "AB", "F"), part in ("B", "AB", "F")
    FUS = part == "F"
    NT1 = NTOKF if FUS else NTOK1

    def dt_(nm, shape, dt, kind):
        t = nc.dram_tensor(nm, list(shape), dt, kind=kind).ap()
        C.B_dram[nm] = Buf(nm)
        return t

    def din(nm, shape, dt=F32):
        return dt_(nm, shape, dt, "ExternalInput")

    def dout(nm, shape, cond=True):
        return dt_(nm, shape, F32, "ExternalOutput" if cond else "Internal")

    def dlink(nm, shape, dt=F32):
        kind = "Internal" if part in ("AB", "F") else ("ExternalOutput" if part == "A" else "ExternalInput")
        if dbg and part == "AB" and dt == F32:
            kind = "ExternalOutput"
        return dt_(nm, shape, dt, kind)

    xin = din("xin", [NT1, D])
    pin = din("pin", [NTOK2, PLE])
    wg1 = din("wg1", [NG, 128, 8 * GW]); wu1 = din("wu1", [NG, 128, 8 * GW]); wd1 = din("wd1", [128, NJ * D])
    wg2 = din("wg2", [NG, 128, 8 * GW]); wu2 = din("wu2", [NG, 128, 8 * GW]); wd2 = din("wd2", [128, NJ * D])
    gains = {k: din(k, [1, D]) for k in ("g1pre", "g1post", "gmpre", "gmpost", "g2pre", "g2post", "gple")}
    wpg = din("wpg", [128, 8 * D])
    wpp = din("wpp", [128, 2 * D])
    d = dict(gmpre=gains["gmpre"][0:1, :], gmpost=gains["gmpost"][0:1, :])
    d["win"] = din("win", [128, 8 * NPROJ])
    d["convw"] = din("convw", [128, 48])
    d["alog"] = din("alog", [1, 4])[0:1, :]; d["dtb"] = din("dtb", [1, 4])[0:1, :]
    d["gdnn"] = din("gdnn", [1, 128])[0:1, :]; d["sinks"] = din("sinks", [1, 8])[0:1, :]
    d["rmask"] = din("rmask", [1, 8])[0:1, :]
    d["wout_g"] = din("wout_g", [128, 4 * D]); d["wout_s"] = din("wout_s", [64, 8 * D])
    for k in ("bprev", "bcur", "bprev1"):
        d[k] = din(k, [128, 8 * 128])
    d["bs_cache"] = din("bs_cache", [128, 32]); d["bs_new"] = din("bs_new", [4, 32])
    d["cmask"] = din("cmask", [128, 4 * 128]); d["sel65"] = din("sel65", [65, 64])
    d["state_conv"] = din("state_conv", [48, 1536]); d["state_gdn"] = din("state_gdn", [16, 4, 128, 128])
    d["cache_k"] = din("cache_k", [16, 128, 128]); d["cache_v"] = din("cache_v", [16, 128, 128])
    y_out = dout("y", [NTOK2, D], Bp)
    d["gdn_p"] = dout("gdn_p", [4, 128, 128], Bp)
    d["conv_p"] = dout("conv_p", [3, 1536], A)
    d["swak_p"] = dout("swak_p", [128, 128], A); d["swav_p"] = dout("swav_p", [128, 128], A)
    d["conv_s"] = dout("conv_s", [48, 1536], A); d["gdn_s"] = dout("gdn_s", [16, 4, 128, 128], A)
    d["swak_s"] = dout("swak_s", [16, 128, 128], A); d["swav_s"] = dout("swav_s", [16, 128, 128], A)
    x1 = dlink("x1", [NT1, D])
    if part == "B":
        d["x2in"] = din("x2in", [NTOK2, D])
        x2 = dt_("x2", [NTOK2, D], F32, "Internal")
    elif part == "A":
        x2 = dt_("x2in", [NTOK2, D], F32, "ExternalOutput")
    else:
        x2 = dt_("x2", [NTOK2, D], F32, "ExternalOutput" if dbg else "Internal")
    d["x1"] = x1; d["x2"] = x2
    NT = NMAIN // 128
    d["oloc_s"] = dlink("oloc_s", [NT, 128, 512]); d["opt_s"] = dlink("opt_s", [NT, 128, 512], BF16)
    d["z_s"] = dlink("z_s", [NT, 128, 512]); d["swat_s"] = dlink("swat_s", [NT, 64, 1024], BF16)
    d["cc_send"] = dlink("cc_send", [128, 1024])
    d["cc_recv"] = dt_("cc_recv", [NCORES * 128, 1024], F32, "ExternalInput" if part == "B" else "Internal")

    C.idf = nc.alloc_sbuf_tensor("idf", [128, 128], F32)
    C.idb = nc.alloc_sbuf_tensor("idb", [128, 128], BF16)
    C.B_id = Buf("id")
    C.epsc = nc.alloc_sbuf_tensor("epsc", [128, 2], F32)
    S.op("pool", lambda: nc.gpsimd.memset(C.epsc[:, 0:1], EPS), writes=[C.B_id])
    S.op("pool", lambda: nc.gpsimd.memset(C.epsc[:, 1:2], 1.0), writes=[C.B_id])
    C.eps_ap = lambda n: C.epsc[:n, 0:1]
    S.op("pool", lambda: nc.gpsimd.memset(C.idf[:], 0.0), writes=[C.B_id])
    S.op("pool", lambda: nc.gpsimd.affine_select(out=C.idf[:], in_=C.idf[:], pattern=[[-1, 128]],
                                                 compare_op=ALU.not_equal, fill=1.0, base=0, channel_multiplier=1),
         reads=[C.B_id], writes=[C.B_id])
    S.op("dve", lambda: nc.vector.tensor_copy(out=C.idb[:], in_=C.idf[:]), reads=[C.B_id], writes=[C.B_id])

    outs = []
    if A:
        npm = (NPRE if FUS else NHALO) + NMAIN
        t1 = [(r0, r0, n) for (r0, n) in tiles_of(npm)] + [(npm, npm, NSAMP)]
        ffn_phase(C, "f1", xin, x1, t1, wg1, wu1, wd1, gains["g1pre"][0:1, :], gains["g1post"][0:1, :])
        outs += ["conv_p", "swak_p", "swav_p", "conv_s", "gdn_s", "swak_s", "swav_s"]
        if part == "A":
            outs += ["x1", "x2in", "oloc_s", "opt_s", "z_s", "swat_s", "cc_send"]
    mix_phase(C, d, part)
    if Bp:
        t2 = [(r0, r0, n) for (r0, n) in tiles_of(NMAIN)] + [(NMAIN, NMAIN, NSAMP)]
        ffn_phase(C, "f2", x2, y_out, t2, wg2, wu2, wd2, gains["g2pre"][0:1, :], gains["g2post"][0:1, :],
                  ple=dict(gain=gains["gple"][0:1, :], wg=wpg, wp=wpp, p=pin, prow=lambda d0: d0))
        outs += ["y", "gdn_p"]
    S.finish([C.B_dram[k] for k in outs])
    return nc, C


def _lay_gu(w):
    return np.ascontiguousarray(w.reshape(8, 128, NG, GW).transpose(2, 1, 0, 3).reshape(NG, 128, 8 * GW))


def _lay_rows(w, nk):
    n = w.shape[1]
    return np.ascontiguousarray(w.reshape(nk, 128, n).transpose(1, 0, 2).reshape(128, nk * n))


def _bucket_table():
    dd = np.arange(128)
    lr = np.log(np.maximum(dd, 1).astype(np.float32) / np.float32(16)) / np.float32(np.log(128 / 16))
    large = np.minimum(16 + (lr.astype(np.float32) * np.float32(16)).astype(np.int32), 31)
    return np.where(dd < 16, dd, large)


def _bias_tables(rel_bias):
    bt = _bucket_table()
    bv = rel_bias[bt, :]
    k = np.arange(128)[:, None]; q = np.arange(128)[None, :]
    dcur = q - k
    dprev = 128 + q - k
    bcur = np.full((128, 8, 128), NEG, np.float32); bprev = np.full((128, 8, 128), NEG, np.float32)
    for h in range(8):
        t = bv[np.clip(dcur, 0, 127), h]
        bcur[:, h, :] = np.where(dcur >= 0, t, NEG)
        t = bv[np.clip(dprev, 0, 127), h]
        bprev[:, h, :] = np.where(dprev < 128, t, NEG)
    j = np.arange(128)[:, None]; t4 = np.arange(4)[None, :]
    dc = 128 + t4 - j
    bsc = np.full((128, 8, 4), NEG, np.float32)
    kk = np.arange(4)[:, None]
    dn = t4 - kk
    bsn = np.full((4, 8, 4), NEG, np.float32)
    for h in range(8):
        bsc[:, h, :] = np.where(dc < 128, bv[np.clip(dc, 0, 127), h], NEG)
        bsn[:, h, :] = np.where(dn >= 0, bv[np.clip(dn, 0, 127), h], NEG)
    return bprev.reshape(128, -1), bcur.reshape(128, -1), bsc.reshape(128, -1), bsn.reshape(4, -1)


def make_in_maps(inp, fused=False):
    f = lambda a: np.ascontiguousarray(np.asarray(a, dtype=np.float32))
    w_in = f(inp["w_in"][0])
    perm = np.concatenate([np.arange(0, 1536), np.arange(1536, 2048), np.arange(2056, 2568), np.arange(2568, 2696),
                           np.arange(2696, 2824), np.arange(2048, 2052), np.arange(2052, 2056)])
    w_out = f(inp["w_out"][0])
    conv_w = f(inp["conv_w"][0])
    bprev, bcur, bsc, bsn = _bias_tables(f(inp["rel_bias"]))
    ii = np.arange(128)
    tri = (ii[:, None] <= ii[None, :]).astype(np.float32)
    cmask = np.stack([tri, tri.T, (ii[:, None] > ii[None, :]).astype(np.float32), np.ones((128, 128), np.float32)], 1)
    sel65 = np.zeros((65, 64), np.float32); sel65[64, :] = 1.0
    shared = {
        "wg1": _lay_gu(f(inp["ffn1_w_gate"][0])), "wu1": _lay_gu(f(inp["ffn1_w_up"][0])),
        "wd1": _lay_rows(f(inp["ffn1_w_down"][0]), NJ),
        "wg2": _lay_gu(f(inp["ffn2_w_gate"][0])), "wu2": _lay_gu(f(inp["ffn2_w_up"][0])),
        "wd2": _lay_rows(f(inp["ffn2_w_down"][0]), NJ),
        "g1pre": f(inp["norm_ffn1_pre"]), "g1post": f(inp["norm_ffn1_post"]),
        "gmpre": f(inp["norm_mix_pre"]), "gmpost": f(inp["norm_mix_post"]),
        "g2pre": f(inp["norm_ffn2_pre"]), "g2post": f(inp["norm_ffn2_post"]),
        "gple": f(inp["norm_ple_post"]),
        "wpg": _lay_rows(f(inp["ple_gate"][0]), 8), "wpp": _lay_rows(f(inp["ple_proj"][0]), 2),
        "win": _lay_rows(np.ascontiguousarray(w_in[:, perm]), 8),
        "convw": np.ascontiguousarray(conv_w.T.reshape(12, 128, 4).transpose(1, 0, 2).reshape(128, 48)),
        "alog": f(inp["gdn_a_log"]), "dtb": f(inp["gdn_dt_bias"]), "gdnn": f(inp["gdn_norm"]),
        "sinks": f(inp["swa_sinks"]),
        "wout_g": _lay_rows(w_out[:512], 4),
        "wout_s": np.ascontiguousarray(w_out[512:].reshape(8, 64, D).transpose(1, 0, 2).reshape(64, 8 * D)),
        "bprev": bprev, "bcur": bcur, "bs_cache": bsc, "bs_new": bsn,
        "cmask": np.ascontiguousarray(cmask.reshape(128, 512)), "sel65": sel65,
    }
    xp = f(inp["x_prompt"]); xs = f(inp["x_sample"]).reshape(-1, D)
    pp = f(inp["p_prompt"][0]); psm = f(inp["p_sample"][0]).reshape(-1, PLE)
    sconv = f(inp["state_conv"][0]); sgdn = f(inp["state_gdn"][0])
    ck = f(inp["cache_swa_k"][0]).reshape(128, 128, 128); cv = f(inp["cache_swa_v"][0]).reshape(128, 128, 128)
    maps = []
    for c in range(NCORES):
        b, q = c // 4, c % 4
        t0 = q * NMAIN
        if fused:
            halo = np.zeros((NPRE, D), np.float32)
            if t0 > 0:
                halo[NPRE - t0:] = xp[b, 0:t0]
        else:
            halo = xp[b, t0 - NHALO:t0] if q > 0 else np.zeros((NHALO, D), np.float32)
        m = dict(shared)
        m["xin"] = np.ascontiguousarray(np.concatenate([halo, xp[b, t0:t0 + NMAIN], xs[c * NSAMP:(c + 1) * NSAMP]], 0))
        m["pin"] = np.ascontiguousarray(np.concatenate([pp[b, t0:t0 + NMAIN], psm[c * NSAMP:(c + 1) * NSAMP]], 0))
        m["bprev1"] = bprev if q > 0 else np.full_like(bprev, NEG)
        rm = np.zeros((1, 8), np.float32)
        for r in range(NCORES):
            if r // 4 == b and r % 4 < q:
                rm[0, r] = 1.0
        m["rmask"] = rm
        m["state_conv"] = np.ascontiguousarray(sconv[c * 16:(c + 1) * 16].reshape(48, 1536))
        m["state_gdn"] = np.ascontiguousarray(sgdn[c * 16:(c + 1) * 16])
        m["cache_k"] = np.ascontiguousarray(ck[c * 16:(c + 1) * 16]); m["cache_v"] = np.ascontiguousarray(cv[c * 16:(c + 1) * 16])
        maps.append(m)
    return maps


def assemble(R):
    yp = np.zeros((2, 8192, D), np.float32); ys = np.zeros((128, 4, D), np.float32)
    conv_p = np.zeros((1, 2, 3, 1536), np.float32); gdn_p = np.zeros((1, 2, 4, 128, 128), np.float32)
    kp = np.zeros((1, 2, 128, 2, 64), np.float32); vp = np.zeros((1, 2, 128, 2, 64), np.float32)
    conv_s = np.zeros((1, 128, 3, 1536), np.float32); gdn_s = np.zeros((1, 128, 4, 128, 128), np.float32)
    ks = np.zeros((1, 128, 128, 2, 64), np.float32); vs = np.zeros((1, 128, 128, 2, 64), np.float32)
    for c in range(NCORES):
        b, q = c // 4, c % 4
        r = R[c]
        yp[b, q * NMAIN:(q + 1) * NMAIN] = r["y"][:NMAIN]
        ys[c * 16:(c + 1) * 16] = r["y"][NMAIN:].reshape(16, 4, D)
        if q == 3:
            conv_p[0, b] = r["conv_p"]; gdn_p[0, b] = r["gdn_p"]
            kp[0, b] = r["swak_p"].reshape(128, 2, 64); vp[0, b] = r["swav_p"].reshape(128, 2, 64)
        conv_s[0, c * 16:(c + 1) * 16] = r["conv_s"].reshape(16, 3, 1536)
        gdn_s[0, c * 16:(c + 1) * 16] = r["gdn_s"]
        ks[0, c * 16:(c + 1) * 16] = r["swak_s"].reshape(16, 128, 2, 64)
        vs[0, c * 16:(c + 1) * 16] = r["swav_s"].reshape(16, 128, 2, 64)
    return (yp, ys, conv_p, gdn_p, kp, vp, conv_s, gdn_s, ks, vs)


_CACHE = {}
A_KEYS = ("x1", "x2in", "oloc_s", "opt_s", "z_s", "swat_s", "cc_send")


def kernel(**inputs):
    if "F" not in _CACHE:
        _CACHE["F"] = build_program(False, "F")[0]
    maps = make_in_maps(inputs, fused=True)
    res = run_bass_kernel_spmd(_CACHE["F"], maps, core_ids=list(range(NCORES)))
    return assemble(res.results)
```
